# Optimizing a Trainium2 kernel written in Bass

```python
import math
import jax, jax.numpy as jnp
from jax import lax
import numpy as np

D_MODEL = 1024
BATCH = 1
SEQ = 16384
DEPTH = 1

GRID_W = 64
CTX_LEN = 256
SSD_HEAD_DIM = 64
D_SSD = D_MODEL
SSD_HEADS = D_SSD // SSD_HEAD_DIM
D_STATE = 128
D_CONV = 5
CHUNK = 128
D_POOL = D_MODEL
POOL_WINDOWS = (2, 4, 8, 16)
N_POOL_GROUPS = len(POOL_WINDOWS)
POOL_GROUP_DIM = D_POOL // N_POOL_GROUPS
D_MIX = D_SSD + D_POOL
D_XBC = D_SSD + 2 * D_STATE
D_IN_PROJ = D_SSD + D_XBC + 2 * SSD_HEADS + D_POOL
D_FF = ((8 * D_MODEL // 3 + 255) // 256) * 256
DEEPNORM_ALPHA = (2 * DEPTH) ** 0.25
DEEPNORM_BETA = (8 * DEPTH) ** -0.25
LN_EPS = 1e-5

kernel_name = 'hybrid_ssd_pool_deepnorm_dit_block'


def layer_norm(x, g, b):
    xf = x.astype(jnp.float32)
    mu = jnp.mean(xf, axis=-1, keepdims=True)
    var = jnp.mean(jnp.square(xf - mu), axis=-1, keepdims=True)
    return ((xf - mu) * lax.rsqrt(var + LN_EPS) * g.astype(jnp.float32) + b.astype(jnp.float32)).astype(x.dtype)


def rms_norm(x, g):
    xf = x.astype(jnp.float32)
    return (xf * lax.rsqrt(jnp.mean(xf * xf, axis=-1, keepdims=True) + LN_EPS) * g.astype(jnp.float32)).astype(x.dtype)


def modulate(x, shift, scale):
    return x * (1 + scale) + shift


def split_projection(h, w_in):
    proj = h @ w_in
    o1 = D_SSD
    o2 = o1 + D_XBC
    o3 = o2 + 2 * SSD_HEADS
    return proj[..., :o1], proj[..., o1:o2], proj[..., o2:o3], proj[..., o3:]


def depthwise_conv_centred(u, w, b):
    out = lax.conv_general_dilated(
        u, w[:, None, :].astype(u.dtype), window_strides=(1,),
        padding=[(D_CONV // 2, D_CONV // 2)],
        dimension_numbers=('NWC', 'WIO', 'NWC'),
        feature_group_count=u.shape[-1])
    return out + b


def ssd_chunked_scan(x, dt, a_neg, B, C, h0):
    f32 = jnp.float32
    b, L, H, P = x.shape
    N = B.shape[-1]
    nc = L // CHUNK
    xc = x.astype(f32).reshape(b, nc, CHUNK, H, P)
    dtc = dt.astype(f32).reshape(b, nc, CHUNK, H)
    Bc = B.astype(f32).reshape(b, nc, CHUNK, N)
    Cc = C.astype(f32).reshape(b, nc, CHUNK, N)
    a_cum = jnp.cumsum(dtc * a_neg.astype(f32), axis=2)
    a_cum_h = jnp.moveaxis(a_cum, -1, 2)
    seg = a_cum_h[..., :, None] - a_cum_h[..., None, :]
    lower = jnp.tril(jnp.ones((CHUNK, CHUNK), dtype=bool))
    decay = jnp.exp(jnp.where(lower, seg, -jnp.inf))
    cb = jnp.einsum('bcin,bcjn->bcij', Cc, Bc)
    scores = cb[:, :, None] * decay * jnp.moveaxis(dtc, -1, 2)[..., None, :]
    y_diag = jnp.einsum('bchij,bcjhp->bcihp', scores, xc)
    w_end = jnp.exp(a_cum[:, :, -1:, :] - a_cum) * dtc
    states = jnp.einsum('bcjn,bcjhp->bchpn', Bc, xc * w_end[..., None])
    chunk_decay = jnp.exp(a_cum[:, :, -1, :])

    def step(h, inp):
        dec, st = inp
        return h * dec[:, :, None, None] + st, h

    h_final, h_prev = lax.scan(step, h0.astype(f32),
                               (jnp.moveaxis(chunk_decay, 1, 0), jnp.moveaxis(states, 1, 0)))
    h_prev = jnp.moveaxis(h_prev, 0, 1)
    y_off = jnp.einsum('bcin,bchpn->bcihp', Cc, h_prev) * jnp.exp(a_cum)[..., None]
    y = (y_diag + y_off).reshape(b, L, H, P)
    return y.astype(x.dtype), h_final


def ssd_bidirectional(xbc, dt_raw, dt_bias, a_log, h0_fwd, h0_bwd):
    b, L, _ = xbc.shape
    xs = xbc[..., :D_SSD].reshape(b, L, SSD_HEADS, SSD_HEAD_DIM)
    Bm = xbc[..., D_SSD:D_SSD + D_STATE]
    Cm = xbc[..., D_SSD + D_STATE:]
    dt = jax.nn.softplus(dt_raw.reshape(b, L, 2, SSD_HEADS) + dt_bias)
    a_neg = -jnp.exp(a_log.astype(jnp.float32))
    y_f, h_f = ssd_chunked_scan(xs, dt[:, :, 0], a_neg[0], Bm, Cm, h0_fwd)
    flip = lambda t: jnp.flip(t, axis=1)
    y_b, h_b = ssd_chunked_scan(flip(xs), flip(dt[:, :, 1]), a_neg[1], flip(Bm), flip(Cm), h0_bwd)
    return y_f + flip(y_b), xs, h_f, h_b


def box_mean(u, w, axis):
    n = u.shape[axis]
    pad = [(0, 0)] * u.ndim
    pad[axis] = (1, 0)
    cs = jnp.pad(jnp.cumsum(u, axis=axis), pad)
    pos = jnp.arange(n)
    lo = jnp.clip(pos - w // 2, 0, n)
    hi = jnp.clip(pos + (w - w // 2), 0, n)
    total = jnp.take(cs, hi, axis=axis) - jnp.take(cs, lo, axis=axis)
    shape = [1] * u.ndim
    shape[axis] = n
    return total / (hi - lo).astype(u.dtype).reshape(shape)


def pool_mixer(u, rows, pool_w, pool_scale):
    b, L, _ = u.shape
    uf = u.astype(jnp.float32).reshape(b, L, N_POOL_GROUPS, POOL_GROUP_DIM)
    outs = []
    for g, w in enumerate(POOL_WINDOWS):
        ug = uf[:, :, g]
        if rows is None:
            m = box_mean(ug, w, 1)
        else:
            grid = ug.reshape(b, rows, GRID_W, POOL_GROUP_DIM)
            m = box_mean(box_mean(grid, w, 1), w, 2).reshape(b, L, POOL_GROUP_DIM)
        outs.append(m - ug)
    d = jnp.stack(outs, axis=2)
    y = jnp.einsum('blgi,gio->blgo', d, pool_w.astype(jnp.float32)).reshape(b, L, D_POOL)
    return (y * pool_scale.astype(jnp.float32)).astype(u.dtype)


def merge_head_groups(y_ssd, xs, z, u_pool, rows, d_skip, ssd_norm_g, pool_w, pool_scale, w_out):
    b, L = z.shape[:2]
    y = (y_ssd + d_skip[:, None] * xs).reshape(b, L, D_SSD)
    y = rms_norm(y * jax.nn.silu(z), ssd_norm_g)
    p = pool_mixer(u_pool, rows, pool_w, pool_scale)
    return jnp.concatenate([y, p], axis=-1) @ w_out


def swiglu(h, w_gate, w_up, w_down):
    return (jax.nn.silu(h @ w_gate) * (h @ w_up)) @ w_down


def setup_inputs(seed: int = 0) -> dict:
    key = jax.random.key(seed)
    ks = jax.random.split(key, 26)
    f32 = jnp.float32
    nrm = lambda k, shape, s: jax.random.normal(k, shape, f32) * s
    dt0 = jnp.exp(jax.random.uniform(ks[10], (DEPTH, 2, SSD_HEADS), f32,
                                     minval=math.log(1e-3), maxval=math.log(1e-1)))
    return {
        'x': nrm(ks[0], (BATCH, SEQ, D_MODEL), 1.0),
        'c': nrm(ks[1], (BATCH, D_MODEL), 1.0),
        'ctx': nrm(ks[2], (BATCH, CTX_LEN, D_MODEL), 1.0),
        'c_ctx': nrm(ks[3], (D_MODEL,), 1.0),
        'emb_ln_g': 1.0 + nrm(ks[4], (D_MODEL,), 0.02),
        'emb_ln_b': nrm(ks[5], (D_MODEL,), 0.02),
        'w_ada': nrm(ks[6], (DEPTH, D_MODEL, 6 * D_MODEL), 0.5 * D_MODEL ** -0.5),
        'b_ada': nrm(ks[7], (DEPTH, 6 * D_MODEL), 0.01),
        'in_proj': nrm(ks[8], (DEPTH, D_MODEL, D_IN_PROJ), D_MODEL ** -0.5),
        'conv_w': nrm(ks[9], (DEPTH, D_CONV, D_XBC), D_CONV ** -0.5),
        'conv_b': nrm(ks[11], (DEPTH, D_XBC), 0.01),
        'dt_bias': dt0 + jnp.log(-jnp.expm1(-dt0)),
        'a_log': jnp.log(jax.random.uniform(ks[12], (DEPTH, 2, SSD_HEADS), f32, minval=1.0, maxval=16.0)),
        'd_skip': 1.0 + nrm(ks[13], (DEPTH, SSD_HEADS), 0.1),
        'ssd_norm_g': 1.0 + nrm(ks[14], (DEPTH, D_SSD), 0.02),
        'pool_w': nrm(ks[15], (DEPTH, N_POOL_GROUPS, POOL_GROUP_DIM, POOL_GROUP_DIM), POOL_GROUP_DIM ** -0.5),
        'pool_scale': 1.0 + nrm(ks[16], (DEPTH, D_POOL), 0.02),
        'w_out': nrm(ks[17], (DEPTH, D_MIX, D_MODEL), DEEPNORM_BETA * D_MIX ** -0.5),
        'ln1_g': 1.0 + nrm(ks[18], (DEPTH, D_MODEL), 0.02),
        'ln1_b': nrm(ks[19], (DEPTH, D_MODEL), 0.02),
        'w_gate': nrm(ks[20], (DEPTH, D_MODEL, D_FF), D_MODEL ** -0.5),
        'w_up': nrm(ks[21], (DEPTH, D_MODEL, D_FF), D_MODEL ** -0.5),
        'w_down': nrm(ks[22], (DEPTH, D_FF, D_MODEL), DEEPNORM_BETA * D_FF ** -0.5),
        'ln2_g': 1.0 + nrm(ks[23], (DEPTH, D_MODEL), 0.02),
        'ln2_b': nrm(ks[24], (DEPTH, D_MODEL), 0.02),
    }


def reference(x, c, ctx, c_ctx, emb_ln_g, emb_ln_b, w_ada, b_ada, in_proj, conv_w, conv_b,
              dt_bias, a_log, d_skip, ssd_norm_g, pool_w, pool_scale, w_out, ln1_g, ln1_b,
              w_gate, w_up, w_down, ln2_g, ln2_b):
    b = x.shape[0]
    rows = x.shape[1] // GRID_W
    x = layer_norm(x, emb_ln_g, emb_ln_b)
    xc = layer_norm(ctx, emb_ln_g, emb_ln_b)
    silu_c = jax.nn.silu(c)
    silu_cc = jax.nn.silu(c_ctx)
    h_zero = jnp.zeros((b, SSD_HEADS, SSD_HEAD_DIM, D_STATE), jnp.float32)
    for l in range(DEPTH):
        mod = (silu_c @ w_ada[l] + b_ada[l])[:, None, :]
        sh1, sc1, g1, sh2, sc2, g2 = jnp.split(mod, 6, axis=-1)
        modc = silu_cc @ w_ada[l] + b_ada[l]
        sh1c, sc1c, g1c, sh2c, sc2c, g2c = jnp.split(modc, 6, axis=-1)

        zc, xbcc, dtc, upc = split_projection(modulate(xc, sh1c, sc1c), in_proj[l])
        xbcc = jax.nn.silu(depthwise_conv_centred(xbcc, conv_w[l], conv_b[l]))
        yc, xsc, hf_ctx, hb_ctx = ssd_bidirectional(xbcc, dtc, dt_bias[l], a_log[l], h_zero, h_zero)

        z, xbc, dt_raw, up = split_projection(modulate(x, sh1, sc1), in_proj[l])
        xbc = jax.nn.silu(depthwise_conv_centred(xbc, conv_w[l], conv_b[l]))
        y, xs, _, _ = ssd_bidirectional(xbc, dt_raw, dt_bias[l], a_log[l], hf_ctx, hb_ctx)
        mix = merge_head_groups(y, xs, z, up, rows, d_skip[l], ssd_norm_g[l], pool_w[l],
                                pool_scale[l], w_out[l])
        x = layer_norm(DEEPNORM_ALPHA * x + g1 * mix, ln1_g[l], ln1_b[l])
        ffn = swiglu(modulate(x, sh2, sc2), w_gate[l], w_up[l], w_down[l])
        x = layer_norm(DEEPNORM_ALPHA * x + g2 * ffn, ln2_g[l], ln2_b[l])

        if l + 1 < DEPTH:
            mixc = merge_head_groups(yc, xsc, zc, upc, None, d_skip[l], ssd_norm_g[l], pool_w[l],
                                     pool_scale[l], w_out[l])
            xc = layer_norm(DEEPNORM_ALPHA * xc + g1c * mixc, ln1_g[l], ln1_b[l])
            ffnc = swiglu(modulate(xc, sh2c, sc2c), w_gate[l], w_up[l], w_down[l])
            xc = layer_norm(DEEPNORM_ALPHA * xc + g2c * ffnc, ln2_g[l], ln2_b[l])
    return x
```

```python
import numpy as np
from contextlib import ExitStack
import concourse.bass as bass
import concourse.mybir as mybir
from concourse.bass_utils import run_bass_kernel_spmd

F32 = mybir.dt.float32
BF16 = mybir.dt.bfloat16
AF = mybir.ActivationFunctionType
ALU = mybir.AluOpType

D = 1024
NCORE = 8
TOK = 2048
GRID_W = 64
NIP = 3360
EPS = 1e-5
ALPHA = 2.0 ** 0.25
E_CTXF, E_CTXB, E_FOR0, E_B7, E_OWNF, E_OWNB, NE = 0, 1, 2, 9, 10, 11, 12
T_CTXF, T_CTXB, T_BLK0 = 0, 2, 4
T_OWN = T_BLK0 + 8 * 16
T_PHA = T_OWN + 16
T_PHB = T_PHA + 4
T_HALO = T_PHB + 4
NT = T_HALO
NBROW = NIP + NE * 16
DEBUG = False


class Prog:
    def __init__(self):
        self.ops = []
        self.lastw = {}
        self.readers = {}
        self.fence_deps = set()
        self.last_on = {}

    def fence(self):
        self.fence_deps = set(self.last_on.values())

    def add(self, eng, fn, r=(), w=(), dma=False):
        idx = len(self.ops)
        deps = set()
        r = [k[:3] if k.startswith("ps") else k for k in r]
        w = [k[:3] if k.startswith("ps") else k for k in w]
        w = list(w) + [k for k in r if k.startswith("ps")]
        r = [k for k in r if not k.startswith("ps")]
        for k in r:
            if k in self.lastw:
                deps.add(self.lastw[k])
        for k in w:
            if k in self.lastw:
                deps.add(self.lastw[k])
            deps.update(self.readers.get(k, ()))
        for k in r:
            self.readers.setdefault(k, []).append(idx)
        for k in w:
            self.lastw[k] = idx
            self.readers[k] = []
        deps.update(self.fence_deps)
        deps.discard(idx)
        self.last_on[eng] = idx
        self.ops.append(dict(eng=eng, fn=fn, deps=deps, dma=dma, sig=False))
        return idx

    def emit(self, nc, stack, n_dma_sems=12):
        ops = self.ops
        for o in ops:
            keep = set()
            for d in o["deps"]:
                od = ops[d]
                if od["eng"] == "pe" and o["eng"] == "pe" and not od["dma"] and not o["dma"]:
                    continue
                keep.add(d)
                od["sig"] = True
            o["deps"] = keep
        engs = ["pe", "act", "dve", "pool", "sp"]
        csem = {e: stack.enter_context(nc.semaphore(f"c_{e}")) for e in engs}
        dq = sorted({o["eng"] for o in ops if o["dma"]})
        dsem = {q: [stack.enter_context(nc.semaphore(f"d_{q}_{i}")) for i in range(n_dma_sems)] for q in dq}
        ccount = {e: 0 for e in engs}
        dcount = {q: [0] * n_dma_sems for q in dq}
        dlast = {q: [None] * n_dma_sems for q in dq}
        rr = {q: 0 for q in dq}
        for i, o in enumerate(ops):
            o["pre"] = None
            if o["dma"]:
                q = o["eng"]
                s = rr[q] % n_dma_sems
                rr[q] += 1
                if dlast[q][s] is not None:
                    o["pre"] = (("d", q, s), dcount[q][s])
                dcount[q][s] += 16
                dlast[q][s] = i
                o["sem"] = ("d", q, s)
                o["val"] = dcount[q][s]
            elif o["sig"]:
                ccount[o["eng"]] += 1
                o["sem"] = ("c", o["eng"])
                o["val"] = ccount[o["eng"]]
        self.final_dma = [(("d", q, s), dcount[q][s]) for q in dq for s in range(n_dma_sems) if dcount[q][s]]
        self.stats = dict(n_ops=len(ops), signals=dict(ccount), dmas={q: rr[q] for q in dq})

        def semobj(key):
            return dsem[key[1]][key[2]] if key[0] == "d" else csem[key[1]]

        block = stack.enter_context(nc.Block())
        handles = {"pe": block.tensor, "act": block.scalar, "dve": block.vector,
                   "pool": block.gpsimd, "sp": block.sync}
        for e in engs:
            mine = [o for o in ops if o["eng"] == e]

            def body(engine, mine=mine, e=e):
                waited = {}
                for o in mine:
                    need = {}
                    if o["pre"] is not None:
                        need[o["pre"][0]] = o["pre"][1]
                    for d in o["deps"]:
                        od = ops[d]
                        k = od["sem"]
                        need[k] = max(need.get(k, 0), od["val"])
                    for k, v in need.items():
                        if waited.get(k, 0) < v:
                            engine.wait_ge(semobj(k), v)
                            waited[k] = v
                    ins = o["fn"](engine)
                    if o["dma"]:
                        ins.then_inc(semobj(o["sem"]), 16)
                    elif o["sig"]:
                        ins.then_inc(semobj(o["sem"]), 1)
                if e == "sp":
                    for k, v in self.final_dma:
                        if waited.get(k, 0) < v:
                            engine.wait_ge(semobj(k), v)

            handles[e](body)


def _fm(v, n):
    return np.ascontiguousarray(np.asarray(v, np.float32).reshape(n, 128).T)


C_I, C_TGT, C_TLT, C_TLE, C_TGE, C_ONE, NCONST = 0, 128, 256, 384, 512, 640, 768


def _consts():
    t = np.arange(128)
    I = np.eye(128, dtype=np.float32)
    Tgt = (t[:, None] > t[None, :]).astype(np.float32)
    Tlt = (t[:, None] < t[None, :]).astype(np.float32)
    Tle = (t[:, None] <= t[None, :]).astype(np.float32)
    Tge = (t[:, None] >= t[None, :]).astype(np.float32)
    ones = np.ones((128, 128), np.float32)
    return np.concatenate([I, Tgt, Tlt, Tle, Tge, ones], axis=1)


def _pack_cols(w, bc):
    K, N = w.shape
    kc = K // 128
    return np.ascontiguousarray(w.reshape(kc, 128, N // bc, bc).transpose(2, 1, 0, 3)).reshape((N // bc) * 128, kc * bc)


def _pack_gu(wg, wu):
    K, N = wg.shape
    kc, nb = K // 128, N // 256
    g = wg.reshape(kc, 128, nb, 256).transpose(2, 1, 0, 3)
    u = wu.reshape(kc, 128, nb, 256).transpose(2, 1, 0, 3)
    return np.ascontiguousarray(np.concatenate([g, u], axis=3)).reshape(nb * 128, kc * 512)


IPK_TILES = [(0, 512), (512, 512), (1024, 512), (1536, 512), (2048, 256), (2336, 512), (2848, 512)]


def _pack_ip(w):
    out = np.zeros((len(IPK_TILES), 128, 8 * 512), np.float32)
    for t, (c0, wd) in enumerate(IPK_TILES):
        blk = w[:, c0:c0 + wd].reshape(8, 128, wd).transpose(1, 0, 2)
        out[t, :, 0:8 * wd] = blk.reshape(128, 8 * wd)
    return out.reshape(len(IPK_TILES) * 128, 8 * 512)


def _pool_ops():
    ops = []
    for w in (2, 4, 8, 16):
        mats = []
        for n in (256, 64):
            pos = np.arange(n)
            lo = np.clip(pos - w // 2, 0, n)
            hi = np.clip(pos + (w - w // 2), 0, n)
            M = ((pos[None, :] >= lo[:, None]) & (pos[None, :] < hi[:, None])).astype(np.float64) / (hi - lo)[:, None]
            mats.append(M)
        ops.append(mats)
    return ops


_PT_LIST = [(0, -1), (0, 0)] + [(1, d_) for d_ in (-1, 0, 1)] + [(2, d_) for d_ in range(-2, 3)] + [(3, d_) for d_ in range(-4, 5)]


def _ptab(k, ops):
    tab = np.zeros((16, 128, 19, 128), np.float32)
    for T in range(16):
        Tg = 16 * k + T
        rt = np.array([2 * Tg, 2 * Tg + 1])
        for ix, (g, dl) in enumerate(_PT_LIST):
            Ts = Tg + dl
            if Ts < 0 or Ts >= 128:
                continue
            rs = np.array([2 * Ts, 2 * Ts + 1])
            Pr, Pc = ops[g]
            blk = np.kron(Pr[np.ix_(rt, rs)], Pc)
            if dl == 0:
                blk = blk - np.eye(128)
            tab[T, :, ix, :] = blk.T
    return tab.reshape(16 * 128, 19 * 128)


def _groups():
    gs = [(T_CTXF, 2, E_CTXF, True), (T_CTXB, 2, E_CTXB, True)]
    for b in range(8):
        for j in range(4):
            gs.append((T_BLK0 + 16 * b + 4 * j, 4, E_FOR0 + b, False))
    for j in range(4):
        gs.append((T_OWN + 4 * j, 4, E_OWNF, False))
    return gs


GROUPS = _groups()
NG = len(GROUPS)
R_DTB, R_ALG, R_SW, R_DSK, NROW = 0, NE * 16, 2 * NE * 16, 2 * NE * 16 + 8, 2 * NE * 16 + 8 + 16


def prep_inputs(inp):
    x = np.asarray(inp["x"], np.float32)[0]
    ctx = np.asarray(inp["ctx"], np.float32)[0]
    L = x.shape[0]
    zeros2 = np.zeros((2, D), np.float32)
    in_proj = np.asarray(inp["in_proj"], np.float32)[0]
    conv_w = np.asarray(inp["conv_w"], np.float32)[0]
    dt_bias = np.asarray(inp["dt_bias"], np.float32)[0]
    a_log = np.asarray(inp["a_log"], np.float32)[0]

    vecF = np.concatenate([
        _fm(inp["emb_ln_g"], 8), _fm(inp["emb_ln_b"], 8), _fm(np.asarray(inp["c"])[0], 8),
        _fm(inp["c_ctx"], 8), _fm(np.asarray(inp["ssd_norm_g"])[0], 8),
        _fm(np.asarray(inp["pool_scale"])[0], 8), _fm(np.asarray(inp["conv_b"])[0], 10),
        _fm(np.asarray(inp["ln1_g"])[0], 8), _fm(np.asarray(inp["ln1_b"])[0], 8),
        _fm(np.asarray(inp["ln2_g"])[0], 8), _fm(np.asarray(inp["ln2_b"])[0], 8),
        _fm(np.repeat(np.asarray(inp["d_skip"], np.float32)[0], 64), 8)], axis=1)
    shared = dict(
        vecF=vecF, consts=_consts(),
        w_ada=np.asarray(inp["w_ada"], np.float32)[0],
        b_ada=np.asarray(inp["b_ada"], np.float32),
        in_proj=in_proj,
        ipk=_pack_ip(in_proj),
        w_out=_pack_cols(np.asarray(inp["w_out"], np.float32)[0], 256),
        w_gu=_pack_gu(np.asarray(inp["w_gate"], np.float32)[0], np.asarray(inp["w_up"], np.float32)[0]),
        w_down=_pack_cols(np.asarray(inp["w_down"], np.float32)[0], 128),
        pool_w=np.asarray(inp["pool_w"], np.float32)[0].reshape(4 * 256, 256),
    )
    pops = _pool_ops()
    maps = []
    for k in range(NCORE):
        dirs = [0, 1] + [0 if b < k else 1 for b in range(7)] + [1, 0, 1]
        blocks = []

        def nat(s):
            lf = x[s - 2:s] if s >= 2 else zeros2
            rt = x[s + TOK:s + TOK + 2] if s + TOK + 2 <= L else zeros2
            return (x[s:s + TOK], lf, rt, float(s >= 2), float(s + TOK + 2 <= L))

        def flp(s):
            blk, lf, rt, fl, fr = nat(s)
            return (blk[::-1], rt[::-1], lf[::-1], fr, fl)

        blocks.append((ctx, zeros2, zeros2, 0.0, 0.0))
        blocks.append((ctx[::-1], zeros2, zeros2, 0.0, 0.0))
        for b in range(7):
            blocks.append(nat(TOK * b) if b < k else flp(TOK * (7 - (b - k))))
        blocks.append(flp(TOK * k))
        blocks.append(nat(TOK * k))
        tiles = [b_[0] for b_ in blocks]
        s0 = TOK * k
        tiles.append(x[s0 - 512:s0] if k > 0 else np.zeros((512, D), np.float32))
        tiles.append(x[s0 + TOK:s0 + TOK + 512] if k < 7 else np.zeros((512, D), np.float32))
        xs = np.concatenate(tiles, axis=0)
        assert xs.shape[0] == NT * 128, xs.shape
        xh = np.zeros((NG, 4, D), np.float32)
        gfl = np.ones((NG, 2), np.float32)
        gi = 0
        for (seq, lf, rt, fl, fr) in blocks:
            ext = np.concatenate([lf, seq, rt], axis=0)
            n = 256 if seq.shape[0] == 256 else 512
            ng = seq.shape[0] // n
            for j in range(ng):
                p0 = j * n
                xh[gi, 0:2] = ext[p0:p0 + 2]
                xh[gi, 2:4] = ext[p0 + n + 2:p0 + n + 4]
                if j == 0:
                    gfl[gi, 0] = fl
                if j == ng - 1:
                    gfl[gi, 1] = fr
                gi += 1
        assert gi == NG
        cw = np.stack([conv_w if d == 0 else conv_w[::-1] for d in dirs], 0)
        cwF = np.ascontiguousarray(cw.reshape(NE, 5, 10, 128).transpose(3, 0, 2, 1))
        wdt = np.stack([in_proj[:, 2304 + 16 * d:2304 + 16 * d + 16] for d in dirs], 0)
        dtb = np.concatenate([dt_bias[d] for d in dirs])
        alg = np.concatenate([a_log[d] for d in dirs])
        sw = np.zeros(8, np.float32)
        sw[k] = 1.0
        rowv = np.concatenate([dtb, alg, sw, np.asarray(inp["d_skip"], np.float32)[0]]).astype(np.float32)
        rowB = np.ascontiguousarray(np.broadcast_to(rowv[None, :], (128, rowv.size)))
        gflB = np.ascontiguousarray(np.broadcast_to(gfl.reshape(1, NG * 2), (128, NG * 2)))
        xo = x[TOK * k:TOK * (k + 1)]
        xT_own = np.zeros((4, D, 516), np.float32)
        for g_ in range(4):
            xT_own[g_, :, 0:512] = xo[512 * g_:512 * (g_ + 1)].T
            xT_own[g_, :, 512:516] = xh[34 + g_].T
        m = dict(shared)
        phf = np.zeros((128, 8), np.float32)
        phf[:, 0:4] = float(k > 0)
        phf[:, 4:8] = float(k < 7)
        m.update(xT_own=xT_own.reshape(4 * D, 516), inj=np.zeros((128, 5 * D), np.float32), ptab=_ptab(k, pops), phf=phf)
        wdtF = np.ascontiguousarray(wdt.reshape(NE, 8, 128, 16).transpose(2, 0, 1, 3)).reshape(128, NE * 128)
        m.update(xs=xs, xh=xh.reshape(NG * 4, D), cwF=cwF, wdt=wdtF, rowB=rowB, gfl=gflB)
        maps.append(m)
    return maps


def build_program():
    nc = bass.Bass("TRN2", target_bir_lowering=False)
    P = Prog()
    stack = ExitStack()

    def dram_in(name, shape):
        return nc.dram_tensor(name, list(shape), F32, kind="ExternalInput").ap()

    xs = dram_in("xs", [NT * 128, D])
    xh_d = dram_in("xh", [NG * 4, D])
    vecF_d = dram_in("vecF", [128, NVEC])
    consts_d = dram_in("consts", [128, NCONST])
    w_ada_d = dram_in("w_ada", [D, 6 * D])
    b_ada_d = dram_in("b_ada", [1, 6 * D])
    in_proj_d = dram_in("in_proj", [D, NIP])
    ipk_d = dram_in("ipk", [len(IPK_TILES) * 128, 8 * 512])
    cwF_d = dram_in("cwF", [128, NE, 10, 5])
    wdt_d = dram_in("wdt", [128, NE * 128])
    rowB_d = dram_in("rowB", [128, NROW])
    gfl_d = dram_in("gfl", [128, NG * 2])
    xT_own_d = dram_in("xT_own", [4 * D, 516])
    inj_d = dram_in("inj", [128, 5 * D])
    w_out_d = dram_in("w_out", [4 * 128, 16 * 256])
    w_gu_d = dram_in("w_gu", [11 * 128, 8 * 512])
    w_down_d = dram_in("w_down", [8 * 128, 22 * 128])
    pool_w_d = dram_in("pool_w", [4 * 256, 256])
    ptab_d = dram_in("ptab", [16 * 128, 19 * 128])
    phf_d = dram_in("phf", [128, 8])
    out_d = nc.dram_tensor("out", [D, TOK], F32, kind="ExternalOutput").ap()
    dbg_d = nc.dram_tensor("dbg", [128, 2048], F32, kind="ExternalOutput").ap() if DEBUG else None

    def sb(name, shape, dt=F32):
        return stack.enter_context(nc.sbuf_tensor(name, list(shape), dt))

    ps = [stack.enter_context(nc.psum_tensor(f"ps{i}", [128, 512], F32)) for i in range(8)]

    def dma(eng, out, in_, r, w, **kw):
        P.add(eng, lambda e, out=out, in_=in_, kw=kw: e.dma_start(out=out, in_=in_, **kw), r, w, dma=True)

    vecF = sb("vecF_s", [128, NVEC])
    consts = sb("consts_s", [128, NCONST])
    cbf = sb("consts_bf", [128, NCONST], BF16)
    cwF = sb("cwF_s", [128, NE, 10, 5])
    rowB = sb("rowB_s", [128, NROW])
    gfl = sb("gfl_s", [128, NG, 2])
    cs2 = sb("cs2", [128, 8, 2])
    modT = sb("modT", [128, 48, 2])
    AB = sb("AB", [128, 6, 8])
    Wdt = sb("Wdt", [128, NE, 8, 16], BF16)
    aneg = sb("aneg", [128, NE * 16])
    F1 = sb("F1", [128, 6144])
    F2 = sb("F2", [128, 8192])
    modrow = F1[0:2, :]
    wst = [F2[:, 0:2048].rearrange("p (k n) -> p k n", k=8), F2[:, 2048:4096].rearrange("p (k n) -> p k n", k=8)]
    wdt_f = F2[:, 4096:4096 + NE * 128].rearrange("p (a b c) -> p a b c", a=NE, b=8)
    badat = [sb(f"bada{i}", [1, 256]) for i in range(2)]

    dma("sp", vecF[:], vecF_d[:, :], [], ["vecF"])
    dma("sp", consts[:], consts_d[:, :], [], ["consts"])
    dma("sp", cwF[:], cwF_d[:, :, :, :], [], ["cwF"])
    dma("sp", rowB[:], rowB_d[:, :], [], ["rowB"])
    dma("sp", gfl[:].rearrange("p g t -> p (g t)"), gfl_d[:, :], [], ["gfl"])
    dma("sp", F2[:, 4096:4096 + NE * 128], wdt_d[:, :], [], ["wdt_f"])
    P.add("dve", lambda e: e.tensor_copy(out=cbf[:], in_=consts[:]), ["consts"], ["cbf"])
    P.add("dve", lambda e: e.tensor_copy(out=Wdt[:].rearrange("p a b c -> p (a b c)"), in_=F2[:, 4096:4096 + NE * 128]),
          ["wdt_f"], ["Wdt"])
    P.add("act", lambda e: e.activation(out=cs2[:, :, 0], in_=vecF[:, 16:24], func=AF.Silu), ["vecF"], ["cs2a"])
    P.add("act", lambda e: e.activation(out=cs2[:, :, 1], in_=vecF[:, 24:32], func=AF.Silu), ["vecF"], ["cs2b"])
    P.add("act", lambda e: e.activation(out=aneg[:], in_=rowB[:, R_ALG:R_ALG + NE * 16], func=AF.Exp), ["rowB"], ["aneg"])
    P.add("dve", lambda e: e.tensor_scalar(out=aneg[:], in0=aneg[:], scalar1=-1.0, scalar2=None, op0=ALU.mult), ["aneg"], ["aneg"])

    w_ada_v = w_ada_d.rearrange("(kc p) n -> p kc n", p=128)
    for g in range(24):
        st = wst[g % 2]
        key = f"wst{g % 2}"
        bt = badat[g % 2]
        bk = f"bada{g % 2}"
        dma("sp", st, w_ada_v[:, :, g * 256:(g + 1) * 256], [], [key])
        dma("sp", bt[:], b_ada_d[:, g * 256:(g + 1) * 256], [], [bk])
        pt = ps[g % 2]
        pk = f"ps{g % 2}"
        for kc in range(8):
            P.add("pe", lambda e, pt=pt, st=st, kc=kc: e.matmul(pt[0:2, 0:256], lhsT=cs2[:, kc, :], rhs=st[:, kc, :],
                                                               start=(kc == 0), stop=False),
                  ["cs2a", "cs2b", key], [pk])
        P.add("pe", lambda e, pt=pt, bt=bt: e.matmul(pt[0:2, 0:256], lhsT=consts[0:1, C_ONE:C_ONE + 2], rhs=bt[0:1, :],
                                                     start=False, stop=True), ["consts", bk], [pk])
        P.add("act", lambda e, pt=pt, g=g: e.activation(out=modrow[:, g * 256:(g + 1) * 256], in_=pt[0:2, 0:256], func=AF.Copy),
              [pk], ["modrow"])
    for blk in range(48):
        P.add("pe", lambda e, blk=blk: e.matmul(ps[2][:, blk * 2:blk * 2 + 2], lhsT=modrow[:, blk * 128:(blk + 1) * 128],
                                                rhs=consts[0:2, C_I:C_I + 2], start=True, stop=True),
              ["modrow", "consts"], ["ps2"])
    P.add("dve", lambda e: e.tensor_copy(out=modT[:].rearrange("p a b -> p (a b)"), in_=ps[2][:, 0:96]), ["ps2"], ["modT"])

    def affine(outA, outB, sc, sh, g, b):
        P.add("dve", lambda e: e.scalar_tensor_tensor(out=outA, in0=sc, scalar=1.0, in1=g, op0=ALU.add, op1=ALU.mult),
              ["modT", "vecF"], ["AB"])
        P.add("dve", lambda e: e.scalar_tensor_tensor(out=outB, in0=sc, scalar=1.0, in1=b, op0=ALU.add, op1=ALU.mult),
              ["modT", "vecF", "AB"], ["AB"])
        P.add("dve", lambda e: e.tensor_tensor(out=outB, in0=outB, in1=sh, op=ALU.add), ["AB", "modT"], ["AB"])

    affine(AB[:, 0, :], AB[:, 1, :], modT[:, 8:16, 0], modT[:, 0:8, 0], vecF[:, 0:8], vecF[:, 8:16])
    affine(AB[:, 2, :], AB[:, 3, :], modT[:, 8:16, 1], modT[:, 0:8, 1], vecF[:, 0:8], vecF[:, 8:16])
    affine(AB[:, 4, :], AB[:, 5, :], modT[:, 32:40, 0], modT[:, 24:32, 0], vecF[:, 58:66], vecF[:, 66:74])
    epsc = sb("epsc", [128, 1])
    P.add("pool", lambda e: e.memset(epsc[:], EPS), [], ["epsc"])
    P.fence()

    dbg = sb("dbg_s", [128, 2048]) if DEBUG else None
    if DEBUG:
        P.add("pool", lambda e: e.memset(dbg[:], 0.0), [], ["dbg"])

    def tap(col0, src, keys, npart=128):
        n = src.shape[-1]
        P.add("dve", lambda e: e.tensor_copy(out=dbg[0:npart, col0:col0 + n], in_=src), list(keys) + ["dbg"], ["dbg"])

    xt = [F1[:, 0:1024], F1[:, 1024:2048]]
    Hrun, Hf_fin, hf_ctx, hb_ctx, Htmp = (F2[:, i * 1024:(i + 1) * 1024] for i in range(5))
    xh4t = F1[0:4, 2048:3072]
    xnb = [sb(f"xnb{i}", [128, D], BF16) for i in range(2)]
    xnb4 = sb("xnb4", [4, D], BF16)
    st6 = [sb(f"st6_{i}", [128, 2, 6]) for i in range(2)]
    mv = [sb(f"mv{i}", [128, 4]) for i in range(2)]
    st6h = sb("st6h", [4, 2, 6])
    mvh = sb("mvh", [4, 4])
    hT = sb("hT", [128, 8, 516], BF16)
    ws = [sb(f"ws{i}", [128, 8, 512], BF16) for i in range(2)]
    U = sb("U", [128, 10, 516], BF16)
    XC = sb("XC", [128, 10, 512], BF16)
    Dsl = sb("Dsl", [128, 2, 5, 128], BF16)
    dtt_a = sb("dtt", [128, 2, 4, 6, 16])
    ew_a = sb("ew", [128, 2, 4, 32])
    par = [0]
    xw = [sb(f"xw{i}", [128, D], BF16) for i in range(2)]
    Btok = [sb(f"Btok{i}", [128, 128], BF16) for i in range(2)]
    Hsnap = sb("Hsnap", [128, 4, D], BF16)
    in_proj_v = in_proj_d.rearrange("(kc p) n -> p kc n", p=128)
    ws_ctr = [0]
    cur_entry = [None]

    def ln_tile(src_ap, xt_ap, xk, st_t, mv_t, sk, out_bf, ok, npart):
        dma("sp", xt_ap, src_ap, [], [xk])
        P.add("dve", lambda e: e.bn_stats(out=st_t[0:npart, 0, :], in_=xt_ap[:, 0:512]), [xk], [sk])
        P.add("dve", lambda e: e.bn_stats(out=st_t[0:npart, 1, :], in_=xt_ap[:, 512:1024]), [xk, sk], [sk])
        P.add("dve", lambda e: e.bn_aggr(out=mv_t[0:npart, 0:2], in_=st_t[0:npart, :, :].rearrange("p a b -> p (a b)")), [sk], [sk])
        P.add("act", lambda e: e.activation(out=mv_t[0:npart, 2:3], in_=mv_t[0:npart, 1:2], func=AF.Ln, bias=epsc[0:npart, 0:1]), [sk, "epsc"], [sk])
        P.add("act", lambda e: e.activation(out=mv_t[0:npart, 2:3], in_=mv_t[0:npart, 2:3], func=AF.Exp, scale=-0.5), [sk], [sk])
        P.add("dve", lambda e: e.tensor_scalar(out=mv_t[0:npart, 3:4], in0=mv_t[0:npart, 0:1], scalar1=mv_t[0:npart, 2:3], scalar2=-1.0,
                                               op0=ALU.mult, op1=ALU.mult), [sk], [sk])
        P.add("dve", lambda e: e.tensor_scalar(out=out_bf, in0=xt_ap, scalar1=mv_t[0:npart, 2:3], scalar2=mv_t[0:npart, 3:4], op0=ALU.mult, op1=ALU.add),
              [xk, sk], [ok])

    def evac_affine(i_op, out_ap, in_ap, a_idx, kc, r, w):
        if i_op % 2 == 0:
            P.add("act", lambda e: e.activation(out=out_ap, in_=in_ap, func=AF.Identity, scale=AB[:, a_idx, kc:kc + 1],
                                                bias=AB[:, a_idx + 1, kc:kc + 1]), r + ["AB"], w)
        else:
            P.add("dve", lambda e: e.tensor_scalar(out=out_ap, in0=in_ap, scalar1=AB[:, a_idx, kc:kc + 1],
                                                   scalar2=AB[:, a_idx + 1, kc:kc + 1], op0=ALU.mult, op1=ALU.add), r + ["AB"], w)

    def stream_w(c0, w):
        s = ws_ctr[0] % 2
        ws_ctr[0] += 1
        if (c0, w) in IPK_TILES:
            t_ = IPK_TILES.index((c0, w))
            flat = ws[s][:].rearrange("p a b -> p (a b)")[:, 0:8 * w]
            dma("pool", flat, ipk_d[t_ * 128:(t_ + 1) * 128, 0:8 * w], [], [f"ws{s}"])
            return flat.rearrange("p (k n) -> p k n", k=8), f"ws{s}"
        dma("pool", ws[s][:, :, 0:w], in_proj_v[:, :, c0:c0 + w], [], [f"ws{s}"])
        return ws[s], f"ws{s}"

    def set_entry(e_):
        cur_entry[0] = e_

    def front(gi):
        t0, n, e_, is_ctx = GROUPS[gi]
        N = 128 * n
        a_idx = 2 if is_ctx else 0
        for i in range(n):
            s = i % 2
            ln_tile(xs[(t0 + i) * 128:(t0 + i + 1) * 128, :], xt[s], f"xt{s}", st6[s], mv[s], f"mv{s}", xnb[s][:], f"xnb{s}", 128)
            for kc in range(8):
                P.add("pe", lambda e, s=s, kc=kc: e.matmul(ps[kc // 4][:, (kc % 4) * 128:(kc % 4 + 1) * 128],
                                                           lhsT=xnb[s][:, kc * 128:(kc + 1) * 128], rhs=cbf[:, C_I:C_I + 128],
                                                           start=True, stop=True), [f"xnb{s}", "cbf"], [f"ps{kc // 4}"])
            for kc in range(8):
                evac_affine(kc, hT[:, kc, i * 128:(i + 1) * 128], ps[kc // 4][:, (kc % 4) * 128:(kc % 4 + 1) * 128], a_idx, kc,
                            [f"ps{kc // 4}"], ["hT"])
        ln_tile(xh_d[gi * 4:gi * 4 + 4, :], xh4t, "xh4", st6h, mvh, "mvh", xnb4[:], "xnb4", 4)
        for kc in range(8):
            P.add("pe", lambda e, kc=kc: e.matmul(ps[4][:, 256 + kc * 4:256 + kc * 4 + 4], lhsT=xnb4[:, kc * 128:(kc + 1) * 128],
                                                  rhs=cbf[0:4, C_I:C_I + 4], start=True, stop=True), ["xnb4", "cbf"], ["ps4h"])
        for kc in range(8):
            evac_affine(kc, hT[:, kc, N:N + 4], ps[4][:, 256 + kc * 4:256 + kc * 4 + 4], a_idx, kc, ["ps4h"], ["hT"])
        return N

    def proj_feat(gi, N, chunks):
        i = 0
        while i < len(chunks):
            grp = chunks[i:i + 4]
            c0 = grp[0][1]
            wt, wk = stream_w(c0, 128 * len(grp))
            for j, (c, col) in enumerate(grp):
                assert col == c0 + 128 * j
                pu = ps[2 + c % 2]
                pk = f"ps{2 + c % 2}"
                for kc in range(8):
                    P.add("pe", lambda e, pu=pu, wt=wt, kc=kc, j=j: e.matmul(pu[:, 0:N], lhsT=wt[:, kc, j * 128:(j + 1) * 128],
                                                                            rhs=hT[:, kc, 0:N], start=(kc == 0), stop=(kc == 7)),
                          [wk, "hT"], [pk])
                for kc in range(8):
                    P.add("pe", lambda e, wt=wt, kc=kc, j=j, c=c: e.matmul(ps[4][:, c * 4:c * 4 + 4], lhsT=wt[:, kc, j * 128:(j + 1) * 128],
                                                                          rhs=hT[:, kc, N:N + 4], start=(kc == 0), stop=(kc == 7)),
                          [wk, "hT"], ["ps4u"])
                P.add("act", lambda e, pu=pu, c=c: e.activation(out=U[:, c, 2:N + 2], in_=pu[:, 0:N], func=AF.Copy), [pk], ["U"])
            i += 4
        nch = len(chunks)
        cs = [c for c, _ in chunks]
        assert cs == list(range(cs[0], cs[0] + nch))
        ph = ps[4][:, cs[0] * 4:(cs[0] + nch) * 4].rearrange("p (c f) -> p c f", f=4)
        P.add("dve", lambda e: e.tensor_scalar(out=U[:, cs[0]:cs[0] + nch, 0:2], in0=ph[:, :, 0:2], scalar1=gfl[:, gi, 0:1], scalar2=None,
                                               op0=ALU.mult), ["ps4u", "gfl"], ["U"])
        P.add("dve", lambda e: e.tensor_scalar(out=U[:, cs[0]:cs[0] + nch, N + 2:N + 4], in0=ph[:, :, 2:4], scalar1=gfl[:, gi, 1:2],
                                               scalar2=None, op0=ALU.mult), ["ps4u", "gfl"], ["U"])

    def dt_tile(e_, i, dcol, Wsrc=None, bias_ap=None, wkey="Wdt", bkey="rowB"):
        Wsrc = Wdt if Wsrc is None else Wsrc
        pp_ = par[0]
        dtt = dtt_a[:, pp_]
        ew = ew_a[:, pp_]
        bias_ap = rowB[:, R_DTB + e_ * 16:R_DTB + e_ * 16 + 16] if bias_ap is None else bias_ap
        pd = ps[4][:, 64 + dcol:64 + dcol + 16]
        for kc in range(8):
            P.add("pe", lambda e, kc=kc: e.matmul(pd, lhsT=hT[:, kc, i * 128:(i + 1) * 128], rhs=Wsrc[:, e_, kc, :],
                                                  start=(kc == 0), stop=(kc == 7)), ["hT", wkey], ["ps4d"])
        v, av, lv, dt_, dta, wd = (dtt[:, i, j, :] for j in range(6))
        k = f"dtt{pp_}_{i}"
        P.add("dve", lambda e: e.tensor_tensor(out=v, in0=pd, in1=bias_ap, op=ALU.add), ["ps4d", bkey], [k])
        P.add("act", lambda e: e.activation(out=av, in_=v, func=AF.Abs), [k], [k])
        P.add("act", lambda e: e.activation(out=av, in_=av, func=AF.Exp, scale=-1.0), [k], [k])
        P.add("act", lambda e: e.activation(out=lv, in_=av, func=AF.Ln, bias=1.0), [k], [k])
        P.add("dve", lambda e: e.scalar_tensor_tensor(out=dt_, in0=v, scalar=0.0, in1=lv, op0=ALU.max, op1=ALU.add), [k], [k])
        P.add("dve", lambda e: e.tensor_tensor(out=dta, in0=dt_, in1=aneg[:, e_ * 16:e_ * 16 + 16], op=ALU.mult), [k, "aneg"], [k])
        pw = ps[4][:, 192:224]
        P.add("pe", lambda e: e.matmul(pw[:, 0:16], lhsT=consts[:, C_TGT:C_TGT + 128], rhs=dta, start=True, stop=True), [k, "consts"], ["ps4w"])
        P.add("pe", lambda e: e.matmul(pw[:, 16:32], lhsT=consts[:, C_ONE:C_ONE + 128], rhs=dta, start=True, stop=True), [k, "consts"], ["ps4w"])
        P.add("act", lambda e: e.activation(out=ew[:, i, :], in_=pw, func=AF.Exp), ["ps4w"], [f"ew{pp_}_{i}"])
        P.add("dve", lambda e: e.tensor_tensor(out=wd, in0=ew[:, i, 0:16], in1=dt_, op=ALU.mult), [f"ew{pp_}_{i}", k], [k])

    def dt_group_a(e_, Wsrc, bias_ap, wkey, bkey):
        pp_ = par[0]
        dtt = dtt_a[:, pp_]
        pd4 = ps[4][:, 64:128].rearrange("p (t h) -> p t h", t=4)
        for i in range(4):
            for kc in range(8):
                P.add("pe", lambda e, kc=kc, i=i: e.matmul(ps[4][:, 64 + 16 * i:80 + 16 * i], lhsT=hT[:, kc, i * 128:(i + 1) * 128], rhs=Wsrc[:, e_, kc, :],
                                                           start=(kc == 0), stop=(kc == 7)), ["hT", wkey], ["ps4"])
        v, av, lv, dt_, dta, wd = (dtt[:, :, j, :] for j in range(6))
        ks = [f"dtt{pp_}_{i}" for i in range(4)]
        P.add("dve", lambda e: e.tensor_tensor(out=v, in0=pd4, in1=bias_ap.unsqueeze(1).to_broadcast([128, 4, 16]), op=ALU.add), ["ps4", bkey], ks)
        P.add("act", lambda e: e.activation(out=av, in_=v, func=AF.Abs), ks, ks)
        P.add("act", lambda e: e.activation(out=av, in_=av, func=AF.Exp, scale=-1.0), ks, ks)
        P.add("act", lambda e: e.activation(out=lv, in_=av, func=AF.Ln, bias=1.0), ks, ks)
        P.add("dve", lambda e: e.scalar_tensor_tensor(out=dt_, in0=v, scalar=0.0, in1=lv, op0=ALU.max, op1=ALU.add), ks, ks)
        P.add("dve", lambda e: e.tensor_tensor(out=dta, in0=dt_, in1=aneg[:, e_ * 16:e_ * 16 + 16].unsqueeze(1).to_broadcast([128, 4, 16]), op=ALU.mult),
              ks + ["aneg"], ks)

    def dt_group_b(pp_):
        dtt = dtt_a[:, pp_]
        ew = ew_a[:, pp_]
        v, av, lv, dt_, dta, wd = (dtt[:, :, j, :] for j in range(6))
        ks = [f"dtt{pp_}_{i}" for i in range(4)]
        es = [f"ew{pp_}_{i}" for i in range(4)]
        for i in range(4):
            P.add("pe", lambda e, i=i: e.matmul(ps[4][:, 128 + 32 * i:144 + 32 * i], lhsT=consts[:, C_TGT:C_TGT + 128], rhs=dtt[:, i, 4, :],
                                                start=True, stop=True), ks + ["consts"], ["ps4"])
            P.add("pe", lambda e, i=i: e.matmul(ps[4][:, 144 + 32 * i:160 + 32 * i], lhsT=consts[:, C_ONE:C_ONE + 128], rhs=dtt[:, i, 4, :],
                                                start=True, stop=True), ks + ["consts"], ["ps4"])
        P.add("act", lambda e: e.activation(out=ew[:, :, :], in_=ps[4][:, 128:256].rearrange("p (t h) -> p t h", t=4), func=AF.Exp), ["ps4"], es)
        P.add("dve", lambda e: e.tensor_tensor(out=wd, in0=ew[:, :, 0:16], in1=dt_, op=ALU.mult), es + ks, ks)

    def conv(N, nch):
        e_ = cur_entry[0]
        for c in range(nch):
            sl_ = c % 2
            for k in range(5):
                if k % 2 == 0:
                    P.add("dve", lambda e, c=c, k=k, sl_=sl_: e.tensor_scalar(out=Dsl[:, sl_, k, :], in0=consts[:, C_I:C_I + 128],
                                                                             scalar1=cwF[:, e_, c, k:k + 1], scalar2=None, op0=ALU.mult),
                          ["consts", "cwF"], [f"Dsl{sl_}_{k}"])
                else:
                    P.add("act", lambda e, c=c, k=k, sl_=sl_: e.activation(out=Dsl[:, sl_, k, :], in_=consts[:, C_I:C_I + 128], func=AF.Copy,
                                                                          scale=cwF[:, e_, c, k:k + 1]), ["consts", "cwF"], [f"Dsl{sl_}_{k}"])
            for k in range(5):
                P.add("pe", lambda e, c=c, k=k, sl_=sl_: e.matmul(ps[5][:, 0:N], lhsT=Dsl[:, sl_, k, :], rhs=U[:, c, k:k + N],
                                                                 start=(k == 0), stop=(k == 4)), [f"Dsl{sl_}_{k}", "U"], ["ps5"])
            P.add("act", lambda e, c=c: e.activation(out=XC[:, c, 0:N], in_=ps[5][:, 0:N], func=AF.Silu, bias=vecF[:, 48 + c:49 + c]),
                  ["ps5", "vecF"], ["XC"])

    def states_tile(i, pp_=None):
        pp_ = par[0] if pp_ is None else pp_
        dtt = dtt_a[:, pp_]
        ew = ew_a[:, pp_]
        s = i % 2
        for c in range(8):
            P.add("pe", lambda e, c=c: e.matmul(ps[c // 4][:, (c % 4) * 128:(c % 4 + 1) * 128], lhsT=XC[:, c, i * 128:(i + 1) * 128],
                                                rhs=cbf[:, C_I:C_I + 128], start=True, stop=True), ["XC", "cbf"], [f"ps{c // 4}"])
        P.add("pe", lambda e: e.matmul(ps[4][:, 384:512], lhsT=XC[:, 8, i * 128:(i + 1) * 128], rhs=cbf[:, C_I:C_I + 128],
                                       start=True, stop=True), ["XC", "cbf"], ["ps4b"])
        for b in range(2):
            P.add("dve", lambda e, b=b: e.tensor_tensor(out=xw[s][:, b * 512:(b + 1) * 512].rearrange("p (h q) -> p h q", q=64),
                                                        in0=ps[b][:, :].rearrange("p (h q) -> p h q", q=64),
                                                        in1=dtt[:, i, 5, b * 8:(b + 1) * 8].unsqueeze(2).to_broadcast([128, 8, 64]),
                                                        op=ALU.mult), [f"ps{b}", f"dtt{pp_}_{i}"], [f"xw{s}"])
        P.add("act", lambda e: e.activation(out=Btok[s][:], in_=ps[4][:, 384:512], func=AF.Copy), ["ps4b"], [f"Btok{s}"])
        for b in range(2):
            P.add("pe", lambda e, b=b: e.matmul(ps[6 + b][:, :], lhsT=Btok[s][:], rhs=xw[s][:, b * 512:(b + 1) * 512],
                                                start=True, stop=True), [f"Btok{s}", f"xw{s}"], [f"ps{6 + b}"])
        P.add("dve", lambda e: e.tensor_tensor(out=Htmp.rearrange("p (h q) -> p h q", q=64), in0=Hrun.rearrange("p (h q) -> p h q", q=64),
                                                in1=ew[:, i, 16:32].unsqueeze(2).to_broadcast([128, 16, 64]), op=ALU.mult),
              ["Hrun", f"ew{pp_}_{i}"], ["Htmp"])
        for b in range(2):
            P.add("dve", lambda e, b=b: e.tensor_tensor(out=Hrun[:, b * 512:(b + 1) * 512], in0=Htmp[:, b * 512:(b + 1) * 512],
                                                        in1=ps[6 + b][:, :], op=ALU.add), ["Htmp", f"ps{6 + b}"], ["Hrun"])

    def states_group(gi):
        t0, n, e_, is_ctx = GROUPS[gi]
        par[0] = 0
        dtt = dtt_a[:, 0]
        ew = ew_a[:, 0]
        set_entry(e_)
        N = front(gi)
        proj_feat(gi, N, [(c, 1024 + 128 * c) for c in range(9)])
        for i in range(n):
            dt_tile(e_, i, 0)
        conv(N, 9)
        if DEBUG and gi == DBG_GROUP:
            tap(0, hT[:, 0, 0:260], ["hT"])
            tap(260, U[:, 0, 0:260], ["U"])
            tap(520, XC[:, 0, 0:256], ["XC"])
            tap(776, XC[:, 8, 0:256], ["XC"])
            tap(1032, dtt[:, 0, :, :].rearrange("p a b -> p (a b)"), ["dtt0_0"])
            tap(1128, ew[:, 0, :], ["ew0_0"])
            tap(1160, AB[:].rearrange("p a b -> p (a b)"), ["AB"])
            tap(1208, mv[0][:, :], ["mv0"])
            tap(1212, modT[:].rearrange("p a b -> p (a b)"), ["modT"])
        for i in range(n):
            states_tile(i)
            if DEBUG and gi == DBG_GROUP and i == 0:
                tap(1308, xw[0][:, 0:512], ["xw0"])
                tap(1820, Btok[0][:, :], ["Btok0"])
                tap(1948, Hrun[:, 0:512], ["Hrun"])

    RB = sb("RB", [128, 12288], BF16)
    PT_LIST = [(0, -1), (0, 0)] + [(1, d_) for d_ in (-1, 0, 1)] + [(2, d_) for d_ in range(-2, 3)] + [(3, d_) for d_ in range(-4, 5)]

    def pool_phase():
        out_v = out_d.rearrange("(kc p) t -> p kc t", p=128)
        Wpl = sb("Wpl", [128, 8, 256], BF16)
        tabs = sb("ptabs", [128, 19, 128], BF16)
        phf = sb("phf_s", [128, 8])
        dma("sp", phf[:], phf_d[:, :], [], ["phf"])
        dma("pool", Wpl[:], pool_w_d.rearrange("(a p) o -> p a o", p=128), [], ["Wpl"])
        upw = RB.rearrange("p (s f) -> p s f", s=12)
        dT = XC[:, 0:8, :]
        stage = F1[:, 2048:6144].rearrange("p (k t) -> p k t", k=8)

        def up_tiles(taus):
            wts = []
            for h in range(2):
                wts.append(stream_w(2336 + 512 * h, 512))
            for tau in taus:
                tix = (T_PHA + tau) if tau < 4 else ((T_OWN + tau - 4) if tau < 20 else (T_PHB + tau - 20))
                s = tau % 2
                ln_tile(xs[tix * 128:(tix + 1) * 128, :], xt[s], f"xt{s}", st6[s], mv[s], f"mv{s}", xnb[s][:], f"xnb{s}", 128)
                for kc in range(8):
                    P.add("pe", lambda e, s=s, kc=kc: e.matmul(ps[kc // 4][:, (kc % 4) * 128:(kc % 4 + 1) * 128],
                                                               lhsT=xnb[s][:, kc * 128:(kc + 1) * 128], rhs=cbf[:, C_I:C_I + 128],
                                                               start=True, stop=True), [f"xnb{s}", "cbf"], [f"ps{kc // 4}"])
                for kc in range(8):
                    evac_affine(kc, hT[:, kc, 0:128], ps[kc // 4][:, (kc % 4) * 128:(kc % 4 + 1) * 128], 0, kc, [f"ps{kc // 4}"], ["hT"])
                for h in range(2):
                    wt, wk = wts[h]
                    for kc in range(8):
                        P.add("pe", lambda e, wt=wt, kc=kc, h=h: e.matmul(ps[2 + h][:, :], lhsT=hT[:, kc, 0:128], rhs=wt[:, kc, :],
                                                                         start=(kc == 0), stop=(kc == 7)), ["hT", wk], [f"ps{2 + h}"])
                    dst = upw[:, tau % 12, h * 512:(h + 1) * 512]
                    if tau < 4 or tau >= 20:
                        fc = tau if tau < 4 else tau - 16
                        P.add("act", lambda e, dst=dst, h=h, fc=fc: e.activation(out=dst, in_=ps[2 + h][:, :], func=AF.Copy, scale=phf[:, fc:fc + 1]),
                              [f"ps{2 + h}", "phf"], [f"upw{tau % 12}"])
                    else:
                        P.add("act", lambda e, dst=dst, h=h: e.activation(out=dst, in_=ps[2 + h][:, :], func=AF.Copy), [f"ps{2 + h}"], [f"upw{tau % 12}"])

        def pool_tile(T):
            dma("pool", tabs[:], ptab_d[T * 128:(T + 1) * 128, :].rearrange("p (a b) -> p a b", a=19), [], ["tabs"])
            for cc in range(8):
                g_ = cc // 2
                idxs = [(ix, dl) for ix, (gg, dl) in enumerate(PT_LIST) if gg == g_]
                for n_, (ix, dl) in enumerate(idxs):
                    sl_ = (T + 4 + dl) % 12
                    P.add("pe", lambda e, cc=cc, ix=ix, sl_=sl_, n_=n_, last=(n_ == len(idxs) - 1): e.matmul(
                        ps[6 + cc // 4][:, (cc % 4) * 128:(cc % 4 + 1) * 128], lhsT=upw[:, sl_, cc * 128:(cc + 1) * 128], rhs=tabs[:, ix, :],
                        start=(n_ == 0), stop=last, skip_group_check=True), [f"upw{sl_}", "tabs"], [f"ps{6 + cc // 4}"])
            tq = (T % 4) * 128
            P.add("act", lambda e: e.activation(out=dT[:, 0:4, tq:tq + 128], in_=ps[6][:, :].rearrange("p (c t) -> p c t", c=4), func=AF.Copy),
                  ["ps6"], ["XC"])
            P.add("dve", lambda e: e.tensor_copy(out=dT[:, 4:8, tq:tq + 128], in_=ps[7][:, :].rearrange("p (c t) -> p c t", c=4)), ["ps7"], ["XC"])

        def pool_group(gq):
            for T in range(4 * gq, 4 * gq + 4):
                pool_tile(T)
            for oc in range(8):
                g_ = oc // 2
                pp = ps[2 + oc % 2]
                for ic in range(2):
                    P.add("pe", lambda e, pp=pp, g_=g_, ic=ic, oc=oc: e.matmul(pp[:, :], lhsT=Wpl[:, g_ * 2 + ic, (oc % 2) * 128:(oc % 2 + 1) * 128],
                                                                              rhs=dT[:, 2 * g_ + ic, :], start=(ic == 0), stop=(ic == 1)),
                          ["Wpl", "XC"], [f"ps{2 + oc % 2}"])
                P.add("act", lambda e, pp=pp, oc=oc: e.activation(out=stage[:, oc, :], in_=pp[:, :], func=AF.Copy, scale=vecF[:, 40 + oc:41 + oc]),
                      [f"ps{2 + oc % 2}", "vecF"], ["stage"])
            dma("sp", out_v[:, :, 512 * gq:512 * (gq + 1)], stage, ["stage"], [f"outg{gq}"])

        up_tiles(range(0, 12))
        for gq in range(POOL_GROUPS):
            if gq > 0:
                up_tiles(range(8 + 4 * gq, 12 + 4 * gq))
            pool_group(gq)
        P.fence()

    if RUN_POOL:
        pool_phase()

    if INJECT:
        dma("sp", Hf_fin, inj_d[:, 0:D], [], ["Hf_fin"])
        for j_ in range(4):
            dma("pool", Hsnap[:, j_, :], inj_d[:, (1 + j_) * D:(2 + j_) * D], [], [f"Hsnap{j_}"])
    def boundary(b):
        swb = rowB[:, R_SW + b:R_SW + b + 1]
        P.add("dve", lambda e: e.scalar_tensor_tensor(out=Hf_fin, in0=Hrun, scalar=swb, in1=Hf_fin, op0=ALU.mult, op1=ALU.add),
              ["Hrun", "Hf_fin", "rowB"], ["Hf_fin"])
        P.add("pool", lambda e: e.tensor_tensor(out=Htmp, in0=hb_ctx, in1=Hrun, op=ALU.subtract), ["hb_ctx", "Hrun"], ["Htmp"])
        P.add("dve", lambda e: e.scalar_tensor_tensor(out=Hrun, in0=Htmp, scalar=swb, in1=Hrun, op0=ALU.mult, op1=ALU.add),
              ["Htmp", "Hrun", "rowB"], ["Hrun"])

    RBW = RB[:, 0:8 * 1152].rearrange("p (k n) -> p k n", k=8)
    DselR = F1[:, 3072:5952].bitcast(BF16).rearrange("p (c k n) -> p c k n", c=9, k=5)
    bproj = sb("bproj", [128, 9])
    bpfl = sb("bpfl", [128, 9, 2, 2])
    Ub = sb("Ub", [128, 9, 516], BF16)
    UH = sb("UH", [128, 9, 4], BF16)
    UH2 = sb("UH2", [128, 9, 4], BF16)
    dtb2 = sb("dtb2", [128, NE * 16])
    dsel_entry = [None]

    def prep_resident():
        P.fence()
        B1rep = F2[:, 6144:7168].rearrange("p (k m) -> p k m", k=8)
        P.add("dve", lambda e: e.tensor_copy(out=B1rep, in_=AB[:, 1, :].unsqueeze(2).to_broadcast([128, 8, 128])), ["AB"], ["B1rep"])
        for g in range(5):
            c0 = 1024 + 256 * g
            ncol = 256 if g < 4 else 128
            st = wst[g % 2]
            key = f"wst{g % 2}"
            dma("sp", st[:, :, 0:ncol], in_proj_v[:, :, c0:c0 + ncol], [], [key])
            for j in range(ncol // 128):
                c = 2 * g + j
                for kc in range(8):
                    P.add("pe", lambda e, st=st, kc=kc, j=j, c=c: e.matmul(ps[2][:, c:c + 1], lhsT=st[:, kc, j * 128:(j + 1) * 128],
                                                                          rhs=AB[:, 1, kc:kc + 1], start=(kc == 0), stop=(kc == 7)),
                          [key, "AB"], ["ps2"])
            for kc in range(8):
                if kc % 2 == 0:
                    P.add("act", lambda e, st=st, kc=kc, c0=c0, ncol=ncol: e.activation(out=RBW[:, kc, c0 - 1024:c0 - 1024 + ncol], in_=st[:, kc, 0:ncol],
                                                                                       func=AF.Copy, scale=AB[:, 0, kc:kc + 1]), [key, "AB"], ["RBW"])
                else:
                    P.add("dve", lambda e, st=st, kc=kc, c0=c0, ncol=ncol: e.tensor_scalar(out=RBW[:, kc, c0 - 1024:c0 - 1024 + ncol], in0=st[:, kc, 0:ncol],
                                                                                          scalar1=AB[:, 0, kc:kc + 1], scalar2=None, op0=ALU.mult),
                          [key, "AB"], ["RBW"])
        P.add("dve", lambda e: e.tensor_copy(out=bproj[:], in_=ps[2][:, 0:9]), ["ps2"], ["bproj"])
        for e_ in range(NE):
            for kc in range(8):
                P.add("pe", lambda e, e_=e_, kc=kc: e.matmul(ps[3][:, e_ * 16:(e_ + 1) * 16], lhsT=B1rep[:, kc, :], rhs=wdt_f[:, e_, kc, :],
                                                             start=(kc == 0), stop=(kc == 7)), ["B1rep", "wdt_f"], ["ps3"])
        P.add("dve", lambda e: e.tensor_tensor(out=dtb2[:], in0=ps[3][:, 0:NE * 16], in1=rowB[:, R_DTB:R_DTB + NE * 16], op=ALU.add),
              ["ps3", "rowB"], ["dtb2"])
        for e_ in range(E_FOR0, E_B7 + 1):
            P.add("dve", lambda e, e_=e_: e.tensor_tensor(out=Wdt[:, e_, :, :], in0=wdt_f[:, e_, :, :],
                                                          in1=AB[:, 0, :].unsqueeze(2).to_broadcast([128, 8, 16]), op=ALU.mult), ["wdt_f", "AB"], ["Wdt"])
        P.fence()

    def set_dsel(e_):
        if dsel_entry[0] == e_:
            return
        dsel_entry[0] = e_
        n_ = 0
        for c in range(9):
            for k in range(5):
                if n_ % 2 == 0:
                    P.add("act", lambda e, c=c, k=k: e.activation(out=DselR[:, c, k, :], in_=consts[:, C_I:C_I + 128], func=AF.Copy,
                                                                  scale=cwF[:, e_, c, k:k + 1]), ["consts", "cwF"], [f"DselR{c}_{k}"])
                else:
                    P.add("dve", lambda e, c=c, k=k: e.tensor_scalar(out=DselR[:, c, k, :], in0=consts[:, C_I:C_I + 128],
                                                                     scalar1=cwF[:, e_, c, k:k + 1], scalar2=None, op0=ALU.mult),
                          ["consts", "cwF"], [f"DselR{c}_{k}"])
                n_ += 1

    def front_ln(gi, i):
        t0, n, e_, is_ctx = GROUPS[gi]
        s = i % 2
        ln_tile(xs[(t0 + i) * 128:(t0 + i + 1) * 128, :], xt[s], f"xt{s}", st6[s], mv[s], f"mv{s}", xnb[s][:], f"xnb{s}", 128)

    def front_tr(gi, i):
        s = i % 2
        for kc in range(8):
            P.add("pe", lambda e, s=s, kc=kc: e.matmul(ps[kc // 4][:, (kc % 4) * 128:(kc % 4 + 1) * 128],
                                                       lhsT=xnb[s][:, kc * 128:(kc + 1) * 128], rhs=cbf[:, C_I:C_I + 128],
                                                       start=True, stop=True), [f"xnb{s}", "cbf"], [f"ps{kc // 4}"])
        P.add("act", lambda e, i=i: e.activation(out=hT[:, 0:4, i * 128:(i + 1) * 128], in_=ps[0][:, :].rearrange("p (k t) -> p k t", k=4),
                                                 func=AF.Copy), ["ps0"], ["hT"])
        P.add("dve", lambda e, i=i: e.tensor_copy(out=hT[:, 4:8, i * 128:(i + 1) * 128], in_=ps[1][:, :].rearrange("p (k t) -> p k t", k=4)),
              ["ps1"], ["hT"])

    UU = [U, Ub]

    def halo_block(gis, UHt=None, uhk="UH"):
        UHt = UH if UHt is None else UHt
        g0_, g3_ = gis[0], gis[-1]
        dma("sp", xh4t[0:2, :], xh_d[g0_ * 4:g0_ * 4 + 2, :], [], ["xh4"])
        dma("sp", xh4t[2:4, :], xh_d[g3_ * 4 + 2:g3_ * 4 + 4, :], [], ["xh4"])
        sk = "mvh"
        P.add("dve", lambda e: e.bn_stats(out=st6h[0:4, 0, :], in_=xh4t[:, 0:512]), ["xh4"], [sk])
        P.add("dve", lambda e: e.bn_stats(out=st6h[0:4, 1, :], in_=xh4t[:, 512:1024]), ["xh4", sk], [sk])
        P.add("dve", lambda e: e.bn_aggr(out=mvh[0:4, 0:2], in_=st6h[0:4, :, :].rearrange("p a b -> p (a b)")), [sk], [sk])
        P.add("act", lambda e: e.activation(out=mvh[0:4, 2:3], in_=mvh[0:4, 1:2], func=AF.Ln, bias=epsc[0:4, 0:1]), [sk, "epsc"], [sk])
        P.add("act", lambda e: e.activation(out=mvh[0:4, 2:3], in_=mvh[0:4, 2:3], func=AF.Exp, scale=-0.5), [sk], [sk])
        P.add("dve", lambda e: e.tensor_scalar(out=mvh[0:4, 3:4], in0=mvh[0:4, 0:1], scalar1=mvh[0:4, 2:3], scalar2=-1.0,
                                               op0=ALU.mult, op1=ALU.mult), [sk], [sk])
        P.add("dve", lambda e: e.tensor_scalar(out=xnb4[:], in0=xh4t, scalar1=mvh[0:4, 2:3], scalar2=mvh[0:4, 3:4], op0=ALU.mult, op1=ALU.add), ["xh4", sk], ["xnb4"])
        for kc in range(8):
            P.add("pe", lambda e, kc=kc: e.matmul(ps[4][:, 256 + kc * 4:256 + kc * 4 + 4], lhsT=xnb4[:, kc * 128:(kc + 1) * 128],
                                                  rhs=cbf[0:4, C_I:C_I + 4], start=True, stop=True), ["xnb4", "cbf"], ["ps4"])
        P.add("act", lambda e: e.activation(out=hT[:, :, 512:516], in_=ps[4][:, 256:288].rearrange("p (k t) -> p k t", k=8), func=AF.Copy),
              ["ps4"], ["hTh"])
        for c in range(9):
            for kc in range(8):
                P.add("pe", lambda e, kc=kc, c=c: e.matmul(ps[4][:, c * 4:c * 4 + 4], lhsT=RBW[:, kc, c * 128:(c + 1) * 128], rhs=hT[:, kc, 512:516],
                                                           start=(kc == 0), stop=(kc == 7)), ["RBW", "hTh"], ["ps4"])
        for sd, gq in ((0, g0_), (1, g3_)):
            P.add("dve", lambda e, sd=sd, gq=gq: e.tensor_scalar(out=bpfl[:, :, sd, :], in0=bproj[:].unsqueeze(2).to_broadcast([128, 9, 2]),
                                                                 scalar1=gfl[:, gq, sd:sd + 1], scalar2=None, op0=ALU.mult), ["bproj", "gfl"], ["bpfl"])
        ph = ps[4][:, 0:36].rearrange("p (c f) -> p c f", f=4)
        P.add("dve", lambda e: e.scalar_tensor_tensor(out=UHt[:, :, 0:2], in0=ph[:, :, 0:2], scalar=gfl[:, g0_, 0:1], in1=bpfl[:, :, 0, :],
                                                      op0=ALU.mult, op1=ALU.add), ["ps4", "gfl", "bpfl"], [uhk])
        P.add("dve", lambda e: e.scalar_tensor_tensor(out=UHt[:, :, 2:4], in0=ph[:, :, 2:4], scalar=gfl[:, g3_, 1:2], in1=bpfl[:, :, 1, :],
                                                      op0=ALU.mult, op1=ALU.add), ["ps4", "gfl", "bpfl"], [uhk])

    def projL(gi, ub):
        Ut = UU[ub]
        uk = "U" if ub == 0 else "Ub"
        for c in range(9):
            pu = ps[2 + c % 2]
            pk = f"ps{2 + c % 2}"
            for kc in range(8):
                P.add("pe", lambda e, pu=pu, kc=kc, c=c: e.matmul(pu[:, :], lhsT=RBW[:, kc, c * 128:(c + 1) * 128], rhs=hT[:, kc, 0:512],
                                                                 start=(kc == 0), stop=(kc == 7)), ["RBW", "hT"], [pk])
            if c % 2 == 0:
                P.add("act", lambda e, pu=pu, c=c: e.activation(out=Ut[:, c, 2:514], in_=pu[:, :], func=AF.Identity, bias=bproj[:, c:c + 1]),
                      [pk, "bproj"], [uk])
            else:
                P.add("dve", lambda e, pu=pu, c=c: e.tensor_scalar(out=Ut[:, c, 2:514], in0=pu[:, :], scalar1=bproj[:, c:c + 1], scalar2=None, op0=ALU.add),
                      [pk, "bproj"], [uk])

    def convL(ub, c0=0, c1=9):
        Ut = UU[ub]
        uk = "U" if ub == 0 else "Ub"
        for c in range(c0, c1):
            for k in range(5):
                P.add("pe", lambda e, c=c, k=k: e.matmul(ps[5][:, :], lhsT=DselR[:, c, k, :], rhs=Ut[:, c, k:k + 512],
                                                         start=(k == 0), stop=(k == 4)), [f"DselR{c}_{k}", uk], ["ps5"])
            P.add("act", lambda e, c=c: e.activation(out=XC[:, c, 0:512], in_=ps[5][:, :], func=AF.Silu, bias=vecF[:, 48 + c:49 + c]),
                  ["ps5", "vecF"], ["XC"])

    def latent_block(gis, snaps=False):
        e_ = GROUPS[gis[0]][2]
        bias_ap = dtb2[:, e_ * 16:e_ * 16 + 16]
        nG = len(gis)
        set_dsel(e_)
        halo_block(gis)
        for i in range(4):
            front_ln(gis[0], i)
            front_tr(gis[0], i)
        projL(gis[0], 0)
        P.add("dve", lambda e: e.tensor_copy(out=U[:, 0:9, 0:2], in_=UH[:, :, 0:2]), ["UH"], ["U"])
        par[0] = 0
        dt_group_a(e_, Wdt, bias_ap, "Wdt", "dtb2")
        dt_group_b(0)
        if nG > 1:
            for i in range(4):
                front_ln(gis[1], i)
                front_tr(gis[1], i)
        for q in range(nG):
            ub, un = q % 2, (q + 1) % 2
            has1 = q + 1 < nG
            has2 = q + 2 < nG
            if has2:
                front_ln(gis[q + 2], 0)
                front_ln(gis[q + 2], 1)
            if has1:
                projL(gis[q + 1], un)
                par[0] = un
                dt_group_a(e_, Wdt, bias_ap, "Wdt", "dtb2")
                kb_, kn_ = ("U", "Ub") if ub == 0 else ("Ub", "U")
                P.add("dve", lambda e, ub=ub, un=un: e.tensor_copy(out=UU[un][:, 0:9, 0:2], in_=UU[ub][:, 0:9, 512:514]), [kb_], [kn_])
                P.add("dve", lambda e, ub=ub, un=un: e.tensor_copy(out=UU[ub][:, 0:9, 514:516], in_=UU[un][:, 0:9, 2:4]), [kn_], [kb_])
            else:
                P.add("dve", lambda e, ub=ub: e.tensor_copy(out=UU[ub][:, 0:9, 514:516], in_=UH[:, :, 2:4]), ["UH"], ["U" if ub == 0 else "Ub"])
            if has2:
                front_tr(gis[q + 2], 0)
                front_tr(gis[q + 2], 1)
                front_ln(gis[q + 2], 2)
                front_ln(gis[q + 2], 3)
            convL(ub, 0, 5)
            if has1:
                dt_group_b(un)
            convL(ub, 5, 9)
            if has2:
                front_tr(gis[q + 2], 2)
                front_tr(gis[q + 2], 3)
            if snaps:
                P.add("act", lambda e, q=q: e.activation(out=Hsnap[:, 3 - q, :], in_=Hrun, func=AF.Copy), ["Hrun"], [f"Hsnap{3 - q}"])
            for i in range(4):
                states_tile(i, ub)

    sfx = sb("sfx", [128, 5, 16])
    wds = sb("wds", [128, 4, 16])

    def states_group4(pp_):
        dtt = dtt_a[:, pp_]
        ew = ew_a[:, pp_]
        ks = [f"dtt{pp_}_{i}" for i in range(4)]
        es = [f"ew{pp_}_{i}" for i in range(4)]
        P.add("dve", lambda e: e.memset(sfx[:, 3, :], 1.0), [], ["sfx"])
        P.add("dve", lambda e: e.tensor_copy(out=sfx[:, 2, :], in_=ew[:, 3, 16:32]), es, ["sfx"])
        P.add("dve", lambda e: e.tensor_tensor(out=sfx[:, 1, :], in0=sfx[:, 2, :], in1=ew[:, 2, 16:32], op=ALU.mult), es + ["sfx"], ["sfx"])
        P.add("dve", lambda e: e.tensor_tensor(out=sfx[:, 0, :], in0=sfx[:, 1, :], in1=ew[:, 1, 16:32], op=ALU.mult), es + ["sfx"], ["sfx"])
        P.add("dve", lambda e: e.tensor_tensor(out=sfx[:, 4, :], in0=sfx[:, 0, :], in1=ew[:, 0, 16:32], op=ALU.mult), es + ["sfx"], ["sfx"])
        P.add("dve", lambda e: e.tensor_tensor(out=wds[:], in0=dtt[:, :, 5, :], in1=sfx[:, 0:4, :], op=ALU.mult), ks + ["sfx"], ["wds"])
        for i in range(4):
            s = i % 2
            for c in range(8):
                P.add("pe", lambda e, c=c, i=i: e.matmul(ps[c // 4][:, (c % 4) * 128:(c % 4 + 1) * 128], lhsT=XC[:, c, i * 128:(i + 1) * 128],
                                                         rhs=cbf[:, C_I:C_I + 128], start=True, stop=True), ["XC", "cbf"], [f"ps{c // 4}"])
            P.add("pe", lambda e, i=i: e.matmul(ps[4][:, 384:512], lhsT=XC[:, 8, i * 128:(i + 1) * 128], rhs=cbf[:, C_I:C_I + 128],
                                                start=True, stop=True), ["XC", "cbf"], ["ps4"])
            for b in range(2):
                P.add("dve", lambda e, b=b, s=s, i=i: e.tensor_tensor(out=xw[s][:, b * 512:(b + 1) * 512].rearrange("p (h q) -> p h q", q=64),
                                                                      in0=ps[b][:, :].rearrange("p (h q) -> p h q", q=64),
                                                                      in1=wds[:, i, b * 8:(b + 1) * 8].unsqueeze(2).to_broadcast([128, 8, 64]),
                                                                      op=ALU.mult), [f"ps{b}", "wds"], [f"xw{s}"])
            P.add("act", lambda e, s=s: e.activation(out=Btok[s][:], in_=ps[4][:, 384:512], func=AF.Copy), ["ps4"], [f"Btok{s}"])
            for b in range(2):
                P.add("pe", lambda e, b=b, s=s, i=i: e.matmul(ps[6 + b][:, :], lhsT=Btok[s][:], rhs=xw[s][:, b * 512:(b + 1) * 512],
                                                              start=(i == 0), stop=(i == 3), skip_group_check=True), [f"Btok{s}", f"xw{s}"], [f"ps{6 + b}"])
        P.add("dve", lambda e: e.tensor_tensor(out=Htmp.rearrange("p (h q) -> p h q", q=64), in0=Hrun.rearrange("p (h q) -> p h q", q=64),
                                               in1=sfx[:, 4, :].unsqueeze(2).to_broadcast([128, 16, 64]), op=ALU.mult), ["Hrun", "sfx"], ["Htmp"])
        for b in range(2):
            P.add("dve", lambda e, b=b: e.tensor_tensor(out=Hrun[:, b * 512:(b + 1) * 512], in0=Htmp[:, b * 512:(b + 1) * 512],
                                                        in1=ps[6 + b][:, :], op=ALU.add), ["Htmp", f"ps{6 + b}"], ["Hrun"])

    def latent_pipeline(blks):
        seq = [(gi, b) for b, gis in enumerate(blks) for gi in gis]
        n = len(seq)
        UHs = [(UH, "UH"), (UH2, "UH2")]
        ukey = lambda u: "U" if u == 0 else "Ub"

        def ent(q):
            return GROUPS[seq[q][0]][2]

        def bias(q):
            return dtb2[:, ent(q) * 16:ent(q) * 16 + 16]

        def front_all(q):
            for i in range(4):
                front_ln(seq[q][0], i)
                front_tr(seq[q][0], i)

        halo_block(blks[0], *UHs[0])
        set_dsel(ent(0))
        front_all(0)
        projL(seq[0][0], 0)
        P.add("dve", lambda e: e.tensor_copy(out=U[:, 0:9, 0:2], in_=UH[:, :, 0:2]), ["UH"], ["U"])
        par[0] = 0
        dt_group_a(ent(0), Wdt, bias(0), "Wdt", "dtb2")
        dt_group_b(0)
        front_all(1)
        for q in range(n):
            gi, b = seq[q]
            ub, un = q % 2, (q + 1) % 2
            has1, has2 = q + 1 < n, q + 2 < n
            firstq, lastq = q % 4 == 0, q % 4 == 3
            uhb, uhbk = UHs[b % 2]
            if has2:
                front_ln(seq[q + 2][0], 0)
                front_ln(seq[q + 2][0], 1)
            if has1:
                g1, b1 = seq[q + 1]
                projL(g1, un)
                par[0] = un
                dt_group_a(ent(q + 1), Wdt, bias(q + 1), "Wdt", "dtb2")
                if b1 == b:
                    P.add("dve", lambda e, ub=ub, un=un: e.tensor_copy(out=UU[un][:, 0:9, 0:2], in_=UU[ub][:, 0:9, 512:514]), [ukey(ub)], [ukey(un)])
                    P.add("dve", lambda e, ub=ub, un=un: e.tensor_copy(out=UU[ub][:, 0:9, 514:516], in_=UU[un][:, 0:9, 2:4]), [ukey(un)], [ukey(ub)])
                else:
                    uhn, uhnk = UHs[b1 % 2]
                    P.add("dve", lambda e, un=un, uhn=uhn: e.tensor_copy(out=UU[un][:, 0:9, 0:2], in_=uhn[:, :, 0:2]), [uhnk], [ukey(un)])
                    P.add("dve", lambda e, ub=ub, uhb=uhb: e.tensor_copy(out=UU[ub][:, 0:9, 514:516], in_=uhb[:, :, 2:4]), [uhbk], [ukey(ub)])
            else:
                P.add("dve", lambda e, ub=ub, uhb=uhb: e.tensor_copy(out=UU[ub][:, 0:9, 514:516], in_=uhb[:, :, 2:4]), [uhbk], [ukey(ub)])
            if has2:
                front_tr(seq[q + 2][0], 0)
                front_tr(seq[q + 2][0], 1)
            if has1:
                dt_group_b(un)
            if has2:
                front_ln(seq[q + 2][0], 2)
                front_ln(seq[q + 2][0], 3)
            convL(ub, 0, 9)
            if has2:
                front_tr(seq[q + 2][0], 2)
                front_tr(seq[q + 2][0], 3)
            if firstq:
                boundary(b)
            if b == 7:
                P.add("act", lambda e, q=q: e.activation(out=Hsnap[:, 3 - (q % 4), :], in_=Hrun, func=AF.Copy), ["Hrun"], [f"Hsnap{3 - (q % 4)}"])
            states_group4(ub)
            if lastq and has1:
                set_dsel(ent(q + 1))
            if q % 4 == 2 and b + 1 < len(blks):
                halo_block(blks[b + 1], *UHs[(b + 1) % 2])

    def b7_block():
        gis = [2 + 28 + j for j in range(4)]
        if RUN_B7:
            latent_block(gis, snaps=True)
        else:
            for q in range(4):
                P.add("act", lambda e, q=q: e.activation(out=Hsnap[:, 3 - q, :], in_=Hrun, func=AF.Copy), ["Hrun"], [f"Hsnap{3 - q}"])

    def run_p1():
        P.add("pool", lambda e: e.memset(Hrun, 0.0), [], ["Hrun"])
        P.add("pool", lambda e: e.memset(Hf_fin, 0.0), [], ["Hf_fin"])
        states_group(0)
        P.add("pool", lambda e: e.tensor_copy(out=hf_ctx, in_=Hrun), ["Hrun"], ["hf_ctx"])
        P.add("pool", lambda e: e.memset(Hrun, 0.0), ["hf_ctx"], ["Hrun"])
        states_group(1)
        P.add("pool", lambda e: e.tensor_copy(out=hb_ctx, in_=Hrun), ["Hrun"], ["hb_ctx"])
        P.add("pool", lambda e: e.tensor_copy(out=Hrun, in_=hf_ctx), ["hf_ctx", "hb_ctx"], ["Hrun"])
        if NFOR_BLOCKS == 7 and RUN_B7 and FLAT_P1:
            latent_pipeline([[2 + 4 * b + j for j in range(4)] for b in range(8)])
        else:
            for b in range(7):
                boundary(b)
                if b < NFOR_BLOCKS:
                    latent_block([2 + 4 * b + j for j in range(4)])
            boundary(7)
            b7_block()

    if not INJECT:
        prep_resident()
        run_p1()

    P.fence()
    xT_own_d_v = xT_own_d.rearrange("(g kc p) t -> g p kc t", kc=8, p=128)
    xres = F1[:, 0:8 * 516].rearrange("p (k t) -> p k t", k=8)
    tmpf = F1[:, 4128:5152]
    sqt = F1[:, 5152:5668]
    rowm = F2[:, 5120:5636]
    rowr = F2[:, 5636:6152]
    Hb = hf_ctx
    yT = RB[:, 0:8192].bitcast(F32).rearrange("p (k t) -> p k t", k=8)
    zT = RB[:, 8192:12288].rearrange("p (k t) -> p k t", k=8)
    rhs1 = F2[:, 6152:7176].rearrange("p (h i) -> p h i", h=8)
    Ee = F2[:, 7176:7688].bitcast(BF16).rearrange("p (h i) -> p h i", h=8)
    scT = [sb(f"scT{i}", [128, 8, 128], BF16) for i in range(2)]
    CBTm = [sb(f"CBTm{i}", [128, 128], BF16) for i in range(2)]
    xdt = xnb
    yofft = sb("yofft", [128, D], BF16)
    HpB1 = sb("HpB", [128, D], BF16)
    HpB = [HpB1, HpB1]
    dt2 = sb("dt2", [128, 4, 2, 6, 16])
    ew2 = sb("ew2", [128, 4, 2, 48])
    ynT = sb("ynT", [128, 8, 512], BF16)
    TM = {0: (C_TGT, C_TLE), 1: (C_TLT, C_TGE)}

    def own_front(g):
        gi = 34 + g
        dma("sp", xres, xT_own_d_v[g], [], ["xres"])
        one = consts[:, C_ONE:C_ONE + 128]
        for kc in range(8):
            P.add("pe", lambda e, kc=kc: e.matmul(ps[6][:, :], lhsT=one, rhs=xres[:, kc, 0:512], start=(kc == 0), stop=(kc == 7)),
                  ["xres", "consts"], ["ps6"])
        for kc in range(8):
            P.add("pe", lambda e, kc=kc: e.matmul(ps[4][:, 300:304], lhsT=one, rhs=xres[:, kc, 512:516], start=(kc == 0), stop=(kc == 7)),
                  ["xres", "consts"], ["ps4"])
        for kc in range(8):
            P.add("act", lambda e, kc=kc: e.activation(out=sqt, in_=xres[:, kc, :], func=AF.Square), ["xres"], ["sqt"])
            P.add("pe", lambda e, kc=kc: e.matmul(ps[7][:, :], lhsT=one, rhs=sqt[:, 0:512], start=(kc == 0), stop=(kc == 7),
                                                  skip_group_check=True), ["sqt", "consts"], ["ps7"])
            P.add("pe", lambda e, kc=kc: e.matmul(ps[5][:, 0:4], lhsT=one, rhs=sqt[:, 512:516], start=(kc == 0), stop=(kc == 7),
                                                  skip_group_check=True), ["sqt", "consts"], ["ps5"])
        ln_rows([(ps[6][:, :], ps[7][:, :], 0, 512, ["ps6", "ps7"]), (ps[4][:, 300:304], ps[5][:, 0:4], 512, 516, ["ps4", "ps5"])])
        for kc in range(8):
            P.add("dve", lambda e, kc=kc: e.tensor_tensor(out=xres[:, kc, :], in0=xres[:, kc, :], in1=rowm, op=ALU.subtract),
                  ["xres", "rowm"], ["xres"])
            P.add("dve", lambda e, kc=kc: e.tensor_tensor(out=xres[:, kc, :], in0=xres[:, kc, :], in1=rowr, op=ALU.mult),
                  ["xres", "rowr"], ["xres"])
        for kc in range(8):
            P.add("act", lambda e, kc=kc: e.activation(out=hT[:, kc, :], in_=xres[:, kc, :], func=AF.Identity, scale=AB[:, 0, kc:kc + 1],
                                                       bias=AB[:, 1, kc:kc + 1]), ["xres", "AB"], ["hT"])
        for kc in range(8):
            P.add("dve", lambda e, kc=kc: e.tensor_scalar(out=xres[:, kc, :], in0=xres[:, kc, :], scalar1=vecF[:, kc:kc + 1],
                                                          scalar2=vecF[:, 8 + kc:9 + kc], op0=ALU.mult, op1=ALU.add), ["xres", "vecF", "hT"], ["xres"])

    def ln_rows(parts):
        for s1, s2, c0, c1, keys in parts:
            P.add("dve", lambda e, s1=s1, c0=c0, c1=c1: e.tensor_scalar(out=rowm[:, c0:c1], in0=s1, scalar1=1.0 / D, scalar2=None, op0=ALU.mult),
                  keys, ["rowm"])
            P.add("dve", lambda e, c0=c0, c1=c1: e.tensor_tensor(out=rowr[:, c0:c1], in0=rowm[:, c0:c1], in1=rowm[:, c0:c1], op=ALU.mult),
                  ["rowm"], ["rowr"])
            P.add("dve", lambda e, s2=s2, c0=c0, c1=c1: e.scalar_tensor_tensor(out=rowr[:, c0:c1], in0=s2, scalar=1.0 / D, in1=rowr[:, c0:c1],
                                                                              op0=ALU.mult, op1=ALU.subtract), keys + ["rowr"], ["rowr"])
            P.add("act", lambda e, c0=c0, c1=c1: e.activation(out=rowr[:, c0:c1], in_=rowr[:, c0:c1], func=AF.Sqrt, bias=epsc[:, 0:1]),
                  ["rowr", "epsc"], ["rowr"])
            P.add("dve", lambda e, c0=c0, c1=c1: e.reciprocal(out=rowr[:, c0:c1], in_=rowr[:, c0:c1]), ["rowr"], ["rowr"])

    def own_proj_z():
        for half in range(2):
            wt, wk = stream_w(512 * half, 512)
            for j in range(4):
                c = 4 * half + j
                pu = ps[2 + c % 2]
                pk = f"ps{2 + c % 2}"
                for kc in range(8):
                    P.add("pe", lambda e, pu=pu, wt=wt, kc=kc, j=j: e.matmul(pu[:, :], lhsT=wt[:, kc, j * 128:(j + 1) * 128], rhs=hT[:, kc, 0:512],
                                                                            start=(kc == 0), stop=(kc == 7)), [wk, "hT"], [pk])
                P.add("act", lambda e, pu=pu, c=c: e.activation(out=zT[:, c, :], in_=pu[:, :], func=AF.Silu), [pk], ["zT"])

    def own_dt(i, d):
        e_ = E_OWNF + d
        k = f"dt2_{i}_{d}"
        pd = ps[4][:, 64:80]
        for kc in range(8):
            P.add("pe", lambda e, kc=kc: e.matmul(pd, lhsT=hT[:, kc, i * 128:(i + 1) * 128], rhs=Wdt[:, e_, kc, :],
                                                  start=(kc == 0), stop=(kc == 7)), ["hT", "Wdt"], ["ps4"])
        v, av, lv, dt_, dta, wd = (dt2[:, i, d, j, :] for j in range(6))
        P.add("dve", lambda e: e.tensor_tensor(out=v, in0=pd, in1=rowB[:, R_DTB + e_ * 16:R_DTB + e_ * 16 + 16], op=ALU.add), ["ps4", "rowB"], [k])
        P.add("act", lambda e: e.activation(out=av, in_=v, func=AF.Abs), [k], [k])
        P.add("act", lambda e: e.activation(out=av, in_=av, func=AF.Exp, scale=-1.0), [k], [k])
        P.add("act", lambda e: e.activation(out=lv, in_=av, func=AF.Ln, bias=1.0), [k], [k])
        P.add("dve", lambda e: e.scalar_tensor_tensor(out=dt_, in0=v, scalar=0.0, in1=lv, op0=ALU.max, op1=ALU.add), [k], [k])
        P.add("dve", lambda e: e.tensor_tensor(out=dta, in0=dt_, in1=aneg[:, e_ * 16:e_ * 16 + 16], op=ALU.mult), [k, "aneg"], [k])
        pw = ps[4][:, 192:240]
        cW, cI = TM[d]
        P.add("pe", lambda e: e.matmul(pw[:, 0:16], lhsT=consts[:, cW:cW + 128], rhs=dta, start=True, stop=True), [k, "consts"], ["ps4"])
        P.add("pe", lambda e: e.matmul(pw[:, 16:32], lhsT=consts[:, C_ONE:C_ONE + 128], rhs=dta, start=True, stop=True), [k, "consts"], ["ps4"])
        P.add("pe", lambda e: e.matmul(pw[:, 32:48], lhsT=consts[:, cI:cI + 128], rhs=dta, start=True, stop=True), [k, "consts"], ["ps4"])
        P.add("act", lambda e: e.activation(out=ew2[:, i, d, :], in_=pw, func=AF.Exp), ["ps4"], [f"ew2_{i}_{d}"])
        P.add("dve", lambda e: e.tensor_tensor(out=wd, in0=ew2[:, i, d, 0:16], in1=dt_, op=ALU.mult), [f"ew2_{i}_{d}", k], [k])

    def own_dt_a(d):
        e_ = E_OWNF + d
        for i in range(4):
            for kc in range(8):
                P.add("pe", lambda e, kc=kc, i=i: e.matmul(ps[4][:, 64 + 16 * i:80 + 16 * i], lhsT=hT[:, kc, i * 128:(i + 1) * 128], rhs=Wdt[:, e_, kc, :],
                                                           start=(kc == 0), stop=(kc == 7)), ["hT", "Wdt"], ["ps4"])
        pd4 = ps[4][:, 64:128].rearrange("p (t h) -> p t h", t=4)
        v, av, lv, dt_, dta, wd = (dt2[:, :, d, j, :] for j in range(6))
        ks = [f"dt2_{i}_{d}" for i in range(4)]
        bias_ap = rowB[:, R_DTB + e_ * 16:R_DTB + e_ * 16 + 16]
        P.add("dve", lambda e: e.tensor_tensor(out=v, in0=pd4, in1=bias_ap.unsqueeze(1).to_broadcast([128, 4, 16]), op=ALU.add), ["ps4", "rowB"], ks)
        P.add("act", lambda e: e.activation(out=av, in_=v, func=AF.Abs), ks, ks)
        P.add("act", lambda e: e.activation(out=av, in_=av, func=AF.Exp, scale=-1.0), ks, ks)
        P.add("act", lambda e: e.activation(out=lv, in_=av, func=AF.Ln, bias=1.0), ks, ks)
        P.add("dve", lambda e: e.scalar_tensor_tensor(out=dt_, in0=v, scalar=0.0, in1=lv, op0=ALU.max, op1=ALU.add), ks, ks)
        P.add("dve", lambda e: e.tensor_tensor(out=dta, in0=dt_, in1=aneg[:, e_ * 16:e_ * 16 + 16].unsqueeze(1).to_broadcast([128, 4, 16]), op=ALU.mult),
              ks + ["aneg"], ks)

    def own_dt_b(d):
        cW, cI = TM[d]
        v, av, lv, dt_, dta, wd = (dt2[:, :, d, j, :] for j in range(6))
        ks = [f"dt2_{i}_{d}" for i in range(4)]
        es = [f"ew2_{i}_{d}" for i in range(4)]
        for i in range(4):
            o = 128 + 48 * i
            P.add("pe", lambda e, i=i, o=o: e.matmul(ps[4][:, o:o + 16], lhsT=consts[:, cW:cW + 128], rhs=dt2[:, i, d, 4, :], start=True, stop=True),
                  ks + ["consts"], ["ps4"])
            P.add("pe", lambda e, i=i, o=o: e.matmul(ps[4][:, o + 16:o + 32], lhsT=consts[:, C_ONE:C_ONE + 128], rhs=dt2[:, i, d, 4, :], start=True, stop=True),
                  ks + ["consts"], ["ps4"])
            P.add("pe", lambda e, i=i, o=o: e.matmul(ps[4][:, o + 32:o + 48], lhsT=consts[:, cI:cI + 128], rhs=dt2[:, i, d, 4, :], start=True, stop=True),
                  ks + ["consts"], ["ps4"])
        P.add("act", lambda e: e.activation(out=ew2[:, :, d, :], in_=ps[4][:, 128:320].rearrange("p (t h) -> p t h", t=4), func=AF.Exp), ["ps4"], es)
        P.add("dve", lambda e: e.tensor_tensor(out=wd, in0=ew2[:, :, d, 0:16], in1=dt_, op=ALU.mult), es + ks, ks)

    def own_tile_dir(i, d, Hst, hkey, first):
        s = d
        cW, cI = TM[d]
        tsl = slice(i * 128, (i + 1) * 128)
        dk, ek = f"dt2_{i}_{d}", f"ew2_{i}_{d}"
        for c in range(8):
            P.add("pe", lambda e, c=c: e.matmul(ps[c // 4][:, (c % 4) * 128:(c % 4 + 1) * 128], lhsT=XC[:, c, tsl],
                                                rhs=cbf[:, C_I:C_I + 128], start=True, stop=True), ["XC", "cbf"], [f"ps{c // 4}"])
        P.add("pe", lambda e: e.matmul(ps[4][:, 384:512], lhsT=XC[:, 8, tsl], rhs=cbf[:, C_I:C_I + 128], start=True, stop=True),
              ["XC", "cbf"], ["ps4"])
        for b in range(2):
            P.add("dve", lambda e, b=b: e.tensor_tensor(out=xdt[s][:, b * 512:(b + 1) * 512].rearrange("p (h q) -> p h q", q=64),
                                                        in0=ps[b][:, :].rearrange("p (h q) -> p h q", q=64),
                                                        in1=dt2[:, i, d, 3, b * 8:(b + 1) * 8].unsqueeze(2).to_broadcast([128, 8, 64]),
                                                        op=ALU.mult), [f"ps{b}", dk], [f"xnb{s}"])
            P.add("dve", lambda e, b=b: e.tensor_tensor(out=xw[s][:, b * 512:(b + 1) * 512].rearrange("p (h q) -> p h q", q=64),
                                                        in0=ps[b][:, :].rearrange("p (h q) -> p h q", q=64),
                                                        in1=dt2[:, i, d, 5, b * 8:(b + 1) * 8].unsqueeze(2).to_broadcast([128, 8, 64]),
                                                        op=ALU.mult), [f"ps{b}", dk], [f"xw{s}"])
        P.add("act", lambda e: e.activation(out=Btok[s][:], in_=ps[4][:, 384:512], func=AF.Copy), ["ps4"], [f"Btok{s}"])
        P.add("pe", lambda e: e.matmul(ps[4][:, 384:512], lhsT=XC[:, 8, tsl], rhs=XC[:, 9, tsl], start=True, stop=True), ["XC", f"Btok{s}"], ["ps4"])
        P.add("dve", lambda e: e.tensor_tensor(out=CBTm[s][:], in0=ps[4][:, 384:512], in1=consts[:, cI:cI + 128], op=ALU.mult),
              ["ps4", "consts"], [f"CBTm{s}"])
        P.add("act", lambda e: e.activation(out=HpB[s][:], in_=Hst, func=AF.Copy), [hkey], ["HpB"])
        for b in range(2):
            P.add("pe", lambda e, b=b: e.matmul(ps[b][:, :], lhsT=XC[:, 9, tsl], rhs=HpB[s][:, b * 512:(b + 1) * 512], start=True, stop=True),
                  ["XC", "HpB"], [f"ps{b}"])
            P.add("dve", lambda e, b=b: e.tensor_tensor(out=yofft[:, b * 512:(b + 1) * 512].rearrange("p (h q) -> p h q", q=64),
                                                        in0=ps[b][:, :].rearrange("p (h q) -> p h q", q=64),
                                                        in1=ew2[:, i, d, 32 + b * 8:32 + (b + 1) * 8].unsqueeze(2).to_broadcast([128, 8, 64]),
                                                        op=ALU.mult), [f"ps{b}", ek], ["yofft"])
        for hf in range(2):
            P.add("dve", lambda e, hf=hf: e.tensor_tensor(out=rhs1,
                                                           in0=consts[:, cI:cI + 128].unsqueeze(1).to_broadcast([128, 8, 128]),
                                                           in1=dt2[:, i, d, 4, hf * 8:(hf + 1) * 8].unsqueeze(2).to_broadcast([128, 8, 128]),
                                                           op=ALU.mult), ["consts", dk], ["rhs1"])
            for q in range(2):
                P.add("pe", lambda e, q=q: e.matmul(ps[2 + q][:, :], lhsT=consts[:, cW:cW + 128],
                                                    rhs=rhs1[:, q * 4:(q + 1) * 4, :].rearrange("p h i -> p (h i)"), start=True, stop=True),
                      ["rhs1", "consts"], [f"ps{2 + q}"])
                P.add("act", lambda e, q=q: e.activation(out=Ee[:, q * 4:(q + 1) * 4, :].rearrange("p h i -> p (h i)"), in_=ps[2 + q][:, :], func=AF.Exp),
                      [f"ps{2 + q}"], ["Ee"])
            sc = scT[hf]
            P.add("dve", lambda e, sc=sc: e.tensor_tensor(out=sc[:], in0=Ee, in1=CBTm[s][:].unsqueeze(1).to_broadcast([128, 8, 128]), op=ALU.mult),
                  ["Ee", f"CBTm{s}"], [f"scT{hf}"])
            pY = ps[6 + hf]
            for hh in range(8):
                h = hf * 8 + hh
                c = hh // 2
                lo = (hh % 2) * 64
                P.add("pe", lambda e, pY=pY, sc=sc, h=h, hh=hh, c=c, lo=lo: e.matmul(
                    pY[lo:lo + 64, c * 128:(c + 1) * 128], lhsT=xdt[s][:, h * 64:(h + 1) * 64], rhs=sc[:, hh, :],
                    start=(hh < 2), stop=False, skip_group_check=True), [f"xnb{s}", f"scT{hf}"], [f"ps{6 + hf}"])
            for c in range(4):
                cc = hf * 4 + c
                P.add("pe", lambda e, pY=pY, c=c, cc=cc: e.matmul(pY[:, c * 128:(c + 1) * 128], lhsT=yofft[:, cc * 128:(cc + 1) * 128],
                                                                 rhs=cbf[:, C_I:C_I + 128], start=False, stop=(c == 3), skip_group_check=True),
                      ["yofft", "cbf"], [f"ps{6 + hf}"])
            ysl = yT[:, hf * 4:(hf + 1) * 4, tsl]
            pv = pY[:, :].rearrange("p (c t) -> p c t", c=4)
            if first:
                P.add("dve", lambda e, ysl=ysl, pv=pv: e.tensor_copy(out=ysl, in_=pv), [f"ps{6 + hf}"], ["yT"])
            else:
                P.add("dve", lambda e, ysl=ysl, pv=pv: e.tensor_tensor(out=ysl, in0=ysl, in1=pv, op=ALU.add), [f"ps{6 + hf}", "yT"], ["yT"])
        for b in range(2):
            P.add("pe", lambda e, b=b: e.matmul(ps[b][:, :], lhsT=Btok[s][:], rhs=xw[s][:, b * 512:(b + 1) * 512], start=True, stop=True),
                  [f"Btok{s}", f"xw{s}"], [f"ps{b}"])
        P.add("dve", lambda e: e.tensor_tensor(out=Htmp.rearrange("p (h q) -> p h q", q=64), in0=Hst.rearrange("p (h q) -> p h q", q=64),
                                                in1=ew2[:, i, d, 16:32].unsqueeze(2).to_broadcast([128, 16, 64]), op=ALU.mult),
              [hkey, ek], ["Htmp"])
        for b in range(2):
            P.add("dve", lambda e, b=b: e.tensor_tensor(out=Hst[:, b * 512:(b + 1) * 512], in0=Htmp[:, b * 512:(b + 1) * 512],
                                                        in1=ps[b][:, :], op=ALU.add), ["Htmp", f"ps{b}"], [hkey])

    def own_ssd(g):
        set_entry(E_OWNF)
        own_front(g)
        own_dt_a(0)
        own_dt_a(1)
        own_proj_z()
        own_dt_b(0)
        own_dt_b(1)
        proj_feat(34 + g, 512, [(c, 1024 + 128 * c) for c in range(10)])
        conv(512, 10)
        for i in range(4):
            own_tile_dir(i, 0, Hf_fin, "Hf_fin", True)
        P.add("act", lambda e: e.activation(out=Hb, in_=Hsnap[:, g, :], func=AF.Copy), [f"Hsnap{g}"], ["Hb"])
        for i in (3, 2, 1, 0):
            own_tile_dir(i, 1, Hb, "Hb", False)
        for c in range(8):
            P.add("dve", lambda e, c=c: e.scalar_tensor_tensor(out=yT[:, c, :], in0=XC[:, c, 0:512], scalar=vecF[:, V_DSK + c:V_DSK + c + 1],
                                                              in1=yT[:, c, :], op0=ALU.mult, op1=ALU.add), ["XC", "vecF", "yT"], ["yT"])

    out_v = out_d.rearrange("(kc p) t -> p kc t", p=128)
    actT = RB[:, 0:22 * 512].rearrange("p (k t) -> p k t", k=22)
    ypT = XC[:, 0:8, :]
    sgt = F1[:, 4128:4640]

    def ws_view(s, k, n):
        return ws[s][:].rearrange("p a b -> p (a b)")[:, 0:k * n].rearrange("p (k n) -> p k n", k=k)

    def stream(src_ap, k, n):
        s = ws_ctr[0] % 2
        ws_ctr[0] += 1
        v = ws_view(s, k, n)
        if k > 8:
            h_ = k // 2
            dma("pool", v[:, 0:h_, :], src_ap[:, 0:h_, :], [], [f"ws{s}"])
            dma("pool", v[:, h_:k, :], src_ap[:, h_:k, :], [], [f"ws{s}"])
        else:
            dma("pool", v, src_ap, [], [f"ws{s}"])
        return v, f"ws{s}"

    def stream_packed(src_rows, k, n):
        s_ = ws_ctr[0] % 2
        ws_ctr[0] += 1
        flat = ws[s_][:].rearrange("p a b -> p (a b)")[:, 0:k * n]
        dma("pool", flat, src_rows, [], [f"ws{s_}"])
        return flat.rearrange("p (k n) -> p k n", k=k), f"ws{s_}"

    def ln_feat(width):
        one = consts[:, C_ONE:C_ONE + 128]
        parts = [(0, 512, ps[6], ps[7], "ps6", "ps7")]
        if width > 512:
            parts.append((512, width, ps[4], ps[5], "ps4", "ps5"))
        for c0, c1, pa, pb, ka, kb in parts:
            n = c1 - c0
            for kc in range(8):
                P.add("pe", lambda e, kc=kc, pa=pa, c0=c0, c1=c1, n=n: e.matmul(pa[:, 0:n], lhsT=one, rhs=xres[:, kc, c0:c1], start=(kc == 0), stop=(kc == 7)),
                      ["xres", "consts"], [ka])
        for kc in range(8):
            P.add("act", lambda e, kc=kc: e.activation(out=sqt[:, 0:width], in_=xres[:, kc, 0:width], func=AF.Square), ["xres"], ["sqt"])
            for c0, c1, pa, pb, ka, kb in parts:
                n = c1 - c0
                P.add("pe", lambda e, kc=kc, pb=pb, c0=c0, c1=c1, n=n: e.matmul(pb[:, 0:n], lhsT=one, rhs=sqt[:, c0:c1], start=(kc == 0), stop=(kc == 7),
                                                                               skip_group_check=True), ["sqt", "consts"], [kb])
        ln_rows([(pa[:, 0:c1 - c0], pb[:, 0:c1 - c0], c0, c1, [ka, kb]) for c0, c1, pa, pb, ka, kb in parts])
        for kc in range(8):
            P.add("dve", lambda e, kc=kc: e.tensor_tensor(out=xres[:, kc, 0:width], in0=xres[:, kc, 0:width], in1=rowm[:, 0:width], op=ALU.subtract),
                  ["xres", "rowm"], ["xres"])
            P.add("dve", lambda e, kc=kc: e.tensor_tensor(out=xres[:, kc, 0:width], in0=xres[:, kc, 0:width], in1=rowr[:, 0:width], op=ALU.mult),
                  ["xres", "rowr"], ["xres"])

    def affine_feat(dst, gcol, bcol, width, keys_w):
        for kc in range(8):
            if kc % 2 == 0:
                P.add("act", lambda e, kc=kc: e.activation(out=dst[:, kc, 0:width], in_=xres[:, kc, 0:width], func=AF.Identity,
                                                           scale=gcol[:, kc:kc + 1], bias=bcol[:, kc:kc + 1]), ["xres", "AB", "vecF"], keys_w)
            else:
                P.add("dve", lambda e, kc=kc: e.tensor_scalar(out=dst[:, kc, 0:width], in0=xres[:, kc, 0:width], scalar1=gcol[:, kc:kc + 1],
                                                              scalar2=bcol[:, kc:kc + 1], op0=ALU.mult, op1=ALU.add), ["xres", "AB", "vecF"], keys_w)

    def own_tail(g):
        tsl = slice(512 * g, 512 * (g + 1))
        P.add("dve", lambda e: e.tensor_tensor(out=yT, in0=yT, in1=zT, op=ALU.mult), ["yT", "zT"], ["yT"])
        one = consts[:, C_ONE:C_ONE + 128]
        for c in range(8):
            P.add("act", lambda e, c=c: e.activation(out=sqt[:, 0:512], in_=yT[:, c, :], func=AF.Square), ["yT"], ["sqt"])
            P.add("pe", lambda e, c=c: e.matmul(ps[7][:, :], lhsT=one, rhs=sqt[:, 0:512], start=(c == 0), stop=(c == 7), skip_group_check=True),
                  ["sqt", "consts"], ["ps7"])
        P.add("dve", lambda e: e.tensor_scalar(out=rowr[:, 0:512], in0=ps[7][:, :], scalar1=1.0 / D, scalar2=None, op0=ALU.mult), ["ps7"], ["rowr"])
        P.add("act", lambda e: e.activation(out=rowr[:, 0:512], in_=rowr[:, 0:512], func=AF.Sqrt, bias=epsc[:, 0:1]), ["rowr", "epsc"], ["rowr"])
        P.add("dve", lambda e: e.reciprocal(out=rowr[:, 0:512], in_=rowr[:, 0:512]), ["rowr"], ["rowr"])
        for c in range(8):
            P.add("dve", lambda e, c=c: e.scalar_tensor_tensor(out=ynT[:, c, :], in0=yT[:, c, :], scalar=vecF[:, 32 + c:33 + c], in1=rowr[:, 0:512],
                                                              op0=ALU.mult, op1=ALU.mult), ["yT", "vecF", "rowr"], ["ynT"])
        dma("pool", ypT, out_v[:, :, tsl], [f"outg{g}"], ["XC"])
        P.add("dve", lambda e: e.tensor_scalar(out=xres[:, :, 0:512], in0=xres[:, :, 0:512], scalar1=ALPHA, scalar2=None, op0=ALU.mult), ["xres"], ["xres"])
        for mp in range(4):
            wt, wk = stream_packed(w_out_d[mp * 128:(mp + 1) * 128, :], 16, 256)
            for mm in range(2):
                m = mp * 2 + mm
                pm = ps[2 + m % 2]
                pk = f"ps{2 + m % 2}"
                for k in range(16):
                    src = ynT[:, k, :] if k < 8 else ypT[:, k - 8, :]
                    sk = "ynT" if k < 8 else "XC"
                    P.add("pe", lambda e, pm=pm, wt=wt, k=k, mm=mm, src=src: e.matmul(pm[:, :], lhsT=wt[:, k, mm * 128:(mm + 1) * 128], rhs=src,
                                                                                     start=(k == 0), stop=(k == 15)), [wk, sk], [pk])
                P.add("dve", lambda e, pm=pm, m=m: e.scalar_tensor_tensor(out=xres[:, m, 0:512], in0=pm[:, :], scalar=modT[:, 16 + m, 0:1],
                                                                         in1=xres[:, m, 0:512], op0=ALU.mult, op1=ALU.add), [pk, "modT", "xres"], ["xres"])
        ln_feat(512)
        affine_feat(hT, AB[:, 4, :], AB[:, 5, :], 512, ["hT"])
        for kc in range(8):
            P.add("dve", lambda e, kc=kc: e.tensor_scalar(out=xres[:, kc, 0:512], in0=xres[:, kc, 0:512], scalar1=vecF[:, 58 + kc:59 + kc],
                                                          scalar2=vecF[:, 66 + kc:67 + kc], op0=ALU.mult, op1=ALU.add), ["xres", "vecF", "hT"], ["xres"])
        for bt in range(11):
            wv, wk = stream_packed(w_gu_d[bt * 128:(bt + 1) * 128, :], 8, 512)
            for jj in range(2):
                j = bt * 2 + jj
                for kc in range(8):
                    P.add("pe", lambda e, wv=wv, kc=kc, jj=jj: e.matmul(ps[2][:, :], lhsT=wv[:, kc, jj * 128:(jj + 1) * 128], rhs=hT[:, kc, 0:512],
                                                                       start=(kc == 0), stop=(kc == 7)), [wk, "hT"], ["ps2"])
                for kc in range(8):
                    P.add("pe", lambda e, wv=wv, kc=kc, jj=jj: e.matmul(ps[3][:, :], lhsT=wv[:, kc, 256 + jj * 128:256 + (jj + 1) * 128], rhs=hT[:, kc, 0:512],
                                                                       start=(kc == 0), stop=(kc == 7)), [wk, "hT"], ["ps3"])
                P.add("act", lambda e: e.activation(out=sgt, in_=ps[2][:, :], func=AF.Silu), ["ps2"], ["sgt"])
                P.add("dve", lambda e, j=j: e.tensor_tensor(out=actT[:, j, :], in0=sgt, in1=ps[3][:, :], op=ALU.mult), ["sgt", "ps3"], ["actT", "yT", "zT"])
        P.add("dve", lambda e: e.tensor_scalar(out=xres[:, :, 0:512], in0=xres[:, :, 0:512], scalar1=ALPHA, scalar2=None, op0=ALU.mult), ["xres"], ["xres"])
        for m in range(8):
            wt, wk = stream_packed(w_down_d[m * 128:(m + 1) * 128, :], 22, 128)
            pm = ps[2 + m % 2]
            pk = f"ps{2 + m % 2}"
            for j in range(22):
                P.add("pe", lambda e, pm=pm, wt=wt, j=j: e.matmul(pm[:, :], lhsT=wt[:, j, :], rhs=actT[:, j, :], start=(j == 0), stop=(j == 21)),
                      [wk, "actT"], [pk])
            P.add("dve", lambda e, pm=pm, m=m: e.scalar_tensor_tensor(out=xres[:, m, 0:512], in0=pm[:, :], scalar=modT[:, 40 + m, 0:1],
                                                                     in1=xres[:, m, 0:512], op0=ALU.mult, op1=ALU.add), [pk, "modT", "xres"], ["xres"])
        ln_feat(512)
        for kc in range(8):
            P.add("dve", lambda e, kc=kc: e.tensor_scalar(out=xres[:, kc, 0:512], in0=xres[:, kc, 0:512], scalar1=vecF[:, V_LN2G + kc:V_LN2G + kc + 1],
                                                          scalar2=vecF[:, V_LN2B + kc:V_LN2B + kc + 1], op0=ALU.mult, op1=ALU.add), ["xres", "vecF"], ["xres"])
        dma("sp", out_v[:, :, tsl], xres[:, :, 0:512], ["xres"], [f"outg{g}"])

    for g in range(N_OWN_GROUPS):
        own_ssd(g)
        if RUN_TAIL:
            own_tail(g)
        if DEBUG and DBG_GROUP == 100 + g:
            tap(0, yT[:, 0, :], ["yT"])
            tap(512, yT[:, 7, :], ["yT"])
            tap(1024, zT[:, 0, :], ["zT"])
            tap(1536, xres[:, 0, 0:512], ["xres"])

    if DEBUG:
        if DBG_GROUP is None:
            if DBG_SNAP2 is None:
                tap(0, Hf_fin, ["Hf_fin"])
            else:
                tap(0, Hsnap[:, DBG_SNAP2, :], [f"Hsnap{DBG_SNAP2}"])
            tap(1024, Hsnap[:, DBG_SNAP, :], [f"Hsnap{DBG_SNAP}"])
        dma("sp", dbg_d[:, :], dbg[:], ["dbg"], ["dbg_d"])

    P.emit(nc, stack)
    stack.close()
    return nc


NFOR_BLOCKS = 7
RUN_B7 = True
FLAT_P1 = True
DBG_SNAP = 3
DBG_SNAP2 = None
N_OWN_GROUPS = 4
RUN_TAIL = True
RUN_POOL = True
POOL_GROUPS = 4
INJECT = False
DBG_GROUP = None
NVEC = 98
V_LN2G, V_LN2B, V_DSK = 74, 82, 90
_NC_CACHE = {}


def kernel(**inputs):
    maps = prep_inputs(inputs)
    if "nc" not in _NC_CACHE:
        _NC_CACHE["nc"] = build_program()
    nc = _NC_CACHE["nc"]
    res = run_bass_kernel_spmd(nc, maps, core_ids=list(range(NCORE)))
    out = np.concatenate([np.asarray(r["out"], np.float32).T for r in res.results], axis=0)
    kernel.last_results = res.results
    return out.reshape(1, NCORE * TOK, D)
```

```python
import numpy as np
from contextlib import ExitStack
import concourse.bass as bass
import concourse.mybir as mybir
from concourse.bass_utils import run_bass_kernel_spmd

F32 = mybir.dt.float32
BF16 = mybir.dt.bfloat16
AF = mybir.ActivationFunctionType
ALU = mybir.AluOpType

D = 1024
NCORE = 8
TOK = 2048
GRID_W = 64
NIP = 3360
EPS = 1e-5
ALPHA = 2.0 ** 0.25
E_CTXF, E_CTXB, E_FOR0, E_B7, E_OWNF, E_OWNB, NE = 0, 1, 2, 9, 10, 11, 12
T_CTXF, T_CTXB, T_BLK0 = 0, 2, 4
T_OWN = T_BLK0 + 8 * 16
T_PHA = T_OWN + 16
T_PHB = T_PHA + 4
T_HALO = T_PHB + 4
NT = T_HALO
NBROW = NIP + NE * 16
DEBUG = False


class Prog:
    def __init__(self):
        self.ops = []
        self.lastw = {}
        self.readers = {}
        self.fence_deps = set()
        self.last_on = {}

    def fence(self):
        self.fence_deps = set(self.last_on.values())

    def add(self, eng, fn, r=(), w=(), dma=False):
        idx = len(self.ops)
        deps = set()
        r = [k[:3] if k.startswith("ps") else k for k in r]
        w = [k[:3] if k.startswith("ps") else k for k in w]
        w = list(w) + [k for k in r if k.startswith("ps")]
        r = [k for k in r if not k.startswith("ps")]
        for k in r:
            if k in self.lastw:
                deps.add(self.lastw[k])
        for k in w:
            if k in self.lastw:
                deps.add(self.lastw[k])
            deps.update(self.readers.get(k, ()))
        for k in r:
            self.readers.setdefault(k, []).append(idx)
        for k in w:
            self.lastw[k] = idx
            self.readers[k] = []
        deps.update(self.fence_deps)
        deps.discard(idx)
        self.last_on[eng] = idx
        self.ops.append(dict(eng=eng, fn=fn, deps=deps, dma=dma, sig=False))
        return idx

    def emit(self, nc, stack, n_dma_sems=12):
        ops = self.ops
        for o in ops:
            keep = set()
            for d in o["deps"]:
                od = ops[d]
                if od["eng"] == "pe" and o["eng"] == "pe" and not od["dma"] and not o["dma"]:
                    continue
                keep.add(d)
                od["sig"] = True
            o["deps"] = keep
        engs = ["pe", "act", "dve", "pool", "sp"]
        csem = {e: stack.enter_context(nc.semaphore(f"c_{e}")) for e in engs}
        dq = sorted({o["eng"] for o in ops if o["dma"]})
        dsem = {q: [stack.enter_context(nc.semaphore(f"d_{q}_{i}")) for i in range(n_dma_sems)] for q in dq}
        ccount = {e: 0 for e in engs}
        dcount = {q: [0] * n_dma_sems for q in dq}
        dlast = {q: [None] * n_dma_sems for q in dq}
        rr = {q: 0 for q in dq}
        for i, o in enumerate(ops):
            o["pre"] = None
            if o["dma"]:
                q = o["eng"]
                s = rr[q] % n_dma_sems
                rr[q] += 1
                if dlast[q][s] is not None:
                    o["pre"] = (("d", q, s), dcount[q][s])
                dcount[q][s] += 16
                dlast[q][s] = i
                o["sem"] = ("d", q, s)
                o["val"] = dcount[q][s]
            elif o["sig"]:
                ccount[o["eng"]] += 1
                o["sem"] = ("c", o["eng"])
                o["val"] = ccount[o["eng"]]
        self.final_dma = [(("d", q, s), dcount[q][s]) for q in dq for s in range(n_dma_sems) if dcount[q][s]]
        self.stats = dict(n_ops=len(ops), signals=dict(ccount), dmas={q: rr[q] for q in dq})

        def semobj(key):
            return dsem[key[1]][key[2]] if key[0] == "d" else csem[key[1]]

        block = stack.enter_context(nc.Block())
        handles = {"pe": block.tensor, "act": block.scalar, "dve": block.vector,
                   "pool": block.gpsimd, "sp": block.sync}
        for e in engs:
            mine = [o for o in ops if o["eng"] == e]

            def body(engine, mine=mine, e=e):
                waited = {}
                for o in mine:
                    need = {}
                    if o["pre"] is not None:
                        need[o["pre"][0]] = o["pre"][1]
                    for d in o["deps"]:
                        od = ops[d]
                        k = od["sem"]
                        need[k] = max(need.get(k, 0), od["val"])
                    for k, v in need.items():
                        if waited.get(k, 0) < v:
                            engine.wait_ge(semobj(k), v)
                            waited[k] = v
                    ins = o["fn"](engine)
                    if o["dma"]:
                        ins.then_inc(semobj(o["sem"]), 16)
                    elif o["sig"]:
                        ins.then_inc(semobj(o["sem"]), 1)
                if e == "sp":
                    for k, v in self.final_dma:
                        if waited.get(k, 0) < v:
                            engine.wait_ge(semobj(k), v)

            handles[e](body)


def _fm(v, n):
    return np.ascontiguousarray(np.asarray(v, np.float32).reshape(n, 128).T)


C_I, C_TGT, C_TLT, C_TLE, C_TGE, C_ONE, NCONST = 0, 128, 256, 384, 512, 640, 768


def _consts():
    t = np.arange(128)
    I = np.eye(128, dtype=np.float32)
    Tgt = (t[:, None] > t[None, :]).astype(np.float32)
    Tlt = (t[:, None] < t[None, :]).astype(np.float32)
    Tle = (t[:, None] <= t[None, :]).astype(np.float32)
    Tge = (t[:, None] >= t[None, :]).astype(np.float32)
    ones = np.ones((128, 128), np.float32)
    return np.concatenate([I, Tgt, Tlt, Tle, Tge, ones], axis=1)


def _pack_cols(w, bc):
    K, N = w.shape
    kc = K // 128
    return np.ascontiguousarray(w.reshape(kc, 128, N // bc, bc).transpose(2, 1, 0, 3)).reshape((N // bc) * 128, kc * bc)


def _pack_gu(wg, wu):
    K, N = wg.shape
    kc, nb = K // 128, N // 256
    g = wg.reshape(kc, 128, nb, 256).transpose(2, 1, 0, 3)
    u = wu.reshape(kc, 128, nb, 256).transpose(2, 1, 0, 3)
    return np.ascontiguousarray(np.concatenate([g, u], axis=3)).reshape(nb * 128, kc * 512)


IPK_TILES = [(0, 512), (512, 512), (1024, 512), (1536, 512), (2048, 256), (2336, 512), (2848, 512)]


def _pack_ip(w):
    out = np.zeros((len(IPK_TILES), 128, 8 * 512), np.float32)
    for t, (c0, wd) in enumerate(IPK_TILES):
        blk = w[:, c0:c0 + wd].reshape(8, 128, wd).transpose(1, 0, 2)
        out[t, :, 0:8 * wd] = blk.reshape(128, 8 * wd)
    return out.reshape(len(IPK_TILES) * 128, 8 * 512)


def _pool_ops():
    ops = []
    for w in (2, 4, 8, 16):
        mats = []
        for n in (256, 64):
            pos = np.arange(n)
            lo = np.clip(pos - w // 2, 0, n)
            hi = np.clip(pos + (w - w // 2), 0, n)
            M = ((pos[None, :] >= lo[:, None]) & (pos[None, :] < hi[:, None])).astype(np.float64) / (hi - lo)[:, None]
            mats.append(M)
        ops.append(mats)
    return ops


_PT_LIST = [(0, -1), (0, 0)] + [(1, d_) for d_ in (-1, 0, 1)] + [(2, d_) for d_ in range(-2, 3)] + [(3, d_) for d_ in range(-4, 5)]


def _ptab(k, ops):
    tab = np.zeros((16, 128, 19, 128), np.float32)
    for T in range(16):
        Tg = 16 * k + T
        rt = np.array([2 * Tg, 2 * Tg + 1])
        for ix, (g, dl) in enumerate(_PT_LIST):
            Ts = Tg + dl
            if Ts < 0 or Ts >= 128:
                continue
            rs = np.array([2 * Ts, 2 * Ts + 1])
            Pr, Pc = ops[g]
            blk = np.kron(Pr[np.ix_(rt, rs)], Pc)
            if dl == 0:
                blk = blk - np.eye(128)
            tab[T, :, ix, :] = blk.T
    return tab.reshape(16 * 128, 19 * 128)


def _groups():
    gs = [(T_CTXF, 2, E_CTXF, True), (T_CTXB, 2, E_CTXB, True)]
    for b in range(8):
        for j in range(4):
            gs.append((T_BLK0 + 16 * b + 4 * j, 4, E_FOR0 + b, False))
    for j in range(4):
        gs.append((T_OWN + 4 * j, 4, E_OWNF, False))
    return gs


GROUPS = _groups()
NG = len(GROUPS)
R_DTB, R_ALG, R_SW, R_DSK, NROW = 0, NE * 16, 2 * NE * 16, 2 * NE * 16 + 8, 2 * NE * 16 + 8 + 16


def prep_inputs(inp):
    x = np.asarray(inp["x"], np.float32)[0]
    ctx = np.asarray(inp["ctx"], np.float32)[0]
    L = x.shape[0]
    zeros2 = np.zeros((2, D), np.float32)
    in_proj = np.asarray(inp["in_proj"], np.float32)[0]
    conv_w = np.asarray(inp["conv_w"], np.float32)[0]
    dt_bias = np.asarray(inp["dt_bias"], np.float32)[0]
    a_log = np.asarray(inp["a_log"], np.float32)[0]

    vecF = np.concatenate([
        _fm(inp["emb_ln_g"], 8), _fm(inp["emb_ln_b"], 8), _fm(np.asarray(inp["c"])[0], 8),
        _fm(inp["c_ctx"], 8), _fm(np.asarray(inp["ssd_norm_g"])[0], 8),
        _fm(np.asarray(inp["pool_scale"])[0], 8), _fm(np.asarray(inp["conv_b"])[0], 10),
        _fm(np.asarray(inp["ln1_g"])[0], 8), _fm(np.asarray(inp["ln1_b"])[0], 8),
        _fm(np.asarray(inp["ln2_g"])[0], 8), _fm(np.asarray(inp["ln2_b"])[0], 8),
        _fm(np.repeat(np.asarray(inp["d_skip"], np.float32)[0], 64), 8)], axis=1)
    shared = dict(
        vecF=vecF, consts=_consts(),
        w_ada=np.asarray(inp["w_ada"], np.float32)[0],
        b_ada=np.asarray(inp["b_ada"], np.float32),
        in_proj=in_proj,
        ipk=_pack_ip(in_proj),
        w_out=_pack_cols(np.asarray(inp["w_out"], np.float32)[0], 256),
        w_gu=_pack_gu(np.asarray(inp["w_gate"], np.float32)[0], np.asarray(inp["w_up"], np.float32)[0]),
        w_down=_pack_cols(np.asarray(inp["w_down"], np.float32)[0], 128),
        pool_w=np.asarray(inp["pool_w"], np.float32)[0].reshape(4 * 256, 256),
    )
    pops = _pool_ops()
    maps = []
    for k in range(NCORE):
        dirs = [0, 1] + [0 if b < k else 1 for b in range(7)] + [1, 0, 1]
        blocks = []

        def nat(s):
            lf = x[s - 2:s] if s >= 2 else zeros2
            rt = x[s + TOK:s + TOK + 2] if s + TOK + 2 <= L else zeros2
            return (x[s:s + TOK], lf, rt, float(s >= 2), float(s + TOK + 2 <= L))

        def flp(s):
            blk, lf, rt, fl, fr = nat(s)
            return (blk[::-1], rt[::-1], lf[::-1], fr, fl)

        blocks.append((ctx, zeros2, zeros2, 0.0, 0.0))
        blocks.append((ctx[::-1], zeros2, zeros2, 0.0, 0.0))
        for b in range(7):
            blocks.append(nat(TOK * b) if b < k else flp(TOK * (7 - (b - k))))
        blocks.append(flp(TOK * k))
        blocks.append(nat(TOK * k))
        tiles = [b_[0] for b_ in blocks]
        s0 = TOK * k
        tiles.append(x[s0 - 512:s0] if k > 0 else np.zeros((512, D), np.float32))
        tiles.append(x[s0 + TOK:s0 + TOK + 512] if k < 7 else np.zeros((512, D), np.float32))
        xs = np.concatenate(tiles, axis=0)
        assert xs.shape[0] == NT * 128, xs.shape
        xh = np.zeros((NG, 4, D), np.float32)
        gfl = np.ones((NG, 2), np.float32)
        gi = 0
        for (seq, lf, rt, fl, fr) in blocks:
            ext = np.concatenate([lf, seq, rt], axis=0)
            n = 256 if seq.shape[0] == 256 else 512
            ng = seq.shape[0] // n
            for j in range(ng):
                p0 = j * n
                xh[gi, 0:2] = ext[p0:p0 + 2]
                xh[gi, 2:4] = ext[p0 + n + 2:p0 + n + 4]
                if j == 0:
                    gfl[gi, 0] = fl
                if j == ng - 1:
                    gfl[gi, 1] = fr
                gi += 1
        assert gi == NG
        cw = np.stack([conv_w if d == 0 else conv_w[::-1] for d in dirs], 0)
        cwF = np.ascontiguousarray(cw.reshape(NE, 5, 10, 128).transpose(3, 0, 2, 1))
        wdt = np.stack([in_proj[:, 2304 + 16 * d:2304 + 16 * d + 16] for d in dirs], 0)
        dtb = np.concatenate([dt_bias[d] for d in dirs])
        alg = np.concatenate([a_log[d] for d in dirs])
        sw = np.zeros(8, np.float32)
        sw[k] = 1.0
        rowv = np.concatenate([dtb, alg, sw, np.asarray(inp["d_skip"], np.float32)[0]]).astype(np.float32)
        rowB = np.ascontiguousarray(np.broadcast_to(rowv[None, :], (128, rowv.size)))
        gflB = np.ascontiguousarray(np.broadcast_to(gfl.reshape(1, NG * 2), (128, NG * 2)))
        xo = x[TOK * k:TOK * (k + 1)]
        xT_own = np.zeros((4, D, 516), np.float32)
        for g_ in range(4):
            xT_own[g_, :, 0:512] = xo[512 * g_:512 * (g_ + 1)].T
            xT_own[g_, :, 512:516] = xh[34 + g_].T
        m = dict(shared)
        phf = np.zeros((128, 8), np.float32)
        phf[:, 0:4] = float(k > 0)
        phf[:, 4:8] = float(k < 7)
        m.update(xT_own=xT_own.reshape(4 * D, 516), inj=np.zeros((128, 5 * D), np.float32), ptab=_ptab(k, pops), phf=phf)
        wdtF = np.ascontiguousarray(wdt.reshape(NE, 8, 128, 16).transpose(2, 0, 1, 3)).reshape(128, NE * 128)
        m.update(xs=xs, xh=xh.reshape(NG * 4, D), cwF=cwF, wdt=wdtF, rowB=rowB, gfl=gflB)
        maps.append(m)
    return maps


def build_program():
    nc = bass.Bass("TRN2", target_bir_lowering=False)
    P = Prog()
    stack = ExitStack()

    def dram_in(name, shape):
        return nc.dram_tensor(name, list(shape), F32, kind="ExternalInput").ap()

    xs = dram_in("xs", [NT * 128, D])
    xh_d = dram_in("xh", [NG * 4, D])
    vecF_d = dram_in("vecF", [128, NVEC])
    consts_d = dram_in("consts", [128, NCONST])
    w_ada_d = dram_in("w_ada", [D, 6 * D])
    b_ada_d = dram_in("b_ada", [1, 6 * D])
    in_proj_d = dram_in("in_proj", [D, NIP])
    ipk_d = dram_in("ipk", [len(IPK_TILES) * 128, 8 * 512])
    cwF_d = dram_in("cwF", [128, NE, 10, 5])
    wdt_d = dram_in("wdt", [128, NE * 128])
    rowB_d = dram_in("rowB", [128, NROW])
    gfl_d = dram_in("gfl", [128, NG * 2])
    xT_own_d = dram_in("xT_own", [4 * D, 516])
    inj_d = dram_in("inj", [128, 5 * D])
    w_out_d = dram_in("w_out", [4 * 128, 16 * 256])
    w_gu_d = dram_in("w_gu", [11 * 128, 8 * 512])
    w_down_d = dram_in("w_down", [8 * 128, 22 * 128])
    pool_w_d = dram_in("pool_w", [4 * 256, 256])
    ptab_d = dram_in("ptab", [16 * 128, 19 * 128])
    phf_d = dram_in("phf", [128, 8])
    out_d = nc.dram_tensor("out", [D, TOK], F32, kind="ExternalOutput").ap()
    dbg_d = nc.dram_tensor("dbg", [128, 2048], F32, kind="ExternalOutput").ap() if DEBUG else None

    def sb(name, shape, dt=F32):
        return stack.enter_context(nc.sbuf_tensor(name, list(shape), dt))

    ps = [stack.enter_context(nc.psum_tensor(f"ps{i}", [128, 512], F32)) for i in range(8)]

    def dma(eng, out, in_, r, w, **kw):
        P.add(eng, lambda e, out=out, in_=in_, kw=kw: e.dma_start(out=out, in_=in_, **kw), r, w, dma=True)

    vecF = sb("vecF_s", [128, NVEC])
    consts = sb("consts_s", [128, NCONST])
    cbf = sb("consts_bf", [128, NCONST], BF16)
    cwF = sb("cwF_s", [128, NE, 10, 5])
    rowB = sb("rowB_s", [128, NROW])
    gfl = sb("gfl_s", [128, NG, 2])
    cs2 = sb("cs2", [128, 8, 2])
    modT = sb("modT", [128, 48, 2])
    AB = sb("AB", [128, 6, 8])
    Wdt = sb("Wdt", [128, NE, 8, 16], BF16)
    aneg = sb("aneg", [128, NE * 16])
    F1 = sb("F1", [128, 6144])
    F2 = sb("F2", [128, 8192])
    modrow = F1[0:2, :]
    wst = [F2[:, 0:2048].rearrange("p (k n) -> p k n", k=8), F2[:, 2048:4096].rearrange("p (k n) -> p k n", k=8)]
    wdt_f = F2[:, 4096:4096 + NE * 128].rearrange("p (a b c) -> p a b c", a=NE, b=8)
    badat = [sb(f"bada{i}", [1, 256]) for i in range(2)]

    dma("sp", vecF[:], vecF_d[:, :], [], ["vecF"])
    dma("sp", consts[:], consts_d[:, :], [], ["consts"])
    dma("sp", cwF[:], cwF_d[:, :, :, :], [], ["cwF"])
    dma("sp", rowB[:], rowB_d[:, :], [], ["rowB"])
    dma("sp", gfl[:].rearrange("p g t -> p (g t)"), gfl_d[:, :], [], ["gfl"])
    dma("sp", F2[:, 4096:4096 + NE * 128], wdt_d[:, :], [], ["wdt_f"])
    P.add("dve", lambda e: e.tensor_copy(out=cbf[:], in_=consts[:]), ["consts"], ["cbf"])
    P.add("dve", lambda e: e.tensor_copy(out=Wdt[:].rearrange("p a b c -> p (a b c)"), in_=F2[:, 4096:4096 + NE * 128]),
          ["wdt_f"], ["Wdt"])
    P.add("act", lambda e: e.activation(out=cs2[:, :, 0], in_=vecF[:, 16:24], func=AF.Silu), ["vecF"], ["cs2a"])
    P.add("act", lambda e: e.activation(out=cs2[:, :, 1], in_=vecF[:, 24:32], func=AF.Silu), ["vecF"], ["cs2b"])
    P.add("act", lambda e: e.activation(out=aneg[:], in_=rowB[:, R_ALG:R_ALG + NE * 16], func=AF.Exp), ["rowB"], ["aneg"])
    P.add("dve", lambda e: e.tensor_scalar(out=aneg[:], in0=aneg[:], scalar1=-1.0, scalar2=None, op0=ALU.mult), ["aneg"], ["aneg"])

    w_ada_v = w_ada_d.rearrange("(kc p) n -> p kc n", p=128)
    for g in range(24):
        st = wst[g % 2]
        key = f"wst{g % 2}"
        bt = badat[g % 2]
        bk = f"bada{g % 2}"
        dma("sp", st, w_ada_v[:, :, g * 256:(g + 1) * 256], [], [key])
        dma("sp", bt[:], b_ada_d[:, g * 256:(g + 1) * 256], [], [bk])
        pt = ps[g % 2]
        pk = f"ps{g % 2}"
        for kc in range(8):
            P.add("pe", lambda e, pt=pt, st=st, kc=kc: e.matmul(pt[0:2, 0:256], lhsT=cs2[:, kc, :], rhs=st[:, kc, :],
                                                               start=(kc == 0), stop=False),
                  ["cs2a", "cs2b", key], [pk])
        P.add("pe", lambda e, pt=pt, bt=bt: e.matmul(pt[0:2, 0:256], lhsT=consts[0:1, C_ONE:C_ONE + 2], rhs=bt[0:1, :],
                                                     start=False, stop=True), ["consts", bk], [pk])
        P.add("act", lambda e, pt=pt, g=g: e.activation(out=modrow[:, g * 256:(g + 1) * 256], in_=pt[0:2, 0:256], func=AF.Copy),
              [pk], ["modrow"])
    for blk in range(48):
        P.add("pe", lambda e, blk=blk: e.matmul(ps[2][:, blk * 2:blk * 2 + 2], lhsT=modrow[:, blk * 128:(blk + 1) * 128],
                                                rhs=consts[0:2, C_I:C_I + 2], start=True, stop=True),
              ["modrow", "consts"], ["ps2"])
    P.add("dve", lambda e: e.tensor_copy(out=modT[:].rearrange("p a b -> p (a b)"), in_=ps[2][:, 0:96]), ["ps2"], ["modT"])

    def affine(outA, outB, sc, sh, g, b):
        P.add("dve", lambda e: e.scalar_tensor_tensor(out=outA, in0=sc, scalar=1.0, in1=g, op0=ALU.add, op1=ALU.mult),
              ["modT", "vecF"], ["AB"])
        P.add("dve", lambda e: e.scalar_tensor_tensor(out=outB, in0=sc, scalar=1.0, in1=b, op0=ALU.add, op1=ALU.mult),
              ["modT", "vecF", "AB"], ["AB"])
        P.add("dve", lambda e: e.tensor_tensor(out=outB, in0=outB, in1=sh, op=ALU.add), ["AB", "modT"], ["AB"])

    affine(AB[:, 0, :], AB[:, 1, :], modT[:, 8:16, 0], modT[:, 0:8, 0], vecF[:, 0:8], vecF[:, 8:16])
    affine(AB[:, 2, :], AB[:, 3, :], modT[:, 8:16, 1], modT[:, 0:8, 1], vecF[:, 0:8], vecF[:, 8:16])
    affine(AB[:, 4, :], AB[:, 5, :], modT[:, 32:40, 0], modT[:, 24:32, 0], vecF[:, 58:66], vecF[:, 66:74])
    epsc = sb("epsc", [128, 1])
    P.add("pool", lambda e: e.memset(epsc[:], EPS), [], ["epsc"])
    P.fence()

    dbg = sb("dbg_s", [128, 2048]) if DEBUG else None
    if DEBUG:
        P.add("pool", lambda e: e.memset(dbg[:], 0.0), [], ["dbg"])

    def tap(col0, src, keys, npart=128):
        n = src.shape[-1]
        P.add("dve", lambda e: e.tensor_copy(out=dbg[0:npart, col0:col0 + n], in_=src), list(keys) + ["dbg"], ["dbg"])

    xt = [F1[:, 0:1024], F1[:, 1024:2048]]
    Hrun, Hf_fin, hf_ctx, hb_ctx, Htmp = (F2[:, i * 1024:(i + 1) * 1024] for i in range(5))
    xh4t = F1[0:4, 2048:3072]
    xnb = [sb(f"xnb{i}", [128, D], BF16) for i in range(2)]
    xnb4 = sb("xnb4", [4, D], BF16)
    st6 = [sb(f"st6_{i}", [128, 2, 6]) for i in range(2)]
    mv = [sb(f"mv{i}", [128, 4]) for i in range(2)]
    st6h = sb("st6h", [4, 2, 6])
    mvh = sb("mvh", [4, 4])
    hT = sb("hT", [128, 8, 516], BF16)
    ws = [sb(f"ws{i}", [128, 8, 512], BF16) for i in range(2)]
    U = sb("U", [128, 10, 516], BF16)
    XC = sb("XC", [128, 10, 512], BF16)
    Dsl = sb("Dsl", [128, 2, 5, 128], BF16)
    dtt_a = sb("dtt", [128, 2, 4, 6, 16])
    ew_a = sb("ew", [128, 2, 4, 32])
    par = [0]
    xw = [sb(f"xw{i}", [128, D], BF16) for i in range(2)]
    Btok = [sb(f"Btok{i}", [128, 128], BF16) for i in range(2)]
    Hsnap = sb("Hsnap", [128, 4, D], BF16)
    in_proj_v = in_proj_d.rearrange("(kc p) n -> p kc n", p=128)
    ws_ctr = [0]
    cur_entry = [None]

    def ln_tile(src_ap, xt_ap, xk, st_t, mv_t, sk, out_bf, ok, npart):
        dma("sp", xt_ap, src_ap, [], [xk])
        P.add("dve", lambda e: e.bn_stats(out=st_t[0:npart, 0, :], in_=xt_ap[:, 0:512]), [xk], [sk])
        P.add("dve", lambda e: e.bn_stats(out=st_t[0:npart, 1, :], in_=xt_ap[:, 512:1024]), [xk, sk], [sk])
        P.add("dve", lambda e: e.bn_aggr(out=mv_t[0:npart, 0:2], in_=st_t[0:npart, :, :].rearrange("p a b -> p (a b)")), [sk], [sk])
        P.add("act", lambda e: e.activation(out=mv_t[0:npart, 2:3], in_=mv_t[0:npart, 1:2], func=AF.Ln, bias=epsc[0:npart, 0:1]), [sk, "epsc"], [sk])
        P.add("act", lambda e: e.activation(out=mv_t[0:npart, 2:3], in_=mv_t[0:npart, 2:3], func=AF.Exp, scale=-0.5), [sk], [sk])
        P.add("dve", lambda e: e.tensor_scalar(out=mv_t[0:npart, 3:4], in0=mv_t[0:npart, 0:1], scalar1=mv_t[0:npart, 2:3], scalar2=-1.0,
                                               op0=ALU.mult, op1=ALU.mult), [sk], [sk])
        P.add("dve", lambda e: e.tensor_scalar(out=out_bf, in0=xt_ap, scalar1=mv_t[0:npart, 2:3], scalar2=mv_t[0:npart, 3:4], op0=ALU.mult, op1=ALU.add),
              [xk, sk], [ok])

    def evac_affine(i_op, out_ap, in_ap, a_idx, kc, r, w):
        if i_op % 2 == 0:
            P.add("act", lambda e: e.activation(out=out_ap, in_=in_ap, func=AF.Identity, scale=AB[:, a_idx, kc:kc + 1],
                                                bias=AB[:, a_idx + 1, kc:kc + 1]), r + ["AB"], w)
        else:
            P.add("dve", lambda e: e.tensor_scalar(out=out_ap, in0=in_ap, scalar1=AB[:, a_idx, kc:kc + 1],
                                                   scalar2=AB[:, a_idx + 1, kc:kc + 1], op0=ALU.mult, op1=ALU.add), r + ["AB"], w)

    def stream_w(c0, w):
        s = ws_ctr[0] % 2
        ws_ctr[0] += 1
        if (c0, w) in IPK_TILES:
            t_ = IPK_TILES.index((c0, w))
            flat = ws[s][:].rearrange("p a b -> p (a b)")[:, 0:8 * w]
            dma("pool", flat, ipk_d[t_ * 128:(t_ + 1) * 128, 0:8 * w], [], [f"ws{s}"])
            return flat.rearrange("p (k n) -> p k n", k=8), f"ws{s}"
        dma("pool", ws[s][:, :, 0:w], in_proj_v[:, :, c0:c0 + w], [], [f"ws{s}"])
        return ws[s], f"ws{s}"

    def set_entry(e_):
        cur_entry[0] = e_

    def front(gi):
        t0, n, e_, is_ctx = GROUPS[gi]
        N = 128 * n
        a_idx = 2 if is_ctx else 0
        for i in range(n):
            s = i % 2
            ln_tile(xs[(t0 + i) * 128:(t0 + i + 1) * 128, :], xt[s], f"xt{s}", st6[s], mv[s], f"mv{s}", xnb[s][:], f"xnb{s}", 128)
            for kc in range(8):
                P.add("pe", lambda e, s=s, kc=kc: e.matmul(ps[kc // 4][:, (kc % 4) * 128:(kc % 4 + 1) * 128],
                                                           lhsT=xnb[s][:, kc * 128:(kc + 1) * 128], rhs=cbf[:, C_I:C_I + 128],
                                                           start=True, stop=True), [f"xnb{s}", "cbf"], [f"ps{kc // 4}"])
            for kc in range(8):
                evac_affine(kc, hT[:, kc, i * 128:(i + 1) * 128], ps[kc // 4][:, (kc % 4) * 128:(kc % 4 + 1) * 128], a_idx, kc,
                            [f"ps{kc // 4}"], ["hT"])
        ln_tile(xh_d[gi * 4:gi * 4 + 4, :], xh4t, "xh4", st6h, mvh, "mvh", xnb4[:], "xnb4", 4)
        for kc in range(8):
            P.add("pe", lambda e, kc=kc: e.matmul(ps[4][:, 256 + kc * 4:256 + kc * 4 + 4], lhsT=xnb4[:, kc * 128:(kc + 1) * 128],
                                                  rhs=cbf[0:4, C_I:C_I + 4], start=True, stop=True), ["xnb4", "cbf"], ["ps4h"])
        for kc in range(8):
            evac_affine(kc, hT[:, kc, N:N + 4], ps[4][:, 256 + kc * 4:256 + kc * 4 + 4], a_idx, kc, ["ps4h"], ["hT"])
        return N

    def proj_feat(gi, N, chunks):
        i = 0
        while i < len(chunks):
            grp = chunks[i:i + 4]
            c0 = grp[0][1]
            wt, wk = stream_w(c0, 128 * len(grp))
            for j, (c, col) in enumerate(grp):
                assert col == c0 + 128 * j
                pu = ps[2 + c % 2]
                pk = f"ps{2 + c % 2}"
                for kc in range(8):
                    P.add("pe", lambda e, pu=pu, wt=wt, kc=kc, j=j: e.matmul(pu[:, 0:N], lhsT=wt[:, kc, j * 128:(j + 1) * 128],
                                                                            rhs=hT[:, kc, 0:N], start=(kc == 0), stop=(kc == 7)),
                          [wk, "hT"], [pk])
                for kc in range(8):
                    P.add("pe", lambda e, wt=wt, kc=kc, j=j, c=c: e.matmul(ps[4][:, c * 4:c * 4 + 4], lhsT=wt[:, kc, j * 128:(j + 1) * 128],
                                                                          rhs=hT[:, kc, N:N + 4], start=(kc == 0), stop=(kc == 7)),
                          [wk, "hT"], ["ps4u"])
                P.add("act", lambda e, pu=pu, c=c: e.activation(out=U[:, c, 2:N + 2], in_=pu[:, 0:N], func=AF.Copy), [pk], ["U"])
            i += 4
        nch = len(chunks)
        cs = [c for c, _ in chunks]
        assert cs == list(range(cs[0], cs[0] + nch))
        ph = ps[4][:, cs[0] * 4:(cs[0] + nch) * 4].rearrange("p (c f) -> p c f", f=4)
        P.add("dve", lambda e: e.tensor_scalar(out=U[:, cs[0]:cs[0] + nch, 0:2], in0=ph[:, :, 0:2], scalar1=gfl[:, gi, 0:1], scalar2=None,
                                               op0=ALU.mult), ["ps4u", "gfl"], ["U"])
        P.add("dve", lambda e: e.tensor_scalar(out=U[:, cs[0]:cs[0] + nch, N + 2:N + 4], in0=ph[:, :, 2:4], scalar1=gfl[:, gi, 1:2],
                                               scalar2=None, op0=ALU.mult), ["ps4u", "gfl"], ["U"])

    def dt_tile(e_, i, dcol, Wsrc=None, bias_ap=None, wkey="Wdt", bkey="rowB"):
        Wsrc = Wdt if Wsrc is None else Wsrc
        pp_ = par[0]
        dtt = dtt_a[:, pp_]
        ew = ew_a[:, pp_]
        bias_ap = rowB[:, R_DTB + e_ * 16:R_DTB + e_ * 16 + 16] if bias_ap is None else bias_ap
        pd = ps[4][:, 64 + dcol:64 + dcol + 16]
        for kc in range(8):
            P.add("pe", lambda e, kc=kc: e.matmul(pd, lhsT=hT[:, kc, i * 128:(i + 1) * 128], rhs=Wsrc[:, e_, kc, :],
                                                  start=(kc == 0), stop=(kc == 7)), ["hT", wkey], ["ps4d"])
        v, av, lv, dt_, dta, wd = (dtt[:, i, j, :] for j in range(6))
        k = f"dtt{pp_}_{i}"
        P.add("dve", lambda e: e.tensor_tensor(out=v, in0=pd, in1=bias_ap, op=ALU.add), ["ps4d", bkey], [k])
        P.add("act", lambda e: e.activation(out=av, in_=v, func=AF.Abs), [k], [k])
        P.add("act", lambda e: e.activation(out=av, in_=av, func=AF.Exp, scale=-1.0), [k], [k])
        P.add("act", lambda e: e.activation(out=lv, in_=av, func=AF.Ln, bias=1.0), [k], [k])
        P.add("dve", lambda e: e.scalar_tensor_tensor(out=dt_, in0=v, scalar=0.0, in1=lv, op0=ALU.max, op1=ALU.add), [k], [k])
        P.add("dve", lambda e: e.tensor_tensor(out=dta, in0=dt_, in1=aneg[:, e_ * 16:e_ * 16 + 16], op=ALU.mult), [k, "aneg"], [k])
        pw = ps[4][:, 192:224]
        P.add("pe", lambda e: e.matmul(pw[:, 0:16], lhsT=consts[:, C_TGT:C_TGT + 128], rhs=dta, start=True, stop=True), [k, "consts"], ["ps4w"])
        P.add("pe", lambda e: e.matmul(pw[:, 16:32], lhsT=consts[:, C_ONE:C_ONE + 128], rhs=dta, start=True, stop=True), [k, "consts"], ["ps4w"])
        P.add("act", lambda e: e.activation(out=ew[:, i, :], in_=pw, func=AF.Exp), ["ps4w"], [f"ew{pp_}_{i}"])
        P.add("dve", lambda e: e.tensor_tensor(out=wd, in0=ew[:, i, 0:16], in1=dt_, op=ALU.mult), [f"ew{pp_}_{i}", k], [k])

    def dt_group_a(e_, Wsrc, bias_ap, wkey, bkey):
        pp_ = par[0]
        dtt = dtt_a[:, pp_]
        pd4 = ps[4][:, 64:128].rearrange("p (t h) -> p t h", t=4)
        for i in range(4):
            for kc in range(8):
                P.add("pe", lambda e, kc=kc, i=i: e.matmul(ps[4][:, 64 + 16 * i:80 + 16 * i], lhsT=hT[:, kc, i * 128:(i + 1) * 128], rhs=Wsrc[:, e_, kc, :],
                                                           start=(kc == 0), stop=(kc == 7)), ["hT", wkey], ["ps4"])
        v, av, lv, dt_, dta, wd = (dtt[:, :, j, :] for j in range(6))
        ks = [f"dtt{pp_}_{i}" for i in range(4)]
        P.add("dve", lambda e: e.tensor_tensor(out=v, in0=pd4, in1=bias_ap.unsqueeze(1).to_broadcast([128, 4, 16]), op=ALU.add), ["ps4", bkey], ks)
        P.add("act", lambda e: e.activation(out=av, in_=v, func=AF.Abs), ks, ks)
        P.add("act", lambda e: e.activation(out=av, in_=av, func=AF.Exp, scale=-1.0), ks, ks)
        P.add("act", lambda e: e.activation(out=lv, in_=av, func=AF.Ln, bias=1.0), ks, ks)
        P.add("dve", lambda e: e.scalar_tensor_tensor(out=dt_, in0=v, scalar=0.0, in1=lv, op0=ALU.max, op1=ALU.add), ks, ks)
        P.add("dve", lambda e: e.tensor_tensor(out=dta, in0=dt_, in1=aneg[:, e_ * 16:e_ * 16 + 16].unsqueeze(1).to_broadcast([128, 4, 16]), op=ALU.mult),
              ks + ["aneg"], ks)

    def dt_group_b(pp_):
        dtt = dtt_a[:, pp_]
        ew = ew_a[:, pp_]
        v, av, lv, dt_, dta, wd = (dtt[:, :, j, :] for j in range(6))
        ks = [f"dtt{pp_}_{i}" for i in range(4)]
        es = [f"ew{pp_}_{i}" for i in range(4)]
        for i in range(4):
            P.add("pe", lambda e, i=i: e.matmul(ps[4][:, 128 + 32 * i:144 + 32 * i], lhsT=consts[:, C_TGT:C_TGT + 128], rhs=dtt[:, i, 4, :],
                                                start=True, stop=True), ks + ["consts"], ["ps4"])
            P.add("pe", lambda e, i=i: e.matmul(ps[4][:, 144 + 32 * i:160 + 32 * i], lhsT=consts[:, C_ONE:C_ONE + 128], rhs=dtt[:, i, 4, :],
                                                start=True, stop=True), ks + ["consts"], ["ps4"])
        P.add("act", lambda e: e.activation(out=ew[:, :, :], in_=ps[4][:, 128:256].rearrange("p (t h) -> p t h", t=4), func=AF.Exp), ["ps4"], es)
        P.add("dve", lambda e: e.tensor_tensor(out=wd, in0=ew[:, :, 0:16], in1=dt_, op=ALU.mult), es + ks, ks)

    def conv(N, nch):
        e_ = cur_entry[0]
        for c in range(nch):
            sl_ = c % 2
            for k in range(5):
                if k % 2 == 0:
                    P.add("dve", lambda e, c=c, k=k, sl_=sl_: e.tensor_scalar(out=Dsl[:, sl_, k, :], in0=consts[:, C_I:C_I + 128],
                                                                             scalar1=cwF[:, e_, c, k:k + 1], scalar2=None, op0=ALU.mult),
                          ["consts", "cwF"], [f"Dsl{sl_}_{k}"])
                else:
                    P.add("act", lambda e, c=c, k=k, sl_=sl_: e.activation(out=Dsl[:, sl_, k, :], in_=consts[:, C_I:C_I + 128], func=AF.Copy,
                                                                          scale=cwF[:, e_, c, k:k + 1]), ["consts", "cwF"], [f"Dsl{sl_}_{k}"])
            for k in range(5):
                P.add("pe", lambda e, c=c, k=k, sl_=sl_: e.matmul(ps[5][:, 0:N], lhsT=Dsl[:, sl_, k, :], rhs=U[:, c, k:k + N],
                                                                 start=(k == 0), stop=(k == 4)), [f"Dsl{sl_}_{k}", "U"], ["ps5"])
            P.add("act", lambda e, c=c: e.activation(out=XC[:, c, 0:N], in_=ps[5][:, 0:N], func=AF.Silu, bias=vecF[:, 48 + c:49 + c]),
                  ["ps5", "vecF"], ["XC"])

    def states_tile(i, pp_=None):
        pp_ = par[0] if pp_ is None else pp_
        dtt = dtt_a[:, pp_]
        ew = ew_a[:, pp_]
        s = i % 2
        for c in range(8):
            P.add("pe", lambda e, c=c: e.matmul(ps[c // 4][:, (c % 4) * 128:(c % 4 + 1) * 128], lhsT=XC[:, c, i * 128:(i + 1) * 128],
                                                rhs=cbf[:, C_I:C_I + 128], start=True, stop=True), ["XC", "cbf"], [f"ps{c // 4}"])
        P.add("pe", lambda e: e.matmul(ps[4][:, 384:512], lhsT=XC[:, 8, i * 128:(i + 1) * 128], rhs=cbf[:, C_I:C_I + 128],
                                       start=True, stop=True), ["XC", "cbf"], ["ps4b"])
        for b in range(2):
            P.add("dve", lambda e, b=b: e.tensor_tensor(out=xw[s][:, b * 512:(b + 1) * 512].rearrange("p (h q) -> p h q", q=64),
                                                        in0=ps[b][:, :].rearrange("p (h q) -> p h q", q=64),
                                                        in1=dtt[:, i, 5, b * 8:(b + 1) * 8].unsqueeze(2).to_broadcast([128, 8, 64]),
                                                        op=ALU.mult), [f"ps{b}", f"dtt{pp_}_{i}"], [f"xw{s}"])
        P.add("act", lambda e: e.activation(out=Btok[s][:], in_=ps[4][:, 384:512], func=AF.Copy), ["ps4b"], [f"Btok{s}"])
        for b in range(2):
            P.add("pe", lambda e, b=b: e.matmul(ps[6 + b][:, :], lhsT=Btok[s][:], rhs=xw[s][:, b * 512:(b + 1) * 512],
                                                start=True, stop=True), [f"Btok{s}", f"xw{s}"], [f"ps{6 + b}"])
        P.add("dve", lambda e: e.tensor_tensor(out=Htmp.rearrange("p (h q) -> p h q", q=64), in0=Hrun.rearrange("p (h q) -> p h q", q=64),
                                                in1=ew[:, i, 16:32].unsqueeze(2).to_broadcast([128, 16, 64]), op=ALU.mult),
              ["Hrun", f"ew{pp_}_{i}"], ["Htmp"])
        for b in range(2):
            P.add("dve", lambda e, b=b: e.tensor_tensor(out=Hrun[:, b * 512:(b + 1) * 512], in0=Htmp[:, b * 512:(b + 1) * 512],
                                                        in1=ps[6 + b][:, :], op=ALU.add), ["Htmp", f"ps{6 + b}"], ["Hrun"])

    def states_group(gi):
        t0, n, e_, is_ctx = GROUPS[gi]
        par[0] = 0
        dtt = dtt_a[:, 0]
        ew = ew_a[:, 0]
        set_entry(e_)
        N = front(gi)
        proj_feat(gi, N, [(c, 1024 + 128 * c) for c in range(9)])
        for i in range(n):
            dt_tile(e_, i, 0)
        conv(N, 9)
        if DEBUG and gi == DBG_GROUP:
            tap(0, hT[:, 0, 0:260], ["hT"])
            tap(260, U[:, 0, 0:260], ["U"])
            tap(520, XC[:, 0, 0:256], ["XC"])
            tap(776, XC[:, 8, 0:256], ["XC"])
            tap(1032, dtt[:, 0, :, :].rearrange("p a b -> p (a b)"), ["dtt0_0"])
            tap(1128, ew[:, 0, :], ["ew0_0"])
            tap(1160, AB[:].rearrange("p a b -> p (a b)"), ["AB"])
            tap(1208, mv[0][:, :], ["mv0"])
            tap(1212, modT[:].rearrange("p a b -> p (a b)"), ["modT"])
        for i in range(n):
            states_tile(i)
            if DEBUG and gi == DBG_GROUP and i == 0:
                tap(1308, xw[0][:, 0:512], ["xw0"])
                tap(1820, Btok[0][:, :], ["Btok0"])
                tap(1948, Hrun[:, 0:512], ["Hrun"])

    RB = sb("RB", [128, 12288], BF16)
    PT_LIST = [(0, -1), (0, 0)] + [(1, d_) for d_ in (-1, 0, 1)] + [(2, d_) for d_ in range(-2, 3)] + [(3, d_) for d_ in range(-4, 5)]

    def pool_phase():
        out_v = out_d.rearrange("(kc p) t -> p kc t", p=128)
        Wpl = sb("Wpl", [128, 8, 256], BF16)
        tabs = sb("ptabs", [128, 19, 128], BF16)
        phf = sb("phf_s", [128, 8])
        dma("sp", phf[:], phf_d[:, :], [], ["phf"])
        dma("pool", Wpl[:], pool_w_d.rearrange("(a p) o -> p a o", p=128), [], ["Wpl"])
        upw = RB.rearrange("p (s f) -> p s f", s=12)
        dT = XC[:, 0:8, :]
        stage = F1[:, 2048:6144].rearrange("p (k t) -> p k t", k=8)

        def up_tiles(taus):
            wts = []
            for h in range(2):
                wts.append(stream_w(2336 + 512 * h, 512))
            for tau in taus:
                tix = (T_PHA + tau) if tau < 4 else ((T_OWN + tau - 4) if tau < 20 else (T_PHB + tau - 20))
                s = tau % 2
                ln_tile(xs[tix * 128:(tix + 1) * 128, :], xt[s], f"xt{s}", st6[s], mv[s], f"mv{s}", xnb[s][:], f"xnb{s}", 128)
                for kc in range(8):
                    P.add("pe", lambda e, s=s, kc=kc: e.matmul(ps[kc // 4][:, (kc % 4) * 128:(kc % 4 + 1) * 128],
                                                               lhsT=xnb[s][:, kc * 128:(kc + 1) * 128], rhs=cbf[:, C_I:C_I + 128],
                                                               start=True, stop=True), [f"xnb{s}", "cbf"], [f"ps{kc // 4}"])
                for kc in range(8):
                    evac_affine(kc, hT[:, kc, 0:128], ps[kc // 4][:, (kc % 4) * 128:(kc % 4 + 1) * 128], 0, kc, [f"ps{kc // 4}"], ["hT"])
                for h in range(2):
                    wt, wk = wts[h]
                    for kc in range(8):
                        P.add("pe", lambda e, wt=wt, kc=kc, h=h: e.matmul(ps[2 + h][:, :], lhsT=hT[:, kc, 0:128], rhs=wt[:, kc, :],
                                                                         start=(kc == 0), stop=(kc == 7)), ["hT", wk], [f"ps{2 + h}"])
                    dst = upw[:, tau % 12, h * 512:(h + 1) * 512]
                    if tau < 4 or tau >= 20:
                        fc = tau if tau < 4 else tau - 16
                        P.add("act", lambda e, dst=dst, h=h, fc=fc: e.activation(out=dst, in_=ps[2 + h][:, :], func=AF.Copy, scale=phf[:, fc:fc + 1]),
                              [f"ps{2 + h}", "phf"], [f"upw{tau % 12}"])
                    else:
                        P.add("act", lambda e, dst=dst, h=h: e.activation(out=dst, in_=ps[2 + h][:, :], func=AF.Copy), [f"ps{2 + h}"], [f"upw{tau % 12}"])

        def pool_tile(T):
            dma("pool", tabs[:], ptab_d[T * 128:(T + 1) * 128, :].rearrange("p (a b) -> p a b", a=19), [], ["tabs"])
            for cc in range(8):
                g_ = cc // 2
                idxs = [(ix, dl) for ix, (gg, dl) in enumerate(PT_LIST) if gg == g_]
                for n_, (ix, dl) in enumerate(idxs):
                    sl_ = (T + 4 + dl) % 12
                    P.add("pe", lambda e, cc=cc, ix=ix, sl_=sl_, n_=n_, last=(n_ == len(idxs) - 1): e.matmul(
                        ps[6 + cc // 4][:, (cc % 4) * 128:(cc % 4 + 1) * 128], lhsT=upw[:, sl_, cc * 128:(cc + 1) * 128], rhs=tabs[:, ix, :],
                        start=(n_ == 0), stop=last, skip_group_check=True), [f"upw{sl_}", "tabs"], [f"ps{6 + cc // 4}"])
            tq = (T % 4) * 128
            P.add("act", lambda e: e.activation(out=dT[:, 0:4, tq:tq + 128], in_=ps[6][:, :].rearrange("p (c t) -> p c t", c=4), func=AF.Copy),
                  ["ps6"], ["XC"])
            P.add("dve", lambda e: e.tensor_copy(out=dT[:, 4:8, tq:tq + 128], in_=ps[7][:, :].rearrange("p (c t) -> p c t", c=4)), ["ps7"], ["XC"])

        def pool_group(gq):
            for T in range(4 * gq, 4 * gq + 4):
                pool_tile(T)
            for oc in range(8):
                g_ = oc // 2
                pp = ps[2 + oc % 2]
                for ic in range(2):
                    P.add("pe", lambda e, pp=pp, g_=g_, ic=ic, oc=oc: e.matmul(pp[:, :], lhsT=Wpl[:, g_ * 2 + ic, (oc % 2) * 128:(oc % 2 + 1) * 128],
                                                                              rhs=dT[:, 2 * g_ + ic, :], start=(ic == 0), stop=(ic == 1)),
                          ["Wpl", "XC"], [f"ps{2 + oc % 2}"])
                P.add("act", lambda e, pp=pp, oc=oc: e.activation(out=stage[:, oc, :], in_=pp[:, :], func=AF.Copy, scale=vecF[:, 40 + oc:41 + oc]),
                      [f"ps{2 + oc % 2}", "vecF"], ["stage"])
            dma("sp", out_v[:, :, 512 * gq:512 * (gq + 1)], stage, ["stage"], [f"outg{gq}"])

        up_tiles(range(0, 12))
        for gq in range(POOL_GROUPS):
            if gq > 0:
                up_tiles(range(8 + 4 * gq, 12 + 4 * gq))
            pool_group(gq)
        P.fence()

    if RUN_POOL:
        pool_phase()

    if INJECT:
        dma("sp", Hf_fin, inj_d[:, 0:D], [], ["Hf_fin"])
        for j_ in range(4):
            dma("pool", Hsnap[:, j_, :], inj_d[:, (1 + j_) * D:(2 + j_) * D], [], [f"Hsnap{j_}"])
    def boundary(b):
        swb = rowB[:, R_SW + b:R_SW + b + 1]
        P.add("dve", lambda e: e.scalar_tensor_tensor(out=Hf_fin, in0=Hrun, scalar=swb, in1=Hf_fin, op0=ALU.mult, op1=ALU.add),
              ["Hrun", "Hf_fin", "rowB"], ["Hf_fin"])
        P.add("pool", lambda e: e.tensor_tensor(out=Htmp, in0=hb_ctx, in1=Hrun, op=ALU.subtract), ["hb_ctx", "Hrun"], ["Htmp"])
        P.add("dve", lambda e: e.scalar_tensor_tensor(out=Hrun, in0=Htmp, scalar=swb, in1=Hrun, op0=ALU.mult, op1=ALU.add),
              ["Htmp", "Hrun", "rowB"], ["Hrun"])

    RBW = RB[:, 0:8 * 1152].rearrange("p (k n) -> p k n", k=8)
    DselR = F1[:, 3072:5952].bitcast(BF16).rearrange("p (c k n) -> p c k n", c=9, k=5)
    bproj = sb("bproj", [128, 9])
    bpfl = sb("bpfl", [128, 9, 2, 2])
    Ub = sb("Ub", [128, 9, 516], BF16)
    UH = sb("UH", [128, 9, 4], BF16)
    UH2 = sb("UH2", [128, 9, 4], BF16)
    dtb2 = sb("dtb2", [128, NE * 16])
    dsel_entry = [None]

    def prep_resident():
        P.fence()
        B1rep = F2[:, 6144:7168].rearrange("p (k m) -> p k m", k=8)
        P.add("dve", lambda e: e.tensor_copy(out=B1rep, in_=AB[:, 1, :].unsqueeze(2).to_broadcast([128, 8, 128])), ["AB"], ["B1rep"])
        for g in range(5):
            c0 = 1024 + 256 * g
            ncol = 256 if g < 4 else 128
            st = wst[g % 2]
            key = f"wst{g % 2}"
            dma("sp", st[:, :, 0:ncol], in_proj_v[:, :, c0:c0 + ncol], [], [key])
            for j in range(ncol // 128):
                c = 2 * g + j
                for kc in range(8):
                    P.add("pe", lambda e, st=st, kc=kc, j=j, c=c: e.matmul(ps[2][:, c:c + 1], lhsT=st[:, kc, j * 128:(j + 1) * 128],
                                                                          rhs=AB[:, 1, kc:kc + 1], start=(kc == 0), stop=(kc == 7)),
                          [key, "AB"], ["ps2"])
            for kc in range(8):
                if kc % 2 == 0:
                    P.add("act", lambda e, st=st, kc=kc, c0=c0, ncol=ncol: e.activation(out=RBW[:, kc, c0 - 1024:c0 - 1024 + ncol], in_=st[:, kc, 0:ncol],
                                                                                       func=AF.Copy, scale=AB[:, 0, kc:kc + 1]), [key, "AB"], ["RBW"])
                else:
                    P.add("dve", lambda e, st=st, kc=kc, c0=c0, ncol=ncol: e.tensor_scalar(out=RBW[:, kc, c0 - 1024:c0 - 1024 + ncol], in0=st[:, kc, 0:ncol],
                                                                                          scalar1=AB[:, 0, kc:kc + 1], scalar2=None, op0=ALU.mult),
                          [key, "AB"], ["RBW"])
        P.add("dve", lambda e: e.tensor_copy(out=bproj[:], in_=ps[2][:, 0:9]), ["ps2"], ["bproj"])
        for e_ in range(NE):
            for kc in range(8):
                P.add("pe", lambda e, e_=e_, kc=kc: e.matmul(ps[3][:, e_ * 16:(e_ + 1) * 16], lhsT=B1rep[:, kc, :], rhs=wdt_f[:, e_, kc, :],
                                                             start=(kc == 0), stop=(kc == 7)), ["B1rep", "wdt_f"], ["ps3"])
        P.add("dve", lambda e: e.tensor_tensor(out=dtb2[:], in0=ps[3][:, 0:NE * 16], in1=rowB[:, R_DTB:R_DTB + NE * 16], op=ALU.add),
              ["ps3", "rowB"], ["dtb2"])
        for e_ in range(E_FOR0, E_B7 + 1):
            P.add("dve", lambda e, e_=e_: e.tensor_tensor(out=Wdt[:, e_, :, :], in0=wdt_f[:, e_, :, :],
                                                          in1=AB[:, 0, :].unsqueeze(2).to_broadcast([128, 8, 16]), op=ALU.mult), ["wdt_f", "AB"], ["Wdt"])
        P.fence()

    def set_dsel(e_):
        if dsel_entry[0] == e_:
            return
        dsel_entry[0] = e_
        n_ = 0
        for c in range(9):
            for k in range(5):
                if n_ % 2 == 0:
                    P.add("act", lambda e, c=c, k=k: e.activation(out=DselR[:, c, k, :], in_=consts[:, C_I:C_I + 128], func=AF.Copy,
                                                                  scale=cwF[:, e_, c, k:k + 1]), ["consts", "cwF"], [f"DselR{c}_{k}"])
                else:
                    P.add("dve", lambda e, c=c, k=k: e.tensor_scalar(out=DselR[:, c, k, :], in0=consts[:, C_I:C_I + 128],
                                                                     scalar1=cwF[:, e_, c, k:k + 1], scalar2=None, op0=ALU.mult),
                          ["consts", "cwF"], [f"DselR{c}_{k}"])
                n_ += 1

    def front_ln(gi, i):
        t0, n, e_, is_ctx = GROUPS[gi]
        s = i % 2
        ln_tile(xs[(t0 + i) * 128:(t0 + i + 1) * 128, :], xt[s], f"xt{s}", st6[s], mv[s], f"mv{s}", xnb[s][:], f"xnb{s}", 128)

    def front_tr(gi, i):
        s = i % 2
        for kc in range(8):
            P.add("pe", lambda e, s=s, kc=kc: e.matmul(ps[kc // 4][:, (kc % 4) * 128:(kc % 4 + 1) * 128],
                                                       lhsT=xnb[s][:, kc * 128:(kc + 1) * 128], rhs=cbf[:, C_I:C_I + 128],
                                                       start=True, stop=True), [f"xnb{s}", "cbf"], [f"ps{kc // 4}"])
        P.add("act", lambda e, i=i: e.activation(out=hT[:, 0:4, i * 128:(i + 1) * 128], in_=ps[0][:, :].rearrange("p (k t) -> p k t", k=4),
                                                 func=AF.Copy), ["ps0"], ["hT"])
        P.add("dve", lambda e, i=i: e.tensor_copy(out=hT[:, 4:8, i * 128:(i + 1) * 128], in_=ps[1][:, :].rearrange("p (k t) -> p k t", k=4)),
              ["ps1"], ["hT"])

    UU = [U, Ub]

    def halo_block(gis, UHt=None, uhk="UH"):
        UHt = UH if UHt is None else UHt
        g0_, g3_ = gis[0], gis[-1]
        dma("sp", xh4t[0:2, :], xh_d[g0_ * 4:g0_ * 4 + 2, :], [], ["xh4"])
        dma("sp", xh4t[2:4, :], xh_d[g3_ * 4 + 2:g3_ * 4 + 4, :], [], ["xh4"])
        sk = "mvh"
        P.add("dve", lambda e: e.bn_stats(out=st6h[0:4, 0, :], in_=xh4t[:, 0:512]), ["xh4"], [sk])
        P.add("dve", lambda e: e.bn_stats(out=st6h[0:4, 1, :], in_=xh4t[:, 512:1024]), ["xh4", sk], [sk])
        P.add("dve", lambda e: e.bn_aggr(out=mvh[0:4, 0:2], in_=st6h[0:4, :, :].rearrange("p a b -> p (a b)")), [sk], [sk])
        P.add("act", lambda e: e.activation(out=mvh[0:4, 2:3], in_=mvh[0:4, 1:2], func=AF.Ln, bias=epsc[0:4, 0:1]), [sk, "epsc"], [sk])
        P.add("act", lambda e: e.activation(out=mvh[0:4, 2:3], in_=mvh[0:4, 2:3], func=AF.Exp, scale=-0.5), [sk], [sk])
        P.add("dve", lambda e: e.tensor_scalar(out=mvh[0:4, 3:4], in0=mvh[0:4, 0:1], scalar1=mvh[0:4, 2:3], scalar2=-1.0,
                                               op0=ALU.mult, op1=ALU.mult), [sk], [sk])
        P.add("dve", lambda e: e.tensor_scalar(out=xnb4[:], in0=xh4t, scalar1=mvh[0:4, 2:3], scalar2=mvh[0:4, 3:4], op0=ALU.mult, op1=ALU.add), ["xh4", sk], ["xnb4"])
        for kc in range(8):
            P.add("pe", lambda e, kc=kc: e.matmul(ps[4][:, 256 + kc * 4:256 + kc * 4 + 4], lhsT=xnb4[:, kc * 128:(kc + 1) * 128],
                                                  rhs=cbf[0:4, C_I:C_I + 4], start=True, stop=True), ["xnb4", "cbf"], ["ps4"])
        P.add("act", lambda e: e.activation(out=hT[:, :, 512:516], in_=ps[4][:, 256:288].rearrange("p (k t) -> p k t", k=8), func=AF.Copy),
              ["ps4"], ["hTh"])
        for c in range(9):
            for kc in range(8):
                P.add("pe", lambda e, kc=kc, c=c: e.matmul(ps[4][:, c * 4:c * 4 + 4], lhsT=RBW[:, kc, c * 128:(c + 1) * 128], rhs=hT[:, kc, 512:516],
                                                           start=(kc == 0), stop=(kc == 7)), ["RBW", "hTh"], ["ps4"])
        for sd, gq in ((0, g0_), (1, g3_)):
            P.add("dve", lambda e, sd=sd, gq=gq: e.tensor_scalar(out=bpfl[:, :, sd, :], in0=bproj[:].unsqueeze(2).to_broadcast([128, 9, 2]),
                                                                 scalar1=gfl[:, gq, sd:sd + 1], scalar2=None, op0=ALU.mult), ["bproj", "gfl"], ["bpfl"])
        ph = ps[4][:, 0:36].rearrange("p (c f) -> p c f", f=4)
        P.add("dve", lambda e: e.scalar_tensor_tensor(out=UHt[:, :, 0:2], in0=ph[:, :, 0:2], scalar=gfl[:, g0_, 0:1], in1=bpfl[:, :, 0, :],
                                                      op0=ALU.mult, op1=ALU.add), ["ps4", "gfl", "bpfl"], [uhk])
        P.add("dve", lambda e: e.scalar_tensor_tensor(out=UHt[:, :, 2:4], in0=ph[:, :, 2:4], scalar=gfl[:, g3_, 1:2], in1=bpfl[:, :, 1, :],
                                                      op0=ALU.mult, op1=ALU.add), ["ps4", "gfl", "bpfl"], [uhk])

    def projL(gi, ub):
        Ut = UU[ub]
        uk = "U" if ub == 0 else "Ub"
        for c in range(9):
            pu = ps[2 + c % 2]
            pk = f"ps{2 + c % 2}"
            for kc in range(8):
                P.add("pe", lambda e, pu=pu, kc=kc, c=c: e.matmul(pu[:, :], lhsT=RBW[:, kc, c * 128:(c + 1) * 128], rhs=hT[:, kc, 0:512],
                                                                 start=(kc == 0), stop=(kc == 7)), ["RBW", "hT"], [pk])
            if c % 2 == 0:
                P.add("act", lambda e, pu=pu, c=c: e.activation(out=Ut[:, c, 2:514], in_=pu[:, :], func=AF.Identity, bias=bproj[:, c:c + 1]),
                      [pk, "bproj"], [uk])
            else:
                P.add("dve", lambda e, pu=pu, c=c: e.tensor_scalar(out=Ut[:, c, 2:514], in0=pu[:, :], scalar1=bproj[:, c:c + 1], scalar2=None, op0=ALU.add),
                      [pk, "bproj"], [uk])

    def convL(ub, c0=0, c1=9):
        Ut = UU[ub]
        uk = "U" if ub == 0 else "Ub"
        for c in range(c0, c1):
            for k in range(5):
                P.add("pe", lambda e, c=c, k=k: e.matmul(ps[5][:, :], lhsT=DselR[:, c, k, :], rhs=Ut[:, c, k:k + 512],
                                                         start=(k == 0), stop=(k == 4)), [f"DselR{c}_{k}", uk], ["ps5"])
            P.add("act", lambda e, c=c: e.activation(out=XC[:, c, 0:512], in_=ps[5][:, :], func=AF.Silu, bias=vecF[:, 48 + c:49 + c]),
                  ["ps5", "vecF"], ["XC"])

    def latent_block(gis, snaps=False):
        e_ = GROUPS[gis[0]][2]
        bias_ap = dtb2[:, e_ * 16:e_ * 16 + 16]
        nG = len(gis)
        set_dsel(e_)
        halo_block(gis)
        for i in range(4):
            front_ln(gis[0], i)
            front_tr(gis[0], i)
        projL(gis[0], 0)
        P.add("dve", lambda e: e.tensor_copy(out=U[:, 0:9, 0:2], in_=UH[:, :, 0:2]), ["UH"], ["U"])
        par[0] = 0
        dt_group_a(e_, Wdt, bias_ap, "Wdt", "dtb2")
        dt_group_b(0)
        if nG > 1:
            for i in range(4):
                front_ln(gis[1], i)
                front_tr(gis[1], i)
        for q in range(nG):
            ub, un = q % 2, (q + 1) % 2
            has1 = q + 1 < nG
            has2 = q + 2 < nG
            if has2:
                front_ln(gis[q + 2], 0)
                front_ln(gis[q + 2], 1)
            if has1:
                projL(gis[q + 1], un)
                par[0] = un
                dt_group_a(e_, Wdt, bias_ap, "Wdt", "dtb2")
                kb_, kn_ = ("U", "Ub") if ub == 0 else ("Ub", "U")
                P.add("dve", lambda e, ub=ub, un=un: e.tensor_copy(out=UU[un][:, 0:9, 0:2], in_=UU[ub][:, 0:9, 512:514]), [kb_], [kn_])
                P.add("dve", lambda e, ub=ub, un=un: e.tensor_copy(out=UU[ub][:, 0:9, 514:516], in_=UU[un][:, 0:9, 2:4]), [kn_], [kb_])
            else:
                P.add("dve", lambda e, ub=ub: e.tensor_copy(out=UU[ub][:, 0:9, 514:516], in_=UH[:, :, 2:4]), ["UH"], ["U" if ub == 0 else "Ub"])
            if has2:
                front_tr(gis[q + 2], 0)
                front_tr(gis[q + 2], 1)
                front_ln(gis[q + 2], 2)
                front_ln(gis[q + 2], 3)
            convL(ub, 0, 5)
            if has1:
                dt_group_b(un)
            convL(ub, 5, 9)
            if has2:
                front_tr(gis[q + 2], 2)
                front_tr(gis[q + 2], 3)
            if snaps:
                P.add("act", lambda e, q=q: e.activation(out=Hsnap[:, 3 - q, :], in_=Hrun, func=AF.Copy), ["Hrun"], [f"Hsnap{3 - q}"])
            for i in range(4):
                states_tile(i, ub)

    sfx = sb("sfx", [128, 5, 16])
    wds = sb("wds", [128, 4, 16])

    def states_group4(pp_):
        dtt = dtt_a[:, pp_]
        ew = ew_a[:, pp_]
        ks = [f"dtt{pp_}_{i}" for i in range(4)]
        es = [f"ew{pp_}_{i}" for i in range(4)]
        P.add("dve", lambda e: e.memset(sfx[:, 3, :], 1.0), [], ["sfx"])
        P.add("dve", lambda e: e.tensor_copy(out=sfx[:, 2, :], in_=ew[:, 3, 16:32]), es, ["sfx"])
        P.add("dve", lambda e: e.tensor_tensor(out=sfx[:, 1, :], in0=sfx[:, 2, :], in1=ew[:, 2, 16:32], op=ALU.mult), es + ["sfx"], ["sfx"])
        P.add("dve", lambda e: e.tensor_tensor(out=sfx[:, 0, :], in0=sfx[:, 1, :], in1=ew[:, 1, 16:32], op=ALU.mult), es + ["sfx"], ["sfx"])
        P.add("dve", lambda e: e.tensor_tensor(out=sfx[:, 4, :], in0=sfx[:, 0, :], in1=ew[:, 0, 16:32], op=ALU.mult), es + ["sfx"], ["sfx"])
        P.add("dve", lambda e: e.tensor_tensor(out=wds[:], in0=dtt[:, :, 5, :], in1=sfx[:, 0:4, :], op=ALU.mult), ks + ["sfx"], ["wds"])
        for i in range(4):
            s = i % 2
            for c in range(8):
                P.add("pe", lambda e, c=c, i=i: e.matmul(ps[c // 4][:, (c % 4) * 128:(c % 4 + 1) * 128], lhsT=XC[:, c, i * 128:(i + 1) * 128],
                                                         rhs=cbf[:, C_I:C_I + 128], start=True, stop=True), ["XC", "cbf"], [f"ps{c // 4}"])
            P.add("pe", lambda e, i=i: e.matmul(ps[4][:, 384:512], lhsT=XC[:, 8, i * 128:(i + 1) * 128], rhs=cbf[:, C_I:C_I + 128],
                                                start=True, stop=True), ["XC", "cbf"], ["ps4"])
            for b in range(2):
                P.add("dve", lambda e, b=b, s=s, i=i: e.tensor_tensor(out=xw[s][:, b * 512:(b + 1) * 512].rearrange("p (h q) -> p h q", q=64),
                                                                      in0=ps[b][:, :].rearrange("p (h q) -> p h q", q=64),
                                                                      in1=wds[:, i, b * 8:(b + 1) * 8].unsqueeze(2).to_broadcast([128, 8, 64]),
                                                                      op=ALU.mult), [f"ps{b}", "wds"], [f"xw{s}"])
            P.add("act", lambda e, s=s: e.activation(out=Btok[s][:], in_=ps[4][:, 384:512], func=AF.Copy), ["ps4"], [f"Btok{s}"])
            for b in range(2):
                P.add("pe", lambda e, b=b, s=s, i=i: e.matmul(ps[6 + b][:, :], lhsT=Btok[s][:], rhs=xw[s][:, b * 512:(b + 1) * 512],
                                                              start=(i == 0), stop=(i == 3), skip_group_check=True), [f"Btok{s}", f"xw{s}"], [f"ps{6 + b}"])
        P.add("dve", lambda e: e.tensor_tensor(out=Htmp.rearrange("p (h q) -> p h q", q=64), in0=Hrun.rearrange("p (h q) -> p h q", q=64),
                                               in1=sfx[:, 4, :].unsqueeze(2).to_broadcast([128, 16, 64]), op=ALU.mult), ["Hrun", "sfx"], ["Htmp"])
        for b in range(2):
            P.add("dve", lambda e, b=b: e.tensor_tensor(out=Hrun[:, b * 512:(b + 1) * 512], in0=Htmp[:, b * 512:(b + 1) * 512],
                                                        in1=ps[6 + b][:, :], op=ALU.add), ["Htmp", f"ps{6 + b}"], ["Hrun"])

    def latent_pipeline(blks):
        seq = [(gi, b) for b, gis in enumerate(blks) for gi in gis]
        n = len(seq)
        UHs = [(UH, "UH"), (UH2, "UH2")]
        ukey = lambda u: "U" if u == 0 else "Ub"

        def ent(q):
            return GROUPS[seq[q][0]][2]

        def bias(q):
            return dtb2[:, ent(q) * 16:ent(q) * 16 + 16]

        def front_all(q):
            for i in range(4):
                front_ln(seq[q][0], i)
                front_tr(seq[q][0], i)

        halo_block(blks[0], *UHs[0])
        set_dsel(ent(0))
        front_all(0)
        projL(seq[0][0], 0)
        P.add("dve", lambda e: e.tensor_copy(out=U[:, 0:9, 0:2], in_=UH[:, :, 0:2]), ["UH"], ["U"])
        par[0] = 0
        dt_group_a(ent(0), Wdt, bias(0), "Wdt", "dtb2")
        dt_group_b(0)
        front_all(1)
        for q in range(n):
            gi, b = seq[q]
            ub, un = q % 2, (q + 1) % 2
            has1, has2 = q + 1 < n, q + 2 < n
            firstq, lastq = q % 4 == 0, q % 4 == 3
            uhb, uhbk = UHs[b % 2]
            if has2:
                front_ln(seq[q + 2][0], 0)
                front_ln(seq[q + 2][0], 1)
            if has1:
                g1, b1 = seq[q + 1]
                projL(g1, un)
                par[0] = un
                dt_group_a(ent(q + 1), Wdt, bias(q + 1), "Wdt", "dtb2")
                if b1 == b:
                    P.add("dve", lambda e, ub=ub, un=un: e.tensor_copy(out=UU[un][:, 0:9, 0:2], in_=UU[ub][:, 0:9, 512:514]), [ukey(ub)], [ukey(un)])
                    P.add("dve", lambda e, ub=ub, un=un: e.tensor_copy(out=UU[ub][:, 0:9, 514:516], in_=UU[un][:, 0:9, 2:4]), [ukey(un)], [ukey(ub)])
                else:
                    uhn, uhnk = UHs[b1 % 2]
                    P.add("dve", lambda e, un=un, uhn=uhn: e.tensor_copy(out=UU[un][:, 0:9, 0:2], in_=uhn[:, :, 0:2]), [uhnk], [ukey(un)])
                    P.add("dve", lambda e, ub=ub, uhb=uhb: e.tensor_copy(out=UU[ub][:, 0:9, 514:516], in_=uhb[:, :, 2:4]), [uhbk], [ukey(ub)])
            else:
                P.add("dve", lambda e, ub=ub, uhb=uhb: e.tensor_copy(out=UU[ub][:, 0:9, 514:516], in_=uhb[:, :, 2:4]), [uhbk], [ukey(ub)])
            if has2:
                front_tr(seq[q + 2][0], 0)
                front_tr(seq[q + 2][0], 1)
            if has1:
                dt_group_b(un)
            if has2:
                front_ln(seq[q + 2][0], 2)
                front_ln(seq[q + 2][0], 3)
            convL(ub, 0, 9)
            if has2:
                front_tr(seq[q + 2][0], 2)
                front_tr(seq[q + 2][0], 3)
            if firstq:
                boundary(b)
            if b == 7:
                P.add("act", lambda e, q=q: e.activation(out=Hsnap[:, 3 - (q % 4), :], in_=Hrun, func=AF.Copy), ["Hrun"], [f"Hsnap{3 - (q % 4)}"])
            states_group4(ub)
            if lastq and has1:
                set_dsel(ent(q + 1))
            if q % 4 == 2 and b + 1 < len(blks):
                halo_block(blks[b + 1], *UHs[(b + 1) % 2])

    def b7_block():
        gis = [2 + 28 + j for j in range(4)]
        if RUN_B7:
            latent_block(gis, snaps=True)
        else:
            for q in range(4):
                P.add("act", lambda e, q=q: e.activation(out=Hsnap[:, 3 - q, :], in_=Hrun, func=AF.Copy), ["Hrun"], [f"Hsnap{3 - q}"])

    def run_p1():
        P.add("pool", lambda e: e.memset(Hrun, 0.0), [], ["Hrun"])
        P.add("pool", lambda e: e.memset(Hf_fin, 0.0), [], ["Hf_fin"])
        states_group(0)
        P.add("pool", lambda e: e.tensor_copy(out=hf_ctx, in_=Hrun), ["Hrun"], ["hf_ctx"])
        P.add("pool", lambda e: e.memset(Hrun, 0.0), ["hf_ctx"], ["Hrun"])
        states_group(1)
        P.add("pool", lambda e: e.tensor_copy(out=hb_ctx, in_=Hrun), ["Hrun"], ["hb_ctx"])
        P.add("pool", lambda e: e.tensor_copy(out=Hrun, in_=hf_ctx), ["hf_ctx", "hb_ctx"], ["Hrun"])
        if NFOR_BLOCKS == 7 and RUN_B7 and FLAT_P1:
            latent_pipeline([[2 + 4 * b + j for j in range(4)] for b in range(8)])
        else:
            for b in range(7):
                boundary(b)
                if b < NFOR_BLOCKS:
                    latent_block([2 + 4 * b + j for j in range(4)])
            boundary(7)
            b7_block()

    if not INJECT:
        prep_resident()
        run_p1()

    P.fence()
    xT_own_d_v = xT_own_d.rearrange("(g kc p) t -> g p kc t", kc=8, p=128)
    xres = F1[:, 0:8 * 516].rearrange("p (k t) -> p k t", k=8)
    tmpf = F1[:, 4128:5152]
    sqt = F1[:, 5152:5668]
    rowm = F2[:, 5120:5636]
    rowr = F2[:, 5636:6152]
    Hb = hf_ctx
    yT = RB[:, 0:8192].bitcast(F32).rearrange("p (k t) -> p k t", k=8)
    zT = RB[:, 8192:12288].rearrange("p (k t) -> p k t", k=8)
    rhs1 = F2[:, 6152:7176].rearrange("p (h i) -> p h i", h=8)
    Ee = F2[:, 7176:7688].bitcast(BF16).rearrange("p (h i) -> p h i", h=8)
    scT = [sb(f"scT{i}", [128, 8, 128], BF16) for i in range(2)]
    CBTm = [sb(f"CBTm{i}", [128, 128], BF16) for i in range(2)]
    xdt = xnb
    yofft = sb("yofft", [128, D], BF16)
    HpB1 = sb("HpB", [128, D], BF16)
    HpB = [HpB1, HpB1]
    dt2 = sb("dt2", [128, 4, 2, 6, 16])
    ew2 = sb("ew2", [128, 4, 2, 48])
    ynT = sb("ynT", [128, 8, 512], BF16)
    TM = {0: (C_TGT, C_TLE), 1: (C_TLT, C_TGE)}

    def own_front(g):
        gi = 34 + g
        dma("sp", xres, xT_own_d_v[g], [], ["xres"])
        one = consts[:, C_ONE:C_ONE + 128]
        for kc in range(8):
            P.add("pe", lambda e, kc=kc: e.matmul(ps[6][:, :], lhsT=one, rhs=xres[:, kc, 0:512], start=(kc == 0), stop=(kc == 7)),
                  ["xres", "consts"], ["ps6"])
        for kc in range(8):
            P.add("pe", lambda e, kc=kc: e.matmul(ps[4][:, 300:304], lhsT=one, rhs=xres[:, kc, 512:516], start=(kc == 0), stop=(kc == 7)),
                  ["xres", "consts"], ["ps4"])
        for kc in range(8):
            P.add("act", lambda e, kc=kc: e.activation(out=sqt, in_=xres[:, kc, :], func=AF.Square), ["xres"], ["sqt"])
            P.add("pe", lambda e, kc=kc: e.matmul(ps[7][:, :], lhsT=one, rhs=sqt[:, 0:512], start=(kc == 0), stop=(kc == 7),
                                                  skip_group_check=True), ["sqt", "consts"], ["ps7"])
            P.add("pe", lambda e, kc=kc: e.matmul(ps[5][:, 0:4], lhsT=one, rhs=sqt[:, 512:516], start=(kc == 0), stop=(kc == 7),
                                                  skip_group_check=True), ["sqt", "consts"], ["ps5"])
        ln_rows([(ps[6][:, :], ps[7][:, :], 0, 512, ["ps6", "ps7"]), (ps[4][:, 300:304], ps[5][:, 0:4], 512, 516, ["ps4", "ps5"])])
        for kc in range(8):
            P.add("pool", lambda e, kc=kc: e.tensor_tensor(out=xres[:, kc, :], in0=xres[:, kc, :], in1=rowm, op=ALU.subtract),
                  ["xres", "rowm"], ["xres"])
            P.add("dve", lambda e, kc=kc: e.tensor_tensor(out=xres[:, kc, :], in0=xres[:, kc, :], in1=rowr, op=ALU.mult),
                  ["xres", "rowr"], ["xres"])
        for kc in range(8):
            P.add("act", lambda e, kc=kc: e.activation(out=hT[:, kc, :], in_=xres[:, kc, :], func=AF.Identity, scale=AB[:, 0, kc:kc + 1],
                                                       bias=AB[:, 1, kc:kc + 1]), ["xres", "AB"], ["hT"])
        for kc in range(8):
            P.add("dve", lambda e, kc=kc: e.tensor_scalar(out=xres[:, kc, :], in0=xres[:, kc, :], scalar1=vecF[:, kc:kc + 1],
                                                          scalar2=vecF[:, 8 + kc:9 + kc], op0=ALU.mult, op1=ALU.add), ["xres", "vecF", "hT"], ["xres"])

    def ln_rows(parts):
        for s1, s2, c0, c1, keys in parts:
            P.add("dve", lambda e, s1=s1, c0=c0, c1=c1: e.tensor_scalar(out=rowm[:, c0:c1], in0=s1, scalar1=1.0 / D, scalar2=None, op0=ALU.mult),
                  keys, ["rowm"])
            P.add("pool", lambda e, c0=c0, c1=c1: e.tensor_tensor(out=rowr[:, c0:c1], in0=rowm[:, c0:c1], in1=rowm[:, c0:c1], op=ALU.mult),
                  ["rowm"], ["rowr"])
            P.add("dve", lambda e, s2=s2, c0=c0, c1=c1: e.scalar_tensor_tensor(out=rowr[:, c0:c1], in0=s2, scalar=1.0 / D, in1=rowr[:, c0:c1],
                                                                              op0=ALU.mult, op1=ALU.subtract), keys + ["rowr"], ["rowr"])
            P.add("act", lambda e, c0=c0, c1=c1: e.activation(out=rowr[:, c0:c1], in_=rowr[:, c0:c1], func=AF.Sqrt, bias=epsc[:, 0:1]),
                  ["rowr", "epsc"], ["rowr"])
            P.add("dve", lambda e, c0=c0, c1=c1: e.reciprocal(out=rowr[:, c0:c1], in_=rowr[:, c0:c1]), ["rowr"], ["rowr"])

    def own_proj_z():
        for half in range(2):
            wt, wk = stream_w(512 * half, 512)
            for j in range(4):
                c = 4 * half + j
                pu = ps[2 + c % 2]
                pk = f"ps{2 + c % 2}"
                for kc in range(8):
                    P.add("pe", lambda e, pu=pu, wt=wt, kc=kc, j=j: e.matmul(pu[:, :], lhsT=wt[:, kc, j * 128:(j + 1) * 128], rhs=hT[:, kc, 0:512],
                                                                            start=(kc == 0), stop=(kc == 7)), [wk, "hT"], [pk])
                P.add("act", lambda e, pu=pu, c=c: e.activation(out=zT[:, c, :], in_=pu[:, :], func=AF.Silu), [pk], ["zT"])

    def own_dt(i, d):
        e_ = E_OWNF + d
        k = f"dt2_{i}_{d}"
        pd = ps[4][:, 64:80]
        for kc in range(8):
            P.add("pe", lambda e, kc=kc: e.matmul(pd, lhsT=hT[:, kc, i * 128:(i + 1) * 128], rhs=Wdt[:, e_, kc, :],
                                                  start=(kc == 0), stop=(kc == 7)), ["hT", "Wdt"], ["ps4"])
        v, av, lv, dt_, dta, wd = (dt2[:, i, d, j, :] for j in range(6))
        P.add("dve", lambda e: e.tensor_tensor(out=v, in0=pd, in1=rowB[:, R_DTB + e_ * 16:R_DTB + e_ * 16 + 16], op=ALU.add), ["ps4", "rowB"], [k])
        P.add("act", lambda e: e.activation(out=av, in_=v, func=AF.Abs), [k], [k])
        P.add("act", lambda e: e.activation(out=av, in_=av, func=AF.Exp, scale=-1.0), [k], [k])
        P.add("act", lambda e: e.activation(out=lv, in_=av, func=AF.Ln, bias=1.0), [k], [k])
        P.add("dve", lambda e: e.scalar_tensor_tensor(out=dt_, in0=v, scalar=0.0, in1=lv, op0=ALU.max, op1=ALU.add), [k], [k])
        P.add("dve", lambda e: e.tensor_tensor(out=dta, in0=dt_, in1=aneg[:, e_ * 16:e_ * 16 + 16], op=ALU.mult), [k, "aneg"], [k])
        pw = ps[4][:, 192:240]
        cW, cI = TM[d]
        P.add("pe", lambda e: e.matmul(pw[:, 0:16], lhsT=consts[:, cW:cW + 128], rhs=dta, start=True, stop=True), [k, "consts"], ["ps4"])
        P.add("pe", lambda e: e.matmul(pw[:, 16:32], lhsT=consts[:, C_ONE:C_ONE + 128], rhs=dta, start=True, stop=True), [k, "consts"], ["ps4"])
        P.add("pe", lambda e: e.matmul(pw[:, 32:48], lhsT=consts[:, cI:cI + 128], rhs=dta, start=True, stop=True), [k, "consts"], ["ps4"])
        P.add("act", lambda e: e.activation(out=ew2[:, i, d, :], in_=pw, func=AF.Exp), ["ps4"], [f"ew2_{i}_{d}"])
        P.add("dve", lambda e: e.tensor_tensor(out=wd, in0=ew2[:, i, d, 0:16], in1=dt_, op=ALU.mult), [f"ew2_{i}_{d}", k], [k])

    def own_dt_a(d):
        e_ = E_OWNF + d
        for i in range(4):
            for kc in range(8):
                P.add("pe", lambda e, kc=kc, i=i: e.matmul(ps[4][:, 64 + 16 * i:80 + 16 * i], lhsT=hT[:, kc, i * 128:(i + 1) * 128], rhs=Wdt[:, e_, kc, :],
                                                           start=(kc == 0), stop=(kc == 7)), ["hT", "Wdt"], ["ps4"])
        pd4 = ps[4][:, 64:128].rearrange("p (t h) -> p t h", t=4)
        v, av, lv, dt_, dta, wd = (dt2[:, :, d, j, :] for j in range(6))
        ks = [f"dt2_{i}_{d}" for i in range(4)]
        bias_ap = rowB[:, R_DTB + e_ * 16:R_DTB + e_ * 16 + 16]
        P.add("dve", lambda e: e.tensor_tensor(out=v, in0=pd4, in1=bias_ap.unsqueeze(1).to_broadcast([128, 4, 16]), op=ALU.add), ["ps4", "rowB"], ks)
        P.add("act", lambda e: e.activation(out=av, in_=v, func=AF.Abs), ks, ks)
        P.add("act", lambda e: e.activation(out=av, in_=av, func=AF.Exp, scale=-1.0), ks, ks)
        P.add("act", lambda e: e.activation(out=lv, in_=av, func=AF.Ln, bias=1.0), ks, ks)
        P.add("dve", lambda e: e.scalar_tensor_tensor(out=dt_, in0=v, scalar=0.0, in1=lv, op0=ALU.max, op1=ALU.add), ks, ks)
        P.add("dve", lambda e: e.tensor_tensor(out=dta, in0=dt_, in1=aneg[:, e_ * 16:e_ * 16 + 16].unsqueeze(1).to_broadcast([128, 4, 16]), op=ALU.mult),
              ks + ["aneg"], ks)

    def own_dt_b(d):
        cW, cI = TM[d]
        v, av, lv, dt_, dta, wd = (dt2[:, :, d, j, :] for j in range(6))
        ks = [f"dt2_{i}_{d}" for i in range(4)]
        es = [f"ew2_{i}_{d}" for i in range(4)]
        for i in range(4):
            o = 128 + 48 * i
            P.add("pe", lambda e, i=i, o=o: e.matmul(ps[4][:, o:o + 16], lhsT=consts[:, cW:cW + 128], rhs=dt2[:, i, d, 4, :], start=True, stop=True),
                  ks + ["consts"], ["ps4"])
            P.add("pe", lambda e, i=i, o=o: e.matmul(ps[4][:, o + 16:o + 32], lhsT=consts[:, C_ONE:C_ONE + 128], rhs=dt2[:, i, d, 4, :], start=True, stop=True),
                  ks + ["consts"], ["ps4"])
            P.add("pe", lambda e, i=i, o=o: e.matmul(ps[4][:, o + 32:o + 48], lhsT=consts[:, cI:cI + 128], rhs=dt2[:, i, d, 4, :], start=True, stop=True),
                  ks + ["consts"], ["ps4"])
        P.add("act", lambda e: e.activation(out=ew2[:, :, d, :], in_=ps[4][:, 128:320].rearrange("p (t h) -> p t h", t=4), func=AF.Exp), ["ps4"], es)
        P.add("dve", lambda e: e.tensor_tensor(out=wd, in0=ew2[:, :, d, 0:16], in1=dt_, op=ALU.mult), es + ks, ks)

    def own_tile_dir(i, d, Hst, hkey, first):
        s = d
        cW, cI = TM[d]
        tsl = slice(i * 128, (i + 1) * 128)
        dk, ek = f"dt2_{i}_{d}", f"ew2_{i}_{d}"
        for c in range(8):
            P.add("pe", lambda e, c=c: e.matmul(ps[c // 4][:, (c % 4) * 128:(c % 4 + 1) * 128], lhsT=XC[:, c, tsl],
                                                rhs=cbf[:, C_I:C_I + 128], start=True, stop=True), ["XC", "cbf"], [f"ps{c // 4}"])
        P.add("pe", lambda e: e.matmul(ps[4][:, 384:512], lhsT=XC[:, 8, tsl], rhs=cbf[:, C_I:C_I + 128], start=True, stop=True),
              ["XC", "cbf"], ["ps4"])
        for b in range(2):
            P.add("dve", lambda e, b=b: e.tensor_tensor(out=xdt[s][:, b * 512:(b + 1) * 512].rearrange("p (h q) -> p h q", q=64),
                                                        in0=ps[b][:, :].rearrange("p (h q) -> p h q", q=64),
                                                        in1=dt2[:, i, d, 3, b * 8:(b + 1) * 8].unsqueeze(2).to_broadcast([128, 8, 64]),
                                                        op=ALU.mult), [f"ps{b}", dk], [f"xnb{s}"])
            P.add("dve", lambda e, b=b: e.tensor_tensor(out=xw[s][:, b * 512:(b + 1) * 512].rearrange("p (h q) -> p h q", q=64),
                                                        in0=ps[b][:, :].rearrange("p (h q) -> p h q", q=64),
                                                        in1=dt2[:, i, d, 5, b * 8:(b + 1) * 8].unsqueeze(2).to_broadcast([128, 8, 64]),
                                                        op=ALU.mult), [f"ps{b}", dk], [f"xw{s}"])
        P.add("act", lambda e: e.activation(out=Btok[s][:], in_=ps[4][:, 384:512], func=AF.Copy), ["ps4"], [f"Btok{s}"])
        P.add("pe", lambda e: e.matmul(ps[4][:, 384:512], lhsT=XC[:, 8, tsl], rhs=XC[:, 9, tsl], start=True, stop=True), ["XC", f"Btok{s}"], ["ps4"])
        P.add("dve", lambda e: e.tensor_tensor(out=CBTm[s][:], in0=ps[4][:, 384:512], in1=consts[:, cI:cI + 128], op=ALU.mult),
              ["ps4", "consts"], [f"CBTm{s}"])
        P.add("act", lambda e: e.activation(out=HpB[s][:], in_=Hst, func=AF.Copy), [hkey], ["HpB"])
        for b in range(2):
            P.add("pe", lambda e, b=b: e.matmul(ps[b][:, :], lhsT=XC[:, 9, tsl], rhs=HpB[s][:, b * 512:(b + 1) * 512], start=True, stop=True),
                  ["XC", "HpB"], [f"ps{b}"])
            P.add("dve", lambda e, b=b: e.tensor_tensor(out=yofft[:, b * 512:(b + 1) * 512].rearrange("p (h q) -> p h q", q=64),
                                                        in0=ps[b][:, :].rearrange("p (h q) -> p h q", q=64),
                                                        in1=ew2[:, i, d, 32 + b * 8:32 + (b + 1) * 8].unsqueeze(2).to_broadcast([128, 8, 64]),
                                                        op=ALU.mult), [f"ps{b}", ek], ["yofft"])
        for hf in range(2):
            P.add("pool", lambda e, hf=hf: e.tensor_tensor(out=rhs1,
                                                           in0=consts[:, cI:cI + 128].unsqueeze(1).to_broadcast([128, 8, 128]),
                                                           in1=dt2[:, i, d, 4, hf * 8:(hf + 1) * 8].unsqueeze(2).to_broadcast([128, 8, 128]),
                                                           op=ALU.mult), ["consts", dk], ["rhs1"])
            for q in range(2):
                P.add("pe", lambda e, q=q: e.matmul(ps[2 + q][:, :], lhsT=consts[:, cW:cW + 128],
                                                    rhs=rhs1[:, q * 4:(q + 1) * 4, :].rearrange("p h i -> p (h i)"), start=True, stop=True),
                      ["rhs1", "consts"], [f"ps{2 + q}"])
                P.add("act", lambda e, q=q: e.activation(out=Ee[:, q * 4:(q + 1) * 4, :].rearrange("p h i -> p (h i)"), in_=ps[2 + q][:, :], func=AF.Exp),
                      [f"ps{2 + q}"], ["Ee"])
            sc = scT[hf]
            P.add("dve", lambda e, sc=sc: e.tensor_tensor(out=sc[:], in0=Ee, in1=CBTm[s][:].unsqueeze(1).to_broadcast([128, 8, 128]), op=ALU.mult),
                  ["Ee", f"CBTm{s}"], [f"scT{hf}"])
            pY = ps[6 + hf]
            for hh in range(8):
                h = hf * 8 + hh
                c = hh // 2
                lo = (hh % 2) * 64
                P.add("pe", lambda e, pY=pY, sc=sc, h=h, hh=hh, c=c, lo=lo: e.matmul(
                    pY[lo:lo + 64, c * 128:(c + 1) * 128], lhsT=xdt[s][:, h * 64:(h + 1) * 64], rhs=sc[:, hh, :],
                    start=(hh < 2), stop=False, skip_group_check=True), [f"xnb{s}", f"scT{hf}"], [f"ps{6 + hf}"])
            for c in range(4):
                cc = hf * 4 + c
                P.add("pe", lambda e, pY=pY, c=c, cc=cc: e.matmul(pY[:, c * 128:(c + 1) * 128], lhsT=yofft[:, cc * 128:(cc + 1) * 128],
                                                                 rhs=cbf[:, C_I:C_I + 128], start=False, stop=(c == 3), skip_group_check=True),
                      ["yofft", "cbf"], [f"ps{6 + hf}"])
            ysl = yT[:, hf * 4:(hf + 1) * 4, tsl]
            pv = pY[:, :].rearrange("p (c t) -> p c t", c=4)
            if first:
                P.add("dve", lambda e, ysl=ysl, pv=pv: e.tensor_copy(out=ysl, in_=pv), [f"ps{6 + hf}"], ["yT"])
            else:
                P.add("dve", lambda e, ysl=ysl, pv=pv: e.tensor_tensor(out=ysl, in0=ysl, in1=pv, op=ALU.add), [f"ps{6 + hf}", "yT"], ["yT"])
        for b in range(2):
            P.add("pe", lambda e, b=b: e.matmul(ps[b][:, :], lhsT=Btok[s][:], rhs=xw[s][:, b * 512:(b + 1) * 512], start=True, stop=True),
                  [f"Btok{s}", f"xw{s}"], [f"ps{b}"])
        P.add("pool", lambda e: e.tensor_tensor(out=Htmp.rearrange("p (h q) -> p h q", q=64), in0=Hst.rearrange("p (h q) -> p h q", q=64),
                                                in1=ew2[:, i, d, 16:32].unsqueeze(2).to_broadcast([128, 16, 64]), op=ALU.mult),
              [hkey, ek], ["Htmp"])
        for b in range(2):
            P.add("dve", lambda e, b=b: e.tensor_tensor(out=Hst[:, b * 512:(b + 1) * 512], in0=Htmp[:, b * 512:(b + 1) * 512],
                                                        in1=ps[b][:, :], op=ALU.add), ["Htmp", f"ps{b}"], [hkey])

    wout_pref = []

    def own_ssd(g):
        set_entry(E_OWNF)
        own_front(g)
        own_dt_a(0)
        own_dt_a(1)
        own_proj_z()
        own_dt_b(0)
        own_dt_b(1)
        proj_feat(34 + g, 512, [(c, 1024 + 128 * c) for c in range(10)])
        conv(512, 10)
        wout_pref[:] = [stream_packed(w_out_d[mp * 128:(mp + 1) * 128, :], 16, 256) for mp in range(2)]
        for i in range(4):
            own_tile_dir(i, 0, Hf_fin, "Hf_fin", True)
        P.add("act", lambda e: e.activation(out=Hb, in_=Hsnap[:, g, :], func=AF.Copy), [f"Hsnap{g}"], ["Hb"])
        for i in (3, 2, 1, 0):
            own_tile_dir(i, 1, Hb, "Hb", False)
        for c in range(8):
            P.add("dve", lambda e, c=c: e.scalar_tensor_tensor(out=yT[:, c, :], in0=XC[:, c, 0:512], scalar=vecF[:, V_DSK + c:V_DSK + c + 1],
                                                              in1=yT[:, c, :], op0=ALU.mult, op1=ALU.add), ["XC", "vecF", "yT"], ["yT"])

    out_v = out_d.rearrange("(kc p) t -> p kc t", p=128)
    actT = RB[:, 0:22 * 512].rearrange("p (k t) -> p k t", k=22)
    ypT = XC[:, 0:8, :]
    sq2 = F1[:, 4128:4644]
    sgt = F1[:, 4128:4640]

    def ws_view(s, k, n):
        return ws[s][:].rearrange("p a b -> p (a b)")[:, 0:k * n].rearrange("p (k n) -> p k n", k=k)

    def stream(src_ap, k, n):
        s = ws_ctr[0] % 2
        ws_ctr[0] += 1
        v = ws_view(s, k, n)
        if k > 8:
            h_ = k // 2
            dma("pool", v[:, 0:h_, :], src_ap[:, 0:h_, :], [], [f"ws{s}"])
            dma("pool", v[:, h_:k, :], src_ap[:, h_:k, :], [], [f"ws{s}"])
        else:
            dma("pool", v, src_ap, [], [f"ws{s}"])
        return v, f"ws{s}"

    def stream_packed(src_rows, k, n):
        s_ = ws_ctr[0] % 2
        ws_ctr[0] += 1
        flat = ws[s_][:].rearrange("p a b -> p (a b)")[:, 0:k * n]
        dma("pool", flat, src_rows, [], [f"ws{s_}"])
        return flat.rearrange("p (k n) -> p k n", k=k), f"ws{s_}"

    def ln_feat(width):
        one = consts[:, C_ONE:C_ONE + 128]
        parts = [(0, 512, ps[6], ps[7], "ps6", "ps7")]
        if width > 512:
            parts.append((512, width, ps[4], ps[5], "ps4", "ps5"))
        for c0, c1, pa, pb, ka, kb in parts:
            n = c1 - c0
            for kc in range(8):
                P.add("pe", lambda e, kc=kc, pa=pa, c0=c0, c1=c1, n=n: e.matmul(pa[:, 0:n], lhsT=one, rhs=xres[:, kc, c0:c1], start=(kc == 0), stop=(kc == 7)),
                      ["xres", "consts"], [ka])
        for kc in range(8):
            sq_, sqk = (sqt, ["sqt"]) if kc % 2 == 0 else (sq2, ["sq2", "sgt"])
            P.add("act", lambda e, kc=kc, sq_=sq_: e.activation(out=sq_[:, 0:width], in_=xres[:, kc, 0:width], func=AF.Square), ["xres"], sqk)
            for c0, c1, pa, pb, ka, kb in parts:
                n = c1 - c0
                P.add("pe", lambda e, kc=kc, pb=pb, c0=c0, c1=c1, n=n, sq_=sq_: e.matmul(pb[:, 0:n], lhsT=one, rhs=sq_[:, c0:c1], start=(kc == 0), stop=(kc == 7),
                                                                                        skip_group_check=True), [sqk[0], "consts"], [kb])
        ln_rows([(pa[:, 0:c1 - c0], pb[:, 0:c1 - c0], c0, c1, [ka, kb]) for c0, c1, pa, pb, ka, kb in parts])
        for kc in range(8):
            P.add("pool", lambda e, kc=kc: e.tensor_tensor(out=xres[:, kc, 0:width], in0=xres[:, kc, 0:width], in1=rowm[:, 0:width], op=ALU.subtract),
                  ["xres", "rowm"], ["xres"])
            P.add("dve", lambda e, kc=kc: e.tensor_tensor(out=xres[:, kc, 0:width], in0=xres[:, kc, 0:width], in1=rowr[:, 0:width], op=ALU.mult),
                  ["xres", "rowr"], ["xres"])

    def affine_feat(dst, gcol, bcol, width, keys_w):
        for kc in range(8):
            if kc % 2 == 0:
                P.add("act", lambda e, kc=kc: e.activation(out=dst[:, kc, 0:width], in_=xres[:, kc, 0:width], func=AF.Identity,
                                                           scale=gcol[:, kc:kc + 1], bias=bcol[:, kc:kc + 1]), ["xres", "AB", "vecF"], keys_w)
            else:
                P.add("dve", lambda e, kc=kc: e.tensor_scalar(out=dst[:, kc, 0:width], in0=xres[:, kc, 0:width], scalar1=gcol[:, kc:kc + 1],
                                                              scalar2=bcol[:, kc:kc + 1], op0=ALU.mult, op1=ALU.add), ["xres", "AB", "vecF"], keys_w)

    def own_tail(g):
        tsl = slice(512 * g, 512 * (g + 1))
        P.add("dve", lambda e: e.tensor_tensor(out=yT, in0=yT, in1=zT, op=ALU.mult), ["yT", "zT"], ["yT"])
        one = consts[:, C_ONE:C_ONE + 128]
        for c in range(8):
            sq_, sqk = (sqt, ["sqt"]) if c % 2 == 0 else (sq2, ["sq2", "sgt"])
            P.add("act", lambda e, c=c, sq_=sq_: e.activation(out=sq_[:, 0:512], in_=yT[:, c, :], func=AF.Square), ["yT"], sqk)
            P.add("pe", lambda e, c=c, sq_=sq_: e.matmul(ps[7][:, :], lhsT=one, rhs=sq_[:, 0:512], start=(c == 0), stop=(c == 7), skip_group_check=True),
                  [sqk[0], "consts"], ["ps7"])
        P.add("dve", lambda e: e.tensor_scalar(out=rowr[:, 0:512], in0=ps[7][:, :], scalar1=1.0 / D, scalar2=None, op0=ALU.mult), ["ps7"], ["rowr"])
        P.add("act", lambda e: e.activation(out=rowr[:, 0:512], in_=rowr[:, 0:512], func=AF.Sqrt, bias=epsc[:, 0:1]), ["rowr", "epsc"], ["rowr"])
        P.add("dve", lambda e: e.reciprocal(out=rowr[:, 0:512], in_=rowr[:, 0:512]), ["rowr"], ["rowr"])
        for c in range(8):
            P.add("dve", lambda e, c=c: e.scalar_tensor_tensor(out=ynT[:, c, :], in0=yT[:, c, :], scalar=vecF[:, 32 + c:33 + c], in1=rowr[:, 0:512],
                                                              op0=ALU.mult, op1=ALU.mult), ["yT", "vecF", "rowr"], ["ynT"])
        dma("pool", ypT, out_v[:, :, tsl], [f"outg{g}"], ["XC"])
        P.add("dve", lambda e: e.tensor_scalar(out=xres[:, :, 0:512], in0=xres[:, :, 0:512], scalar1=ALPHA, scalar2=None, op0=ALU.mult), ["xres"], ["xres"])
        for mp in range(4):
            wt, wk = wout_pref[mp] if mp < 2 else stream_packed(w_out_d[mp * 128:(mp + 1) * 128, :], 16, 256)
            for mm in range(2):
                m = mp * 2 + mm
                pm = ps[2 + m % 2]
                pk = f"ps{2 + m % 2}"
                for k in range(16):
                    src = ynT[:, k, :] if k < 8 else ypT[:, k - 8, :]
                    sk = "ynT" if k < 8 else "XC"
                    P.add("pe", lambda e, pm=pm, wt=wt, k=k, mm=mm, src=src: e.matmul(pm[:, :], lhsT=wt[:, k, mm * 128:(mm + 1) * 128], rhs=src,
                                                                                     start=(k == 0), stop=(k == 15)), [wk, sk], [pk])
                P.add("dve", lambda e, pm=pm, m=m: e.scalar_tensor_tensor(out=xres[:, m, 0:512], in0=pm[:, :], scalar=modT[:, 16 + m, 0:1],
                                                                         in1=xres[:, m, 0:512], op0=ALU.mult, op1=ALU.add), [pk, "modT", "xres"], ["xres"])
        ln_feat(512)
        affine_feat(hT, AB[:, 4, :], AB[:, 5, :], 512, ["hT"])
        for kc in range(8):
            P.add("dve", lambda e, kc=kc: e.tensor_scalar(out=xres[:, kc, 0:512], in0=xres[:, kc, 0:512], scalar1=vecF[:, 58 + kc:59 + kc],
                                                          scalar2=vecF[:, 66 + kc:67 + kc], op0=ALU.mult, op1=ALU.add), ["xres", "vecF", "hT"], ["xres"])
        for bt in range(11):
            wv, wk = stream_packed(w_gu_d[bt * 128:(bt + 1) * 128, :], 8, 512)
            for jj in range(2):
                j = bt * 2 + jj
                for kc in range(8):
                    P.add("pe", lambda e, wv=wv, kc=kc, jj=jj: e.matmul(ps[2][:, :], lhsT=wv[:, kc, jj * 128:(jj + 1) * 128], rhs=hT[:, kc, 0:512],
                                                                       start=(kc == 0), stop=(kc == 7)), [wk, "hT"], ["ps2"])
                for kc in range(8):
                    P.add("pe", lambda e, wv=wv, kc=kc, jj=jj: e.matmul(ps[3][:, :], lhsT=wv[:, kc, 256 + jj * 128:256 + (jj + 1) * 128], rhs=hT[:, kc, 0:512],
                                                                       start=(kc == 0), stop=(kc == 7)), [wk, "hT"], ["ps3"])
                P.add("act", lambda e: e.activation(out=sgt, in_=ps[2][:, :], func=AF.Silu), ["ps2"], ["sgt", "sq2"])
                P.add("dve", lambda e, j=j: e.tensor_tensor(out=actT[:, j, :], in0=sgt, in1=ps[3][:, :], op=ALU.mult), ["sgt", "ps3"], ["actT", "yT", "zT"])
        P.add("dve", lambda e: e.tensor_scalar(out=xres[:, :, 0:512], in0=xres[:, :, 0:512], scalar1=ALPHA, scalar2=None, op0=ALU.mult), ["xres"], ["xres"])
        for m in range(8):
            wt, wk = stream_packed(w_down_d[m * 128:(m + 1) * 128, :], 22, 128)
            pm = ps[2 + m % 2]
            pk = f"ps{2 + m % 2}"
            for j in range(22):
                P.add("pe", lambda e, pm=pm, wt=wt, j=j: e.matmul(pm[:, :], lhsT=wt[:, j, :], rhs=actT[:, j, :], start=(j == 0), stop=(j == 21)),
                      [wk, "actT"], [pk])
            P.add("dve", lambda e, pm=pm, m=m: e.scalar_tensor_tensor(out=xres[:, m, 0:512], in0=pm[:, :], scalar=modT[:, 40 + m, 0:1],
                                                                     in1=xres[:, m, 0:512], op0=ALU.mult, op1=ALU.add), [pk, "modT", "xres"], ["xres"])
        ln_feat(512)
        for kc in range(8):
            P.add("dve", lambda e, kc=kc: e.tensor_scalar(out=xres[:, kc, 0:512], in0=xres[:, kc, 0:512], scalar1=vecF[:, V_LN2G + kc:V_LN2G + kc + 1],
                                                          scalar2=vecF[:, V_LN2B + kc:V_LN2B + kc + 1], op0=ALU.mult, op1=ALU.add), ["xres", "vecF"], ["xres"])
        dma("sp", out_v[:, :, tsl], xres[:, :, 0:512], ["xres"], [f"outg{g}"])

    for g in range(N_OWN_GROUPS):
        own_ssd(g)
        if RUN_TAIL:
            own_tail(g)
        if DEBUG and DBG_GROUP == 100 + g:
            tap(0, yT[:, 0, :], ["yT"])
            tap(512, yT[:, 7, :], ["yT"])
            tap(1024, zT[:, 0, :], ["zT"])
            tap(1536, xres[:, 0, 0:512], ["xres"])

    if DEBUG:
        if DBG_GROUP is None:
            if DBG_SNAP2 is None:
                tap(0, Hf_fin, ["Hf_fin"])
            else:
                tap(0, Hsnap[:, DBG_SNAP2, :], [f"Hsnap{DBG_SNAP2}"])
            tap(1024, Hsnap[:, DBG_SNAP, :], [f"Hsnap{DBG_SNAP}"])
        dma("sp", dbg_d[:, :], dbg[:], ["dbg"], ["dbg_d"])

    P.emit(nc, stack)
    stack.close()
    return nc


NFOR_BLOCKS = 7
RUN_B7 = True
FLAT_P1 = True
DBG_SNAP = 3
DBG_SNAP2 = None
N_OWN_GROUPS = 4
RUN_TAIL = True
RUN_POOL = True
POOL_GROUPS = 4
INJECT = False
DBG_GROUP = None
NVEC = 98
V_LN2G, V_LN2B, V_DSK = 74, 82, 90
_NC_CACHE = {}


def kernel(**inputs):
    maps = prep_inputs(inputs)
    if "nc" not in _NC_CACHE:
        _NC_CACHE["nc"] = build_program()
    nc = _NC_CACHE["nc"]
    res = run_bass_kernel_spmd(nc, maps, core_ids=list(range(NCORE)))
    out = np.concatenate([np.asarray(r["out"], np.float32).T for r in res.results], axis=0)
    kernel.last_results = res.results
    return out.reshape(1, NCORE * TOK, D)
```
